# Optimizing a Trainium2 kernel written in Bass

```python
import jax, jax.numpy as jnp
from jax import lax
import numpy as np

D_MODEL = 1024
BATCH = 1
SEQ = 16384
DEPTH = 2
DEC_BATCH = 128
DEC_SEQ = 1
PAST_LEN = 16384
PAGE_SIZE = 128

CHUNK = 128
D_A = D_MODEL
A_GROUPS = 8
A_GROUP_DIM = D_A // A_GROUPS
D_B = D_MODEL
CONV_W = 3
N_HEADS = 16
N_KV_HEADS = 4
HEAD_DIM = 64
Q_PER_KV = N_HEADS // N_KV_HEADS
WINDOW = 128
ATT_BLOCK = WINDOW
ROPE_THETA = 10000.0
N_BRANCH = 3
D_FF = ((-(-8 * D_MODEL // 3) + 255) // 256) * 256
IN_SIZES = (D_A, D_A, D_B, D_B, D_B, N_HEADS * HEAD_DIM, N_KV_HEADS * HEAD_DIM, N_KV_HEADS * HEAD_DIM, N_BRANCH * D_MODEL)
IN_WIDTH = sum(IN_SIZES)
NEG_INF = -1e30

kernel_name = 'hybrid_gated_chunkmlp_shortconv_swa_step'


def split_cols(t, sizes):
    idx = [int(i) for i in np.cumsum(sizes)[:-1]]
    return jnp.split(t, idx, axis=-1)


def rmsnorm(x, g, eps=1e-6):
    xf = x.astype(jnp.float32)
    y = xf * lax.rsqrt(jnp.mean(xf * xf, axis=-1, keepdims=True) + eps)
    return (y * g.astype(jnp.float32)).astype(x.dtype)


def layernorm(x, g, b, eps=1e-5):
    xf = x.astype(jnp.float32)
    mu = jnp.mean(xf, axis=-1, keepdims=True)
    var = jnp.mean(jnp.square(xf - mu), axis=-1, keepdims=True)
    y = (xf - mu) * lax.rsqrt(var + eps) * g.astype(jnp.float32) + b.astype(jnp.float32)
    return y.astype(x.dtype)


def rope(x, pos):
    half = HEAD_DIM // 2
    inv = jnp.power(jnp.float32(ROPE_THETA), -jnp.arange(half, dtype=jnp.float32) * (2.0 / HEAD_DIM))
    ang = pos.astype(jnp.float32)[:, None] * inv[None, :]
    cos = jnp.cos(ang)[:, None, :]
    sin = jnp.sin(ang)[:, None, :]
    xf = x.astype(jnp.float32)
    x1, x2 = xf[..., :half], xf[..., half:]
    return jnp.concatenate([x1 * cos - x2 * sin, x2 * cos + x1 * sin], axis=-1).astype(x.dtype)


def chunk_spatial(v, w_s, b_s, n):
    tril = jnp.tril(jnp.ones((n, n), dtype=bool))
    w = jnp.where(tril[None], w_s[:, :n, :n], 0.0).astype(v.dtype)
    out = jnp.einsum('gts,bcsgd->bctgd', w, v)
    return out + jnp.transpose(b_s[:, :n])[None, None, :, :, None].astype(v.dtype)


def causal_conv(xp, w, b):
    L = xp.shape[1] - (CONV_W - 1)
    y = b[None, None, :]
    for j in range(CONV_W):
        y = y + w[j][None, None, :] * xp[:, j:j + L]
    return y


def with_prev_block(t):
    prev = jnp.concatenate([jnp.zeros_like(t[:, :1]), t[:, :-1]], axis=1)
    return jnp.concatenate([prev, t], axis=2)


def sink_attention(q, k, v, mask, sinks):
    s = jnp.einsum('bnqgrd,bnkgd->bngrqk', q, k).astype(jnp.float32) * (HEAD_DIM ** -0.5)
    s = jnp.where(mask[None, :, None, None], s, NEG_INF)
    sk = sinks.astype(jnp.float32).reshape(N_KV_HEADS, Q_PER_KV)[None, None, :, :, None, None]
    m = jnp.maximum(jnp.max(s, axis=-1, keepdims=True), sk)
    e = jnp.exp(s - m)
    p = e / (jnp.sum(e, axis=-1, keepdims=True) + jnp.exp(sk - m))
    return jnp.einsum('bngrqk,bnkgd->bnqgrd', p.astype(v.dtype), v)


def trunk_layer(x, lp, prompt, conv_hist, win_k, win_v):
    bsz, L, _ = x.shape
    h = rmsnorm(x, lp['norm_pre_mix'])
    proj = h @ lp['w_in']
    ua, va, bg, cg, hb, q, k, v, gates = split_cols(proj, IN_SIZES)

    ua = jax.nn.gelu(ua)
    va = layernorm(jax.nn.gelu(va), lp['chunk_ln_g'], lp['chunk_ln_b'])
    n = CHUNK if prompt else L
    nc = L // n
    sa = chunk_spatial(va.reshape(bsz, nc, n, A_GROUPS, A_GROUP_DIM), lp['w_spatial'], lp['b_spatial'], n)
    y_a = (ua * sa.reshape(bsz, L, D_A)) @ lp['w_br_a']

    cx = cg * hb
    hist = jnp.zeros((bsz, CONV_W - 1, D_B), x.dtype) if prompt else conv_hist.astype(x.dtype)
    cxp = jnp.concatenate([hist, cx], axis=1)
    y_b = (bg * causal_conv(cxp, lp['conv_w'].astype(x.dtype), lp['conv_b'].astype(x.dtype))) @ lp['w_br_b']
    new_conv = cxp[:, -(CONV_W - 1):]

    pos = jnp.arange(L, dtype=jnp.int32) if prompt else PAST_LEN + jnp.arange(L, dtype=jnp.int32)
    q = rope(q.reshape(bsz, L, N_HEADS, HEAD_DIM), pos)
    k = rope(k.reshape(bsz, L, N_KV_HEADS, HEAD_DIM), pos)
    v = v.reshape(bsz, L, N_KV_HEADS, HEAD_DIM)
    if prompt:
        nb = L // ATT_BLOCK
        qb = q.reshape(bsz, nb, ATT_BLOCK, N_KV_HEADS, Q_PER_KV, HEAD_DIM)
        kk = with_prev_block(k.reshape(bsz, nb, ATT_BLOCK, N_KV_HEADS, HEAD_DIM))
        vv = with_prev_block(v.reshape(bsz, nb, ATT_BLOCK, N_KV_HEADS, HEAD_DIM))
        qpos = pos.reshape(nb, ATT_BLOCK)
        kpos = jnp.concatenate([qpos - ATT_BLOCK, qpos], axis=1)
        new_k = k[:, -WINDOW:]
        new_v = v[:, -WINDOW:]
    else:
        qb = q.reshape(bsz, 1, L, N_KV_HEADS, Q_PER_KV, HEAD_DIM)
        kfull = jnp.concatenate([win_k.astype(x.dtype), k], axis=1)
        vfull = jnp.concatenate([win_v.astype(x.dtype), v], axis=1)
        kk = kfull[:, None]
        vv = vfull[:, None]
        qpos = pos[None]
        kpos = (PAST_LEN - WINDOW + jnp.arange(WINDOW + L, dtype=jnp.int32))[None]
        new_k = kfull[:, -WINDOW:]
        new_v = vfull[:, -WINDOW:]
    diff = qpos[:, :, None] - kpos[:, None, :]
    mask = (diff >= 0) & (diff <= WINDOW) & (kpos[:, None, :] >= 0)
    o = sink_attention(qb, kk, vv, mask, lp['attn_sinks'])
    y_c = o.reshape(bsz, L, N_HEADS * HEAD_DIM) @ lp['w_br_c']

    g_a, g_b, g_c = jnp.split(jax.nn.sigmoid(gates), N_BRANCH, axis=-1)
    merged = g_a * y_a + g_b * y_b + g_c * y_c
    x = x + rmsnorm(merged @ lp['w_out'], lp['norm_post_mix'])

    h2 = rmsnorm(x, lp['norm_pre_ffn'])
    f = (jax.nn.silu(h2 @ lp['w_ffn_gate']) * (h2 @ lp['w_ffn_up'])) @ lp['w_ffn_down']
    x = x + rmsnorm(f, lp['norm_post_ffn'])
    return x, new_conv, new_k, new_v, va


def setup_inputs(seed: int = 0) -> dict:
    key = jax.random.key(seed)
    ks = jax.random.split(key, 32)
    f32 = jnp.float32

    def nrm(k, shape, scale):
        return jax.random.normal(k, shape, f32) * scale

    def gain(k, shape):
        return 1.0 + 0.05 * jax.random.normal(k, shape, f32)

    kv_shape = (DEPTH, DEC_BATCH, WINDOW, N_KV_HEADS, HEAD_DIM)
    return {
        'x_prompt': nrm(ks[0], (BATCH, SEQ, D_MODEL), 1.0),
        'x_sample': nrm(ks[1], (DEC_BATCH, DEC_SEQ, D_MODEL), 1.0),
        'state_conv': nrm(ks[2], (DEPTH, DEC_BATCH, CONV_W - 1, D_B), 1.0),
        'cache_win_k': nrm(ks[3], kv_shape, 1.0),
        'cache_win_v': nrm(ks[4], kv_shape, 1.0),
        'norm_pre_mix': gain(ks[5], (DEPTH, D_MODEL)),
        'norm_post_mix': gain(ks[6], (DEPTH, D_MODEL)),
        'norm_pre_ffn': gain(ks[7], (DEPTH, D_MODEL)),
        'norm_post_ffn': gain(ks[8], (DEPTH, D_MODEL)),
        'w_in': nrm(ks[9], (DEPTH, D_MODEL, IN_WIDTH), D_MODEL ** -0.5),
        'chunk_ln_g': gain(ks[10], (DEPTH, D_A)),
        'chunk_ln_b': nrm(ks[11], (DEPTH, D_A), 0.02),
        'w_spatial': nrm(ks[12], (DEPTH, A_GROUPS, CHUNK, CHUNK), CHUNK ** -0.5),
        'b_spatial': 1.0 + nrm(ks[13], (DEPTH, A_GROUPS, CHUNK), 0.1),
        'conv_w': nrm(ks[14], (DEPTH, CONV_W, D_B), CONV_W ** -0.5),
        'conv_b': nrm(ks[15], (DEPTH, D_B), 0.02),
        'attn_sinks': nrm(ks[16], (DEPTH, N_HEADS), 0.5),
        'w_br_a': nrm(ks[17], (DEPTH, D_A, D_MODEL), D_A ** -0.5),
        'w_br_b': nrm(ks[18], (DEPTH, D_B, D_MODEL), D_B ** -0.5),
        'w_br_c': nrm(ks[19], (DEPTH, N_HEADS * HEAD_DIM, D_MODEL), (N_HEADS * HEAD_DIM) ** -0.5),
        'w_out': nrm(ks[20], (DEPTH, D_MODEL, D_MODEL), D_MODEL ** -0.5),
        'w_ffn_gate': nrm(ks[21], (DEPTH, D_MODEL, D_FF), D_MODEL ** -0.5),
        'w_ffn_up': nrm(ks[22], (DEPTH, D_MODEL, D_FF), D_MODEL ** -0.5),
        'w_ffn_down': nrm(ks[23], (DEPTH, D_FF, D_MODEL), D_FF ** -0.5),
    }


def reference(x_prompt, x_sample, state_conv, cache_win_k, cache_win_v,
              norm_pre_mix, norm_post_mix, norm_pre_ffn, norm_post_ffn, w_in,
              chunk_ln_g, chunk_ln_b, w_spatial, b_spatial, conv_w, conv_b, attn_sinks,
              w_br_a, w_br_b, w_br_c, w_out, w_ffn_gate, w_ffn_up, w_ffn_down):
    xp, xs = x_prompt, x_sample
    p_conv, p_k, p_v = [], [], []
    s_conv, s_k, s_v, s_cv = [], [], [], []
    for l in range(DEPTH):
        lp = {
            'norm_pre_mix': norm_pre_mix[l], 'norm_post_mix': norm_post_mix[l],
            'norm_pre_ffn': norm_pre_ffn[l], 'norm_post_ffn': norm_post_ffn[l],
            'w_in': w_in[l], 'chunk_ln_g': chunk_ln_g[l], 'chunk_ln_b': chunk_ln_b[l],
            'w_spatial': w_spatial[l], 'b_spatial': b_spatial[l],
            'conv_w': conv_w[l], 'conv_b': conv_b[l], 'attn_sinks': attn_sinks[l],
            'w_br_a': w_br_a[l], 'w_br_b': w_br_b[l], 'w_br_c': w_br_c[l], 'w_out': w_out[l],
            'w_ffn_gate': w_ffn_gate[l], 'w_ffn_up': w_ffn_up[l], 'w_ffn_down': w_ffn_down[l],
        }
        xp, c, k, v, _ = trunk_layer(xp, lp, True, None, None, None)
        p_conv.append(c)
        p_k.append(k)
        p_v.append(v)
        xs, c2, k2, v2, cv = trunk_layer(xs, lp, False, state_conv[l], cache_win_k[l], cache_win_v[l])
        s_conv.append(c2)
        s_k.append(k2)
        s_v.append(v2)
        s_cv.append(cv)
    return (xp, xs, jnp.stack(p_conv), jnp.stack(p_k), jnp.stack(p_v),
            jnp.stack(s_conv), jnp.stack(s_k), jnp.stack(s_v), jnp.stack(s_cv))
```

```python
import numpy as np
import ml_dtypes
from contextlib import ExitStack
import concourse.bass as bass
import concourse.mybir as mybir
from concourse.bass_utils import run_bass_kernel_spmd

F32 = mybir.dt.float32
BF16 = mybir.dt.bfloat16
AF = mybir.ActivationFunctionType
ALU = mybir.AluOpType

NCORE = 8
D = 1024
TP = 768
NGRP = 3
NSMP = 16
TMAX = TP + NSMP
NT = NGRP * TP + NSMP
NYT = 2048 + NSMP
NSLAB = 46
NB = 3
ATT_DEPTH = 2
PAST = 16384
EPS_RMS = 1e-6
EPS_LN = 1e-5
MASKNEG = -30000.0

K_PREMIX, K_POSTMIX, K_PREFFN, K_POSTFFN, K_LNG, K_LNB, K_CW0, K_CW1, K_CW2, K_CB = range(10)

QUEUES = ("pe", "act", "dve", "pool", "sp")
NDMASEM = 24
EPOCH = 400
NEPOCH = 8
KLIMIT = 9999
KGROUPS = ""
KSKIP = ""
KDBG = False


_CACHE = {}


class _Stop(Exception):
    pass


class Plan:
    def __init__(self):
        self.ops = {q: [] for q in QUEUES}
        self.state = {}
        self.dma_uses = [0] * NDMASEM
        self.tag = "setup"
        self.dma_rr = {"sp": 0, "pool": 0, "act": 0}
        self.dma_rng = {"sp": (0, 14), "pool": (14, 8), "act": (22, 2)}

    def _st(self, k):
        s = self.state.get(k)
        if s is None:
            s = [None, {}]
            self.state[k] = s
        return s

    def op(self, q, fn, reads=(), writes=(), dma=False, extra_deps=()):
        deps = []
        for k in reads:
            s = self._st(k)
            if s[0] is not None:
                deps.append(s[0])
        for k in writes:
            s = self._st(k)
            if s[0] is not None:
                deps.append(s[0])
            deps.extend(s[1].values())
        deps.extend(extra_deps)
        idx = len(self.ops[q])
        rec = dict(fn=fn, deps=deps, signal=False, dma=None, tag=self.tag)
        if dma:
            base, n = self.dma_rng[q]
            si = base + self.dma_rr[q] % n
            self.dma_rr[q] += 1
            prev = self.dma_uses[si]
            self.dma_uses[si] += 1
            if prev > 0:
                rec["deps"].append(("d", si, 16 * prev))
            rec["dma"] = si
            tok = ("d", si, 16 * (prev + 1))
        else:
            tok = ("e", q, idx)
        self.ops[q].append(rec)
        for k in reads:
            s = self._st(k)
            tk = tok[:2]
            old = s[1].get(tk)
            if old is None or old[2] < tok[2]:
                s[1][tk] = tok
        for k in writes:
            s = self._st(k)
            s[0] = tok
            s[1] = {}
        return tok

    def emit(self, block, sems, dsems):
        for q in QUEUES:
            for rec in self.ops[q]:
                for d in rec["deps"]:
                    if d[0] == "e":
                        if d[1] == q and q in ("pe", "sp"):
                            continue
                        self.ops[d[1]][d[2]]["signal"] = True
        cnt = {}
        for q in QUEUES:
            c = 0
            arr = []
            for rec in self.ops[q]:
                if rec["signal"]:
                    c += 1
                arr.append(c)
            cnt[q] = arr
        engs = dict(pe=block.tensor, act=block.scalar, dve=block.vector, pool=block.gpsimd, sp=block.sync)
        stats = {}
        for q in QUEUES:
            ops = self.ops[q]
            if not ops:
                continue

            def body(eng, q=q, ops=ops):
                waited = {}
                nw = 0
                mycnt = [0]
                for rec in ops:
                    need = {}
                    for d in rec["deps"]:
                        if d[0] == "e":
                            if d[1] == q and q in ("pe", "sp"):
                                continue
                            key = ("e", d[1])
                            val = cnt[d[1]][d[2]]
                        else:
                            key = ("d", d[1])
                            val = d[2]
                        if waited.get(key, 0) < val and need.get(key, 0) < val:
                            need[key] = val
                    for key, val in need.items():
                        if key[0] == "e":
                            eng.wait_ge(sems[key[1]][(val - 1) // EPOCH], (val - 1) % EPOCH + 1)
                        else:
                            eng.wait_ge(dsems[key[1]], val)
                        waited[key] = val
                        nw += 1
                    ins = rec["fn"](eng)
                    if rec["dma"] is not None:
                        ins.then_inc(dsems[rec["dma"]], 16)
                    elif rec["signal"]:
                        mycnt[0] += 1
                        ins.then_inc(sems[q][(mycnt[0] - 1) // EPOCH], 1)
                stats[q] = (len(ops), nw)

            engs[q](body)
        return stats


class Rot:
    def __init__(self, name, aps):
        self.name, self.aps, self.i = name, aps, 0

    def get(self):
        k = self.i % len(self.aps)
        self.i += 1
        return self.aps[k], (self.name, k)


def unit_sequence():
    seq = []
    seq += [("va", c) for c in range(8)]
    seq += [("ua", c) for c in range(8)]
    for c in range(8):
        seq += [("cg", c), ("hb", c), ("bg", c)]
    for c in range(8):
        seq += [("wa", c), ("ga", c)]
    seq += [("q", c) for c in range(8)]
    seq += [("k", 0), ("k", 1), ("v", 0), ("v", 1)]
    for c in range(8):
        seq += [("wb", c), ("gb", c)]
    for c in range(8):
        seq += [("wc", c), ("gc", c)]
    seq += [("wo", c) for c in range(8)]
    for j in range(22):
        seq += [("fg", j), ("fu", j)]
    return seq


def head_of(c, e):
    G2, r = c // 4, c % 4
    return 4 * (2 * G2 + e) + r


def build_program():
    nc = bass.Bass("TRN2", target_bir_lowering=False)

    def din(name, shape, dt=F32):
        return nc.dram_tensor(name, list(shape), dt, kind="ExternalInput").ap()

    def dout(name, shape, dt=F32):
        return nc.dram_tensor(name, list(shape), dt, kind="ExternalOutput").ap()

    xT = din("xT", [128, 8, NT])
    cosT = din("cosT", [128, NT])
    sinT = din("sinT", [128, NT])
    wst = din("wst", [2, NSLAB, 128, 4096])
    cvec = din("cvec", [128, 160])
    wsT_in = din("wsT_in", [2, 128, 8, 128])
    bs_bc = din("bs_bc", [2, 128, 8, 128])
    maskb_in = din("maskb_in", [128, 2, 1024])
    identf_in = din("identf_in", [128, 128])
    tri_in = din("tri_in", [128, 128])
    rotm_in = din("rotm_in", [128, 128], BF16)
    sinks_bc = din("sinks_bc", [128, 32])
    ws00_bc = din("ws00_bc", [16, 16])
    idm16_in = din("idm16_in", [16, 256])
    cm_in = din("cm_in", [128, 1024])
    histT = din("histT", [2, 128, 2, 8, NSMP])
    kcT_in = din("kcT_in", [2, 128, NSMP, 2, 128])
    ck_nat = din("ck_nat", [2, NSMP, 128, 256])
    cv_nat = din("cv_nat", [2, NSMP, 128, 256])
    sc_nat = din("sc_nat", [2, NSMP, 2, 1024])

    yT = dout("yT", [128, 8, NYT])
    pconvT = dout("pconvT", [2, 128, 8, 2])
    pkT = dout("pkT", [2, 128, 2, 128])
    pv = dout("pv", [2, 128, 256])
    sconv_newT = dout("sconv_newT", [2, 128, 8, NSMP])
    sconv_old = dout("sconv_old", [2, NSMP, 1024])
    sk_shift = dout("sk_shift", [2, NSMP, 127, 256])
    sk_newT = dout("sk_newT", [2, 128, 2, NSMP])
    sv_shift = dout("sv_shift", [2, NSMP, 127, 256])
    sv_new = dout("sv_new", [2, NSMP, 256])
    scvT = dout("scvT", [2, 128, 8, NSMP])
    if KDBG:
        dbgf = dout("dbgf", [3, 128, 8, TMAX])
        dbgb = dout("dbgb", [2, 128, 8, TMAX], BF16)

    P = Plan()
    out_toks = []

    with ExitStack() as es:
        def sb(name, shape, dt):
            return es.enter_context(nc.sbuf_tensor(name, list(shape), dt))

        xs = sb("xs", [128, 8, TMAX], F32)
        hb = sb("hb", [128, 8, TMAX], BF16)
        arena = sb("arena", [128, 22 * TMAX], BF16)
        merged = sb("merged", [128, 8, TMAX], F32)
        kcur = sb("kcur", [128, 2, TMAX], BF16)
        kprev = sb("kprev", [128, 2, 2, 128], BF16)
        vcur = sb("vcur", [128, 7, 256], BF16)
        vprev = sb("vprev", [128, 2, 256], BF16)
        kTf = sb("kTf", [128, 2, 144], F32)
        cxe_t = sb("cxe", [128, 2, 2 + TMAX], F32)
        cxh = sb("cxh", [128, 2, 8, 2], F32)
        scx = sb("scx", [128, 8, NSMP], F32)
        scv_sb = sb("scv_sb", [128, 8, NSMP], F32)
        cos_sb = sb("cos_sb", [128, TMAX], F32)
        sin_sb = sb("sin_sb", [128, TMAX], F32)
        rstd = sb("rstd", [128, TMAX], F32)
        sq_t = sb("sq_t", [128, 3, TMAX], BF16)
        tmpf_t = sb("tmpf_t", [128, 3, 1024], F32)
        pT_t = sb("pT_t", [128, ATT_DEPTH + 1, 1024], BF16)
        ring = sb("ring", [128, NB, 4096], BF16)
        cv_sb = sb("cv_sb", [128, 160], F32)
        wsTb = sb("wsTb", [128, 2, 8, 128], BF16)
        bias2 = sb("bias2", [128, 2, 8, 128], F32)
        maskb = sb("maskb", [128, 2, 1024], BF16)
        identf = sb("identf", [128, 128], F32)
        identb = sb("identb", [128, 128], BF16)
        tri = sb("tri", [128, 128], F32)
        ones = sb("ones", [128, 128], BF16)
        rotm = sb("rotm", [128, 128], BF16)
        sinkexp = sb("sinkexp", [128, 32], F32)
        ws00 = sb("ws00", [16, 16], F32)
        wsd = sb("wsd", [16, 2, 8, 16], BF16)
        idm16 = sb("idm16", [16, 256], F32)
        cst = sb("cst", [128, 4], F32)
        hist_sb = sb("hist_sb", [128, 2, 8, NSMP], F32)
        kcT = sb("kcT", [128, NSMP, 2, 128], BF16)
        vc = sb("vc", [128, NSMP, 256], BF16)
        lnst = sb("lnst", [128, 7, 2, 6], F32)
        lnmv = sb("lnmv", [128, 7, 2], F32)
        lnr = sb("lnr", [128, 7], F32)
        pd_sb = sb("pd_sb", [16, 256], BF16)
        cmb = sb("cmb", [128, 1024], BF16)

        ps = es.enter_context(nc.psum_tensor("ps", [128, 4096], F32))
        sems = {q: [es.enter_context(nc.semaphore("s_%s_%d" % (q, i))) for i in range(NEPOCH)] for q in QUEUES}
        dsems = [es.enter_context(nc.semaphore("d%d" % i)) for i in range(NDMASEM)]
        block = es.enter_context(nc.Block())

        buf1 = arena[:, 0:8 * TMAX].rearrange("p (c t) -> p c t", c=8)
        B2OFF = 8 * TMAX
        buf2 = arena[:, B2OFF:B2OFF + 8 * TMAX].rearrange("p (c t) -> p c t", c=8)
        vatm = arena[:, B2OFF:B2OFF + 7 * 1024].rearrange("p (i f) -> p i f", i=7)
        actb = arena[:, :].rearrange("p (c t) -> p c t", c=22)

        def arkeys(start, end):
            return [("ar", k) for k in range(start // TMAX, (end - 1) // TMAX + 1)]

        def vatm_keys(i):
            return arkeys(B2OFF + i * 1024, B2OFF + (i + 1) * 1024)

        tmpf = Rot("tmpf", [tmpf_t[:, i, :] for i in range(3)])
        sqb = Rot("sq", [sq_t[:, i, :] for i in range(3)])
        qbs = sqb
        pTs = Rot("pT", [pT_t[:, i, :] for i in range(ATT_DEPTH + 1)])
        cxes = Rot("cxe", [cxe_t[:, i, :] for i in range(2)])

        def psf(S):
            return ps[:, S * 1024:(S + 1) * 1024]

        CS = [0]
        slot_state = dict(rr=0, pinned=set())

        def alloc(pin=False):
            while True:
                s = slot_state["rr"] % 4
                slot_state["rr"] += 1
                if s not in slot_state["pinned"]:
                    break
            if pin:
                slot_state["pinned"].add(s)
            return s

        def unpin(s):
            slot_state["pinned"].discard(s)

        def cv(l, kind, c):
            i = l * 80 + kind * 8 + c
            return cv_sb[:, i:i + 1]

        def MM(out, lhsT, rhs, start, stop, reads, writes):
            P.op("pe", lambda e: e.matmul(out, lhsT=lhsT, rhs=rhs, start=start, stop=stop), reads, writes)

        def ACT(out, in_, func, reads, writes, scale=1.0, bias=None):
            if bias is None:
                P.op("act", lambda e: e.activation(out=out, in_=in_, func=func, scale=scale), reads, writes)
            else:
                P.op("act", lambda e: e.activation(out=out, in_=in_, func=func, scale=scale, bias=bias), reads, writes)

        def TT(out, in0, in1, op, reads, writes, q="dve"):
            if op == ALU.add and "ttadd" not in KSKIP:
                P.op(q, lambda e: e.scalar_tensor_tensor(out=out, in0=in0, scalar=1.0, in1=in1, op0=ALU.mult, op1=ALU.add), reads, writes)
            else:
                P.op(q, lambda e: e.tensor_tensor(out=out, in0=in0, in1=in1, op=op), reads, writes)

        def STT(out, in0, scalar, in1, op0, op1, reads, writes):
            P.op("dve", lambda e: e.scalar_tensor_tensor(out=out, in0=in0, scalar=scalar, in1=in1, op0=op0, op1=op1), reads, writes)

        def TS(out, in0, s1, s2, op0, op1, reads, writes, q="dve"):
            if s2 is None:
                P.op(q, lambda e: e.tensor_scalar(out=out, in0=in0, scalar1=s1, scalar2=None, op0=op0), reads, writes)
            else:
                P.op(q, lambda e: e.tensor_scalar(out=out, in0=in0, scalar1=s1, scalar2=s2, op0=op0, op1=op1), reads, writes)

        def CP(out, in_, reads, writes, q="dve"):
            if q == "act":
                P.op(q, lambda e: e.activation(out=out, in_=in_, func=AF.Identity), reads, writes)
            else:
                P.op(q, lambda e: e.tensor_copy(out=out, in_=in_), reads, writes)

        def DMA(out, in_, reads, writes, q="sp", is_out=False, **kw):
            t = P.op(q, lambda e: e.dma_start(out=out, in_=in_, **kw), reads, writes, dma=True)
            if is_out:
                out_toks.append(t)
            return t

        useq = unit_sequence()
        total_slabs = NGRP * 2 * NSLAB
        ws = dict(next_plan=0, oldest=0, upos=0, hold=False)

        def slab_coords(n):
            gl, s = divmod(n, NSLAB)
            return gl % 2, s

        def w_ensure():
            while ws["next_plan"] < min(ws["oldest"] + NB, total_slabs):
                n = ws["next_plan"]
                l, s = slab_coords(n)
                k = n % NB
                ncol = 4096 if s < 38 else 2816
                DMA(ring[:, k, 0:ncol], wst[l, s, :, 0:ncol], [], [("ring", k)], q="pool", max_dma_last_dim=8192)
                ws["next_plan"] += 1

        def w_unit(name, idx, glayer):
            u = ws["upos"]
            assert useq[u] == (name, idx), (useq[u], name, idx)
            ws["upos"] += 1
            n = glayer * NSLAB + u // 4
            if not ws["hold"] and n > ws["oldest"]:
                ws["oldest"] = n
            w_ensure()
            return n % NB, u % 4, ("ring", n % NB)

        def w_down(c, glayer):
            n = glayer * NSLAB + 38 + c
            if n > ws["oldest"]:
                ws["oldest"] = n
            w_ensure()
            return n % NB, ("ring", n % NB)

        def lhs_unit(uinfo, kc):
            k, j, _ = uinfo
            o = j * 1024 + kc * 128
            return ring[:, k, o:o + 128]

        def chunk_mm(S, uinfo, rhs_fn, rkey_fn, T):
            wkey = uinfo[2]
            for kc in range(8):
                lhsT = lhs_unit(uinfo, kc)
                r = rhs_fn(kc)
                for (c0, c1) in ((CS[0], 512), (512, T)):
                    MM(psf(S)[:, c0:c1], lhsT, r[:, c0:c1], kc == 0, kc == 7, [wkey] + rkey_fn(kc), [("ps", S)])

        DMA(cv_sb[:], cvec, [], ["cv"])
        DMA(identf[:], identf_in, [], ["identf"])
        DMA(tri[:], tri_in, [], ["tri"])
        DMA(rotm[:], rotm_in, [], ["rotm"])
        DMA(sinkexp[:], sinks_bc, [], ["sinkexp"])
        DMA(ws00[:], ws00_bc, [], ["ws00"])
        DMA(idm16[:], idm16_in, [], ["idm16"])
        DMA(bias2[:, 0], bs_bc[0], [], [("bias2", 0)])
        DMA(bias2[:, 1], bs_bc[1], [], [("bias2", 1)])
        DMA(maskb[:], maskb_in, [], ["maskb"], q="pool", max_dma_last_dim=8192)
        DMA(cmb[:], cm_in, [], ["cmb"], q="pool", max_dma_last_dim=8192)
        P.op("dve", lambda e: e.memset(ones[:], 1.0), [], ["ones"])
        P.op("dve", lambda e: e.memset(kprev[:], 0.0), [], [("kprev", 0), ("kprev", 1)])
        P.op("dve", lambda e: e.memset(vprev[:], 0.0), [], [("vprev", 0), ("vprev", 1)])
        P.op("dve", lambda e: e.memset(cxh[:], 0.0), [], [("cxh", 0), ("cxh", 1)])
        P.op("dve", lambda e: e.memset(cst[:, 0:1], EPS_RMS), [], ["cst"])
        P.op("dve", lambda e: e.memset(cst[:, 1:2], EPS_LN), [], ["cst"])
        CP(identb[:], identf[:], ["identf"], ["identb"])
        ACT(sinkexp[:], sinkexp[:], AF.Exp, ["sinkexp"], ["sinkexp"])
        for l in range(2):
            tw, twk = tmpf.get()
            twv = tw.rearrange("p (g t) -> p g t", g=8)
            DMA(twv, wsT_in[l], [], [twk])
            TT(wsTb[:, l], twv, tri[:].unsqueeze(1).broadcast_to([128, 8, 128]), ALU.mult, [twk, "tri"], [("wsTb", l)])
            S = alloc()
            for g in range(8):
                MM(psf(S)[:, g * 128:(g + 1) * 128], ones[:], wsTb[:, l, g, :], True, True, ["ones", ("wsTb", l)], [("ps", S)])
            for g in range(8):
                STT(bias2[:, l, g, :], psf(S)[:, g * 128:(g + 1) * 128], cv(l, K_LNB, g), bias2[:, l, g, :], ALU.mult, ALU.add,
                    [("ps", S), "cv", ("bias2", l)], [("bias2", l)])
            for g in range(8):
                TS(wsd[:, l, g, :], identf[0:16, 0:16], ws00[:, l * 8 + g:l * 8 + g + 1], None, ALU.mult, None, ["identf", "ws00"], ["wsd"])

        def stats_finish(SN, T):
            ACT(rstd[:, CS[0]:T], psf(SN)[:, CS[0]:T], AF.Ln, [("ps", SN), "cst"], ["rstd"], scale=1.0 / D, bias=cst[:, 0:1])
            ACT(rstd[:, CS[0]:T], rstd[:, CS[0]:T], AF.Exp, ["rstd"], ["rstd"], scale=-0.5)
            unpin(SN)

        def stats_sq(src, skeys, T):
            sq, sk = sqb.get()
            ACT(sq[:, CS[0]:T], src, AF.Square, skeys, [sk])
            return sq, sk

        def stats_mm(SN, sq, sk, c, T):
            for (c0, c1) in ((CS[0], 512), (512, T)):
                MM(psf(SN)[:, c0:c1], ones[:], sq[:, c0:c1], c == 0, c == 7, [sk, "ones"], [("ps", SN)])

        def stats_add(SN, src, skeys, c, T):
            sq, sk = stats_sq(src, skeys, T)
            stats_mm(SN, sq, sk, c, T)

        def norm_to_hb(l, kind, T):
            SN = alloc(pin=True)
            for c in range(8):
                stats_add(SN, xs[:, c, CS[0]:T], [("xs", c)], c, T)
            stats_finish(SN, T)
            for c in range(8):
                STT(hb[:, c, CS[0]:T], xs[:, c, CS[0]:T], cv(l, kind, c), rstd[:, CS[0]:T], ALU.mult, ALU.mult,
                    [("xs", c), "cv", "rstd"], [("hb", c)])

        def post_apply(l, kind, T):
            for c in range(8):
                t, tk = tmpf.get()
                STT(t[:, CS[0]:T], merged[:, c, CS[0]:T], cv(l, kind, c), rstd[:, CS[0]:T], ALU.mult, ALU.mult,
                    [("mg", c), "cv", "rstd"], [tk])
                TT(xs[:, c, CS[0]:T], xs[:, c, CS[0]:T], t[:, CS[0]:T], ALU.add, [("xs", c), tk], [("xs", c)])

        hbr = lambda kc: hb[:, kc, :]
        hbk = lambda kc: [("hb", kc)]

        def gated_branch(l, glayer, T, wname, gname, src, srckeys, mode):
            for c in range(8):
                uw = w_unit(wname, c, glayer)
                S = alloc()
                chunk_mm(S, uw, lambda kc: src[:, kc, :], lambda kc: srckeys(kc), T)
                ug = w_unit(gname, c, glayer)
                S2 = alloc()
                chunk_mm(S2, ug, hbr, hbk, T)
                tg, tgk = tmpf.get()
                ACT(tg[:, CS[0]:T], psf(S2)[:, CS[0]:T], AF.Sigmoid, [("ps", S2)], [tgk])
                if mode == 0:
                    TT(merged[:, c, CS[0]:T], psf(S)[:, CS[0]:T], tg[:, CS[0]:T], ALU.mult, [("ps", S), tgk], [("mg", c)])
                else:
                    TT(tg[:, CS[0]:T], psf(S)[:, CS[0]:T], tg[:, CS[0]:T], ALU.mult, [("ps", S), tgk], [tgk])
                    if mode == 1:
                        TT(merged[:, c, CS[0]:T], merged[:, c, CS[0]:T], tg[:, CS[0]:T], ALU.add, [("mg", c), tgk], [("mg", c)])
                    else:
                        TT(buf1[:, c, CS[0]:T], merged[:, c, CS[0]:T], tg[:, CS[0]:T], ALU.add, [("mg", c), tgk], [("ar", c)])

        phase = [0]

        def chk():
            phase[0] += 1
            if phase[0] > KLIMIT:
                raise _Stop()

        def main_loops():
          for gi in range(NGRP):
            if KGROUPS and str(gi) not in KGROUPS:
                continue
            if KGROUPS and ws["next_plan"] < gi * 2 * NSLAB:
                ws["next_plan"] = ws["oldest"] = gi * 2 * NSLAB
            if True:
                T = TP + (NSMP if gi == NGRP - 1 else 0)
                last = gi == NGRP - 1
                ntile = 7 if last else 6
                col0 = gi * TP
                for c in range(8):
                    DMA(xs[:, c, :T], xT[:, c, col0:col0 + T], [], [("xs", c)])
                DMA(cos_sb[:, :T], cosT[:, col0:col0 + T], [], ["cos"])
                DMA(sin_sb[:, :T], sinT[:, col0:col0 + T], [], ["sin"])

                for l in range(2):
                    glayer = gi * 2 + l
                    ws["upos"] = 0
                    ck = (0, 128)[l] if gi == 0 else 0
                    cs = (128, 256)[l] if gi == 0 else 0
                    i0 = cs // 128
                    i0k = ck // 128
                    if last:
                        DMA(hist_sb[:], histT[l], [], ["hist"])
                        DMA(kcT[:], kcT_in[l], [], ["kcT"], q="pool", max_dma_last_dim=8192)
                        DMA(vc[:], cv_nat[l].rearrange("s k f -> k s f"), [], ["vc"], q="pool", max_dma_last_dim=8192)
                        if gi == NGRP - 1:
                            DMA(sk_shift[l], ck_nat[l, :, 1:128, :], [], [], is_out=True)
                            DMA(sv_shift[l], cv_nat[l, :, 1:128, :], [], [], is_out=True)
                            DMA(sconv_old[l], sc_nat[l, :, 1, :], [], [], is_out=True)

                    chk()
                    P.tag = "P1"
                    CS[0] = ck
                    norm_to_hb(l, K_PREMIX, T)
                    CS[0] = cs

                    chk()
                    P.tag = "P2"
                    ws["hold"] = True
                    uva = [w_unit("va", c, glayer) for c in range(8)]
                    for i in range(i0, ntile):
                        M = 128 if i < 6 else NSMP
                        tc0 = i * 128
                        S = alloc()
                        for h2 in range(2):
                            u0 = uva[h2 * 4]
                            slab = ring[:, u0[0], :].rearrange("p (u k c) -> p u k c", u=4, k=8)
                            for kc in range(8):
                                MM(psf(S)[0:M, h2 * 512:(h2 + 1) * 512], hb[:, kc, tc0:tc0 + M], slab[:, :, kc, :], kc == 0, kc == 7,
                                   [u0[2], ("hb", kc)], [("ps", S)])
                        tg, tgk = tmpf.get()
                        ACT(tg[0:M, :], psf(S)[0:M, :], AF.Gelu_apprx_tanh, [("ps", S)], [tgk])
                        for h2 in range(2):
                            P.op("dve", lambda e, tg=tg, M=M, i=i, h2=h2: e.bn_stats(out=lnst[0:M, i, h2, :], in_=tg[0:M, h2 * 512:(h2 + 1) * 512]),
                                 [tgk], [("lnst", i)])
                        P.op("dve", lambda e, M=M, i=i: e.bn_aggr(out=lnmv[0:M, i, :], in_=lnst[0:M, i, :, :].rearrange("p a b -> p (a b)")),
                             [("lnst", i)], [("lnmv", i)])
                        CP(vatm[0:M, i, :], tg[0:M, :], [tgk], vatm_keys(i))
                        if i == 6:
                            tlast, tlastk = tg, tgk
                    ws["hold"] = False
                    nl = ntile
                    ACT(lnr[:, i0:nl], lnmv[:, i0:nl, 1], AF.Ln, [("lnmv", i) for i in range(i0, nl)] + ["cst"], ["lnr"], bias=cst[:, 1:2])
                    ACT(lnr[:, i0:nl], lnr[:, i0:nl], AF.Exp, ["lnr"], ["lnr"], scale=-0.5)
                    for i in range(i0, ntile):
                        M = 128 if i < 6 else NSMP
                        TS(vatm[0:M, i, :], vatm[0:M, i, :], lnmv[0:M, i, 0:1], lnr[0:M, i:i + 1], ALU.subtract, ALU.mult,
                           vatm_keys(i) + [("lnmv", i), "lnr"], vatm_keys(i))
                    for c in range(8):
                        u = w_unit("ua", c, glayer)
                        S = alloc()
                        chunk_mm(S, u, hbr, hbk, T)
                        ACT(buf1[:, c, CS[0]:T], psf(S)[:, CS[0]:T], AF.Gelu_apprx_tanh, [("ps", S)], [("ar", c)])
                    if last:
                        TS(tlast[0:NSMP, :], tlast[0:NSMP, :], lnmv[0:NSMP, 6, 0:1], lnr[0:NSMP, 6:7], ALU.subtract, ALU.mult,
                           [tlastk, ("lnmv", 6), "lnr"], [tlastk])
                        S = alloc()
                        for c in range(8):
                            P.op("pe", lambda e, S=S, c=c, tlast=tlast: e.transpose(psf(S)[:, c * NSMP:(c + 1) * NSMP], tlast[0:NSMP, c * 128:(c + 1) * 128], identf[0:NSMP, 0:NSMP]),
                                 [tlastk, "identf"], [("ps", S)])
                        for c in range(8):
                            ACT(scv_sb[:, c, :], psf(S)[:, c * NSMP:(c + 1) * NSMP], AF.Identity, [("ps", S), "cv"], ["scv"],
                                scale=cv(l, K_LNG, c), bias=cv(l, K_LNB, c))
                        DMA(scvT[l], scv_sb[:], ["scv"], [], is_out=True)

                    chk()
                    P.tag = "spatial"
                    for g in range(8):
                        S = alloc()
                        for i in range(i0, 6):
                            MM(psf(S)[:, i * 128:(i + 1) * 128], vatm[:, i, g * 128:(g + 1) * 128], wsTb[:, l, g, :], True, True,
                               vatm_keys(i) + [("wsTb", l)], [("ps", S)])
                        if last:
                            MM(psf(S)[:, TP:T], vatm[0:NSMP, 6, g * 128:(g + 1) * 128], wsd[:, l, g, :], True, True,
                               vatm_keys(6) + ["wsd"], [("ps", S)])
                        t, tk = tmpf.get()
                        STT(t[:, cs:TP].rearrange("p (i t) -> p i t", i=6 - i0), psf(S)[:, cs:TP].rearrange("p (i t) -> p i t", i=6 - i0), cv(l, K_LNG, g),
                            bias2[:, l, g, :].unsqueeze(1).broadcast_to([128, 6 - i0, 128]), ALU.mult, ALU.add,
                            [("ps", S), "cv", ("bias2", l)], [tk])
                        if last:
                            STT(t[:, TP:T], psf(S)[:, TP:T], cv(l, K_LNG, g), bias2[:, l, g, 0:1].broadcast_to([128, NSMP]), ALU.mult, ALU.add,
                                [("ps", S), "cv", ("bias2", l)], [tk])
                        TT(buf1[:, g, CS[0]:T], buf1[:, g, CS[0]:T], t[:, CS[0]:T], ALU.mult, [("ar", g), tk], [("ar", g)])

                    chk()
                    P.tag = "P3"
                    for c in range(8):
                        CS[0] = ck
                        uc = w_unit("cg", c, glayer)
                        Sc = alloc()
                        chunk_mm(Sc, uc, hbr, hbk, T)
                        uh = w_unit("hb", c, glayer)
                        Sh = alloc()
                        chunk_mm(Sh, uh, hbr, hbk, T)
                        CS[0] = cs
                        ub = w_unit("bg", c, glayer)
                        Sb = alloc()
                        chunk_mm(Sb, ub, hbr, hbk, T)
                        CS[0] = ck
                        th, thk = tmpf.get()
                        ACT(th[:, CS[0]:T], psf(Sh)[:, CS[0]:T], AF.Identity, [("ps", Sh)], [thk])
                        cxe, cxk = cxes.get()
                        CP(cxe[:, 0:2], cxh[:, l, c, :], [("cxh", l)], [cxk], q="act")
                        TT(cxe[:, 2 + ck:2 + T], psf(Sc)[:, CS[0]:T], th[:, CS[0]:T], ALU.mult, [("ps", Sc), thk], [cxk])
                        CS[0] = cs
                        ACT(th[:, cs:TP], cxe[:, 2 + cs:2 + TP], AF.Identity, [cxk, "cv"], [thk], scale=cv(l, K_CW2, c), bias=cv(l, K_CB, c))
                        STT(th[:, cs:TP], cxe[:, 1 + cs:1 + TP], cv(l, K_CW1, c), th[:, cs:TP], ALU.mult, ALU.add, [cxk, "cv", thk], [thk])
                        STT(th[:, cs:TP], cxe[:, cs:TP], cv(l, K_CW0, c), th[:, cs:TP], ALU.mult, ALU.add, [cxk, "cv", thk], [thk])
                        TT(buf2[:, c, cs:TP], psf(Sb)[:, cs:TP], th[:, cs:TP], ALU.mult, [("ps", Sb), thk], [("ar", 8 + c)])
                        CP(cxh[:, l, c, :], cxe[:, TP:TP + 2], [cxk], [("cxh", l)], q="act")
                        if last:
                            sc_ = slice(TP, T)
                            ACT(th[:, sc_], cxe[:, 2 + TP:2 + T], AF.Identity, [cxk, "cv"], [thk], scale=cv(l, K_CW2, c), bias=cv(l, K_CB, c))
                            STT(th[:, sc_], hist_sb[:, 1, c, :], cv(l, K_CW1, c), th[:, sc_], ALU.mult, ALU.add, ["hist", "cv", thk], [thk])
                            STT(th[:, sc_], hist_sb[:, 0, c, :], cv(l, K_CW0, c), th[:, sc_], ALU.mult, ALU.add, ["hist", "cv", thk], [thk])
                            TT(buf2[:, c, sc_], psf(Sb)[:, sc_], th[:, sc_], ALU.mult, [("ps", Sb), thk], [("ar", 8 + c)])
                            CP(scx[:, c, :], cxe[:, 2 + TP:2 + T], [cxk], ["scx"], q="act")
                    if last:
                        DMA(pconvT[l], cxh[:, l], [("cxh", l)], [], is_out=True)
                        DMA(sconv_newT[l], scx[:], ["scx"], [], is_out=True)

                    chk()
                    P.tag = "P4"
                    gated_branch(l, glayer, T, "wa", "ga", buf1, lambda kc: [("ar", kc)], 0)
                    if KDBG and gi == 0 and l == 0:
                        DMA(dbgf[0], merged[:], [("mg", c) for c in range(8)], [], is_out=True)

                    chk()
                    P.tag = "P5"
                    def rope_tail(c, qb, qbk):
                        isq = c < 8
                        CS[0] = cs if isq else ck
                        S2 = alloc()
                        for (c0, c1) in ((CS[0], 512), (512, T)):
                            MM(psf(S2)[:, c0:c1], rotm[:], qb[:, c0:c1], True, True, ["rotm", qbk], [("ps", S2)])
                        t1, t1k = tmpf.get()
                        t2, t2k = tmpf.get()
                        TT(t1[:, CS[0]:T], qb[:, CS[0]:T], cos_sb[:, CS[0]:T], ALU.mult, [qbk, "cos"], [t1k])
                        TT(t2[:, CS[0]:T], psf(S2)[:, CS[0]:T], sin_sb[:, CS[0]:T], ALU.mult, [("ps", S2), "sin"], [t2k])
                        if isq:
                            TT(buf1[:, c, CS[0]:T], t1[:, CS[0]:T], t2[:, CS[0]:T], ALU.add, [t1k, t2k], [("ar", c)])
                        else:
                            G2 = c - 8
                            TT(kcur[:, G2, CS[0]:T], t1[:, CS[0]:T], t2[:, CS[0]:T], ALU.add, [t1k, t2k], [("kcur", G2)])
                            if last:
                                TT(kTf[:, G2, 0:T - 640], t1[:, 640:T], t2[:, 640:T], ALU.add, [t1k, t2k], ["kTf"])

                    pend = None
                    for c in range(10):
                        isq = c < 8
                        CS[0] = cs if isq else ck
                        u = w_unit("q" if isq else "k", c if isq else c - 8, glayer)
                        S = alloc()
                        chunk_mm(S, u, hbr, hbk, T)
                        qb, qbk = qbs.get()
                        ACT(qb[:, CS[0]:T], psf(S)[:, CS[0]:T], AF.Identity, [("ps", S)], [qbk])
                        if pend is not None:
                            rope_tail(*pend)
                        pend = (c, qb, qbk)
                    rope_tail(*pend)
                    CS[0] = cs
                    if last:
                        DMA(pkT[l], kTf[:, :, 0:128], ["kTf"], [], is_out=True)
                        DMA(sk_newT[l], kTf[:, :, 128:144], ["kTf"], [], is_out=True)
                    uv0 = w_unit("v", 0, glayer)
                    uv1 = w_unit("v", 1, glayer)
                    vslab = ring[:, uv0[0], :].rearrange("p (u k c) -> p u k c", u=4, k=8)
                    Sv = None
                    for i in range(i0k, ntile):
                        M = 128 if i < 6 else NSMP
                        j = (i - i0k) % 4
                        if j == 0:
                            Sv = alloc()
                        for kc in range(8):
                            MM(psf(Sv)[0:M, j * 256:(j + 1) * 256], hb[:, kc, i * 128:i * 128 + M], vslab[:, 2:4, kc, :], kc == 0, kc == 7,
                               [uv0[2], ("hb", kc)], [("ps", Sv)])
                        ACT(vcur[0:M, i, :], psf(Sv)[0:M, j * 256:(j + 1) * 256], AF.Identity, [("ps", Sv)], [("vcur", i)])
                        if last and i == 5:
                            vf, vfk = tmpf.get()
                            ACT(vf[:, 0:256], psf(Sv)[:, j * 256:(j + 1) * 256], AF.Identity, [("ps", Sv)], [vfk])
                            DMA(pv[l], vf[:, 0:256], [vfk], [], is_out=True)
                        if last and i == 6:
                            vf, vfk = tmpf.get()
                            ACT(vf[0:NSMP, 0:256], psf(Sv)[0:NSMP, j * 256:(j + 1) * 256], AF.Identity, [("ps", Sv)], [vfk])
                            DMA(sv_new[l], vf[0:NSMP, 0:256], [vfk], [], is_out=True)

                    chk()
                    P.tag = "P6"
                    gated_branch(l, glayer, T, "wb", "gb", buf2, lambda kc: [("ar", 8 + kc)], 1)
                    if KDBG and gi == 0 and l == 0:
                        DMA(dbgf[1], merged[:], [("mg", c) for c in range(8)], [], is_out=True)

                    chk()
                    P.tag = "P7"
                    qkeys = [("ar", c) for c in range(8)]
                    def att_a(i, g):
                        ti = gi * 6 + i
                        mprev = 2 if ti == 2 else 1
                        tsl = slice(i * 128, (i + 1) * 128)
                        G2, e = g // 2, g % 2
                        rows = slice(e * 64, (e + 1) * 64)
                        SA = alloc()
                        qr = buf1[rows, G2 * 4:(G2 + 1) * 4, tsl]
                        if i == 0:
                            kp, kpk = kprev[rows, l, G2, :], ("kprev", l)
                        else:
                            kp, kpk = kcur[rows, G2, (i - 1) * 128:i * 128], ("kcur", G2)
                        kc_ = kcur[rows, G2, tsl]
                        MM(psf(SA)[:, 0:512], kp, qr, True, True, [kpk] + qkeys[G2 * 4:(G2 + 1) * 4], [("ps", SA)])
                        MM(psf(SA)[:, 512:1024], kc_, qr, True, True, [("kcur", G2)] + qkeys[G2 * 4:(G2 + 1) * 4], [("ps", SA)])
                        pt, ptk = pTs.get()
                        ACT(pt[:, :], psf(SA)[:, :], AF.Exp, [("ps", SA)], [ptk], scale=0.125)
                        TT(pt[:, :], pt[:, :], maskb[:, mprev - 1, :], ALU.mult, [ptk, "maskb"], [ptk])
                        return (i, g, pt, ptk)

                    def att_b(i, g, pt, ptk):
                        tsl = slice(i * 128, (i + 1) * 128)
                        G2, e = g // 2, g % 2
                        rows = slice(e * 64, (e + 1) * 64)
                        SB = alloc()
                        if i == 0:
                            vp, vpk = vprev[:, l, g * 64:(g + 1) * 64], ("vprev", l)
                        else:
                            vp, vpk = vcur[:, i - 1, g * 64:(g + 1) * 64], ("vcur", i - 1)
                        MM(psf(SB)[rows, 0:512], vp, pt[:, 0:512], True, False, [vpk, ptk], [("ps", SB)])
                        MM(psf(SB)[rows, 0:512], vcur[:, i, g * 64:(g + 1) * 64], pt[:, 512:1024], False, True, [("vcur", i), ptk], [("ps", SB)])
                        MM(psf(SB)[:, 512:1024], ones[:], pt[:, 0:512], True, False, ["ones", ptk], [("ps", SB)])
                        MM(psf(SB)[:, 512:1024], ones[:], pt[:, 512:1024], False, True, ["ones", ptk], [("ps", SB)])
                        tl, tlk = tmpf.get()
                        hc0 = l * 16 + 4 * g
                        STT(tl[:, 0:512].rearrange("p (r t) -> p r t", r=4), psf(SB)[:, 512:1024].rearrange("p (r t) -> p r t", r=4), 1.0,
                            sinkexp[:, hc0:hc0 + 4].unsqueeze(2).broadcast_to([128, 4, 128]), ALU.mult, ALU.add,
                            [("ps", SB), "sinkexp"], [tlk])
                        ACT(tl[:, 0:512], tl[:, 0:512], AF.Ln, [tlk], [tlk])
                        ACT(tl[:, 0:512], tl[:, 0:512], AF.Exp, [tlk], [tlk], scale=-1.0)
                        TT(buf2[rows, G2 * 4:(G2 + 1) * 4, tsl], psf(SB)[rows, 0:512].rearrange("p (r t) -> p r t", r=4),
                           tl[rows, 0:512].rearrange("p (r t) -> p r t", r=4), ALU.mult,
                           [("ps", SB), tlk], [("ar", 8 + G2 * 4 + r) for r in range(4)])

                    pend = []
                    for i in range(i0, 6):
                        for g in range(4):
                            pend.append(att_a(i, g))
                            if len(pend) > ATT_DEPTH:
                                att_b(*pend.pop(0))
                    while pend:
                        att_b(*pend.pop(0))
                    if last:
                        chk()
                        SB = alloc(pin=True)
                        for g in range(4):
                            G2, e = g // 2, g % 2
                            rows = slice(e * 64, (e + 1) * 64)
                            qg = buf1[rows, G2 * 4:(G2 + 1) * 4, TP:T]
                            SA = alloc()
                            for s in range(NSMP):
                                MM(psf(SA)[:, s * 64:(s + 1) * 64], kcT[rows, s, G2, :], qg, True, True,
                                   ["kcT"] + qkeys[G2 * 4:(G2 + 1) * 4], [("ps", SA)])
                            pt, ptk = pTs.get()
                            ACT(pt[:, :], psf(SA)[:, :], AF.Exp, [("ps", SA)], [ptk], scale=0.125)
                            TT(pt[:, :], pt[:, :], cmb[:, :], ALU.mult, [ptk, "cmb"], [ptk])
                            SN2 = alloc()
                            MM(psf(SN2)[0:NSMP, 0:64], kcur[rows, G2, TP:T], qg, True, True,
                               [("kcur", G2)] + qkeys[G2 * 4:(G2 + 1) * 4], [("ps", SN2)])
                            tn, tnk = tmpf.get()
                            ACT(tn[0:NSMP, 0:64], psf(SN2)[0:NSMP, 0:64], AF.Exp, [("ps", SN2)], [tnk], scale=0.125)
                            TT(pd_sb[:, g * 64:(g + 1) * 64], tn[0:NSMP, 0:64], idm16[:, 0:64], ALU.mult, [tnk, "idm16"], [("pd", g)])
                            for s in range(NSMP):
                                MM(psf(SB)[rows, g * 64:(g + 1) * 64], vc[:, s, g * 64:(g + 1) * 64], pt[:, s * 64:(s + 1) * 64], s == 0, False,
                                   ["vc", ptk], [("ps", SB)])
                            MM(psf(SB)[rows, g * 64:(g + 1) * 64], vcur[0:NSMP, 6, g * 64:(g + 1) * 64], pd_sb[:, g * 64:(g + 1) * 64], False, True,
                               [("vcur", 6), ("pd", g)], [("ps", SB)])
                            for s in range(NSMP):
                                MM(psf(SB)[:, 512 + g * 64:512 + (g + 1) * 64], ones[:], pt[:, s * 64:(s + 1) * 64], s == 0, False,
                                   ["ones", ptk], [("ps", SB)])
                            MM(psf(SB)[:, 512 + g * 64:512 + (g + 1) * 64], ones[0:NSMP, :], pd_sb[:, g * 64:(g + 1) * 64], False, True,
                               ["ones", ("pd", g)], [("ps", SB)])
                        tl, tlk = tmpf.get()
                        for g in range(4):
                            hc0 = l * 16 + 4 * g
                            STT(tl[:, g * 64:(g + 1) * 64].rearrange("p (r s) -> p r s", r=4),
                                psf(SB)[:, 512 + g * 64:512 + (g + 1) * 64].rearrange("p (r s) -> p r s", r=4), 1.0,
                                sinkexp[:, hc0:hc0 + 4].unsqueeze(2).broadcast_to([128, 4, NSMP]), ALU.mult, ALU.add,
                                [("ps", SB), "sinkexp"], [tlk])
                        ACT(tl[:, 0:256], tl[:, 0:256], AF.Ln, [tlk], [tlk])
                        ACT(tl[:, 0:256], tl[:, 0:256], AF.Exp, [tlk], [tlk], scale=-1.0)
                        for g in range(4):
                            G2, e = g // 2, g % 2
                            rows = slice(e * 64, (e + 1) * 64)
                            TT(buf2[rows, G2 * 4:(G2 + 1) * 4, TP:T], psf(SB)[rows, g * 64:(g + 1) * 64].rearrange("p (r s) -> p r s", r=4),
                               tl[rows, g * 64:(g + 1) * 64].rearrange("p (r s) -> p r s", r=4), ALU.mult,
                               [("ps", SB), tlk], [("ar", 8 + G2 * 4 + r) for r in range(4)])
                        unpin(SB)
                    CP(kprev[:, l], kcur[:, :, 640:768], [("kcur", 0), ("kcur", 1)], [("kprev", l)], q="act")
                    CP(vprev[:, l, :], vcur[:, 5, :], [("vcur", 5)], [("vprev", l)], q="act")

                    chk()
                    P.tag = "P8"
                    if KDBG and gi == 0 and l == 0:
                        DMA(dbgb[0], buf2[:], [("ar", 8 + c) for c in range(8)], [], is_out=True)
                    gated_branch(l, glayer, T, "wc", "gc", buf2, lambda kc: [("ar", 8 + kc)], 2)
                    if KDBG and gi == 0 and l == 0:
                        DMA(dbgb[1], buf1[:], [("ar", c) for c in range(8)], [], is_out=True)

                    chk()
                    P.tag = "P9"
                    SN = alloc(pin=True)
                    for c in range(8):
                        u = w_unit("wo", c, glayer)
                        S = alloc()
                        chunk_mm(S, u, lambda kc: buf1[:, kc, :], lambda kc: [("ar", kc)], T)
                        ACT(merged[:, c, CS[0]:T], psf(S)[:, CS[0]:T], AF.Identity, [("ps", S)], [("mg", c)])
                        sq, sk = stats_sq(psf(S)[:, CS[0]:T], [("ps", S)], T)
                        if c > 0:
                            stats_mm(SN, *pend_sq, c - 1, T)
                        pend_sq = (sq, sk)
                    stats_mm(SN, *pend_sq, 7, T)
                    stats_finish(SN, T)
                    post_apply(l, K_POSTMIX, T)

                    chk()
                    P.tag = "P10..12"
                    norm_to_hb(l, K_PREFFN, T)
                    for j in range(22):
                        ug = w_unit("fg", j, glayer)
                        S1 = alloc()
                        chunk_mm(S1, ug, hbr, hbk, T)
                        uu = w_unit("fu", j, glayer)
                        S2 = alloc()
                        chunk_mm(S2, uu, hbr, hbk, T)
                        t, tk = tmpf.get()
                        ACT(t[:, CS[0]:T], psf(S1)[:, CS[0]:T], AF.Silu, [("ps", S1)], [tk])
                        TT(actb[:, j, CS[0]:T], psf(S2)[:, CS[0]:T], t[:, CS[0]:T], ALU.mult, [("ps", S2), tk], [("ar", j)])
                    SN = alloc(pin=True)
                    for c in range(8):
                        k, wkey = w_down(c, glayer)
                        S = alloc()
                        for kc in range(22):
                            lhsT = ring[:, k, kc * 128:(kc + 1) * 128]
                            for (c0, c1) in ((CS[0], 512), (512, T)):
                                MM(psf(S)[:, c0:c1], lhsT, actb[:, kc, c0:c1], kc == 0, kc == 21, [wkey, ("ar", kc)], [("ps", S)])
                        ACT(merged[:, c, CS[0]:T], psf(S)[:, CS[0]:T], AF.Identity, [("ps", S)], [("mg", c)])
                        sq, sk = stats_sq(psf(S)[:, CS[0]:T], [("ps", S)], T)
                        if c > 0:
                            stats_mm(SN, *pend_sq, c - 1, T)
                        pend_sq = (sq, sk)
                    stats_mm(SN, *pend_sq, 7, T)
                    stats_finish(SN, T)
                    post_apply(l, K_POSTFFN, T)

                chk()
                if gi == 0:
                    DMA(yT[:, :, 0:512], xs[:, :, 256:768], [("xs", c) for c in range(8)], [], is_out=True)
                elif gi == 1:
                    DMA(yT[:, :, 512:1280], xs[:, :, 0:768], [("xs", c) for c in range(8)], [], is_out=True)
                else:
                    DMA(yT[:, :, 1280:NYT], xs[:, :, 0:T], [("xs", c) for c in range(8)], [], is_out=True)

        try:
            main_loops()
        except _Stop:
            pass
        P.op("sp", lambda e: e.nop(), extra_deps=list(out_toks))
        stats = P.emit(block, sems, dsems)
    _CACHE['plan'] = P
    return nc, stats


IN_OFF = dict(ua=0, va=1024, bg=2048, cg=3072, hb=4096, q=5120, k=6144, v=6400, ga=6656, gb=7680, gc=8704)


def _unit(W, cols, rows=None):
    Wr = W if rows is None else W[rows]
    blk = Wr[:, cols]
    KC = blk.shape[0] // 128
    return blk.reshape(KC, 128, 128).transpose(1, 0, 2).reshape(128, KC * 128)


def pack_weights(inp):
    out = np.zeros((2, NSLAB, 128, 4096), np.float32)
    ar = np.arange(128)
    qcols = {}
    for c in range(8):
        cols = np.empty(128, np.int64)
        for e in range(2):
            h = head_of(c, e)
            cols[e * 64:(e + 1) * 64] = h * 64 + np.arange(64)
        qcols[c] = cols
    rows_c = np.concatenate([qcols[c] for c in range(8)])
    for l in range(2):
        w_in = np.asarray(inp["w_in"][l])
        mats = dict(wa=np.asarray(inp["w_br_a"][l]), wb=np.asarray(inp["w_br_b"][l]), wc=np.asarray(inp["w_br_c"][l]),
                    wo=np.asarray(inp["w_out"][l]), fg=np.asarray(inp["w_ffn_gate"][l]), fu=np.asarray(inp["w_ffn_up"][l]))
        fd = np.asarray(inp["w_ffn_down"][l])
        units = []
        for (name, idx) in unit_sequence():
            if name in ("ua", "va", "bg", "cg", "hb", "k", "v", "ga", "gb", "gc"):
                units.append(_unit(w_in, IN_OFF[name] + idx * 128 + ar))
            elif name == "q":
                units.append(_unit(w_in, IN_OFF["q"] + qcols[idx]))
            elif name == "wc":
                units.append(_unit(mats["wc"], idx * 128 + ar, rows=rows_c))
            else:
                units.append(_unit(mats[name], idx * 128 + ar))
        for s in range(38):
            out[l, s] = np.concatenate(units[s * 4:(s + 1) * 4], axis=1)
        for c in range(8):
            out[l, 38 + c, :, 0:2816] = _unit(fd, c * 128 + ar)
    return out


def fm(v):
    v = np.asarray(v)
    return np.moveaxis(v.reshape(v.shape[:-1] + (8, 128)), -1, 0)


def prep(inp):
    f32 = np.float32
    x_prompt = np.asarray(inp["x_prompt"], f32)[0]
    x_sample = np.asarray(inp["x_sample"], f32)[:, 0]
    state_conv = np.asarray(inp["state_conv"], f32)
    ck = np.asarray(inp["cache_win_k"], f32)
    cvv = np.asarray(inp["cache_win_v"], f32)
    shared = {}
    shared["wst"] = pack_weights(inp)
    cvec = np.zeros((128, 160), f32)
    for l in range(2):
        vecs = [inp["norm_pre_mix"][l], inp["norm_post_mix"][l], inp["norm_pre_ffn"][l], inp["norm_post_ffn"][l],
                inp["chunk_ln_g"][l], inp["chunk_ln_b"][l], inp["conv_w"][l][0], inp["conv_w"][l][1], inp["conv_w"][l][2],
                inp["conv_b"][l]]
        for kd, v in enumerate(vecs):
            cvec[:, l * 80 + kd * 8:l * 80 + kd * 8 + 8] = np.asarray(v, f32).reshape(8, 128).T
    shared["cvec"] = cvec
    wsp = np.asarray(inp["w_spatial"], f32)
    shared["wsT_in"] = np.ascontiguousarray(wsp.transpose(0, 3, 1, 2))
    bsp = np.asarray(inp["b_spatial"], f32)
    shared["bs_bc"] = np.ascontiguousarray(np.broadcast_to(bsp[:, None], (2, 128, 8, 128)))
    shared["identf_in"] = np.eye(128, dtype=f32)
    shared["tri_in"] = np.triu(np.ones((128, 128), f32))
    rot = np.zeros((128, 128), f32)
    for m in range(128):
        if m % 64 < 32:
            rot[m + 32, m] = -1.0
        else:
            rot[m - 32, m] = 1.0
    shared["rotm_in"] = rot.astype(ml_dtypes.bfloat16)
    sinks = np.asarray(inp["attn_sinks"], f32)
    shared["sinks_bc"] = np.ascontiguousarray(np.broadcast_to(sinks.reshape(1, 32), (128, 32)))
    shared["ws00_bc"] = np.ascontiguousarray(np.broadcast_to(wsp[:, :, 0, 0].reshape(1, 16), (16, 16)))
    idm = np.zeros((16, 4, 4, 16), f32)
    for s in range(16):
        idm[s, :, :, s] = 1.0
    shared["idm16_in"] = idm.reshape(16, 256)
    cm = np.zeros((128, 16, 4, 16), f32)
    for s in range(16):
        cm[:, s, :, s] = 1.0
    shared["cm_in"] = cm.reshape(128, 1024)
    kk = np.arange(128)[:, None]
    qq = np.arange(128)[None, :]
    m_cur = np.where(kk <= qq, 1.0, 0.0).astype(f32)
    m_prev = np.where(kk >= qq, 1.0, 0.0).astype(f32)
    m_none = np.zeros((128, 128), f32)
    inv = np.power(np.float32(10000.0), -np.arange(32, dtype=f32) * np.float32(2.0 / 64)).astype(f32)
    invp = inv[(np.arange(128) % 64) % 32]

    xpad = np.concatenate([np.zeros((256, D), f32), x_prompt], axis=0)
    per_core = []
    for core in range(NCORE):
        d = {}
        xp = xpad[core * 2048:core * 2048 + 2304]
        xsmp = x_sample[core * NSMP:(core + 1) * NSMP]
        xa = np.concatenate([xp, xsmp], axis=0)
        d["xT"] = np.ascontiguousarray(xa.reshape(NT, 8, 128).transpose(2, 1, 0))
        pos = np.concatenate([np.arange(core * 2048 - 256, core * 2048 + 2048), np.full(NSMP, PAST)]).astype(np.int32)
        ang = pos.astype(f32)[None, :] * invp[:, None]
        d["cosT"] = np.cos(ang).astype(f32)
        d["sinT"] = np.sin(ang).astype(f32)
        mp0 = m_none if core == 0 else m_prev
        def rep4(m):
            return np.broadcast_to(m[:, None, :], (128, 4, 128)).reshape(128, 512)
        d["maskb_in"] = np.ascontiguousarray(np.stack([np.concatenate([rep4(m_prev), rep4(m_cur)], axis=1),
                                                       np.concatenate([rep4(mp0), rep4(m_cur)], axis=1)], axis=1))
        sl = slice(core * NSMP, (core + 1) * NSMP)
        sc = state_conv[:, sl]
        d["sc_nat"] = np.ascontiguousarray(sc)
        d["histT"] = np.ascontiguousarray(sc.reshape(2, NSMP, 2, 8, 128).transpose(0, 4, 2, 3, 1))
        ckc = ck[:, sl]
        d["ck_nat"] = np.ascontiguousarray(ckc.reshape(2, NSMP, 128, 256))
        d["cv_nat"] = np.ascontiguousarray(cvv[:, sl].reshape(2, NSMP, 128, 256))
        t = ckc.reshape(2, NSMP, 128, 2, 2, 64)
        d["kcT_in"] = np.ascontiguousarray(t.transpose(0, 4, 5, 1, 3, 2).reshape(2, 128, NSMP, 2, 128))
        d.update(shared)
        per_core.append(d)
    return per_core


def kernel(**inputs):
    if "nc" not in _CACHE:
        _CACHE["nc"] = build_program()
    nc, _ = _CACHE["nc"]
    in_maps = prep(inputs)
    res = run_bass_kernel_spmd(nc, in_maps, core_ids=list(range(NCORE)))
    R = res.results
    f32 = np.float32
    y_prompt = np.zeros((1, 16384, D), f32)
    y_sample = np.zeros((128, 1, D), f32)
    for core in range(NCORE):
        yT = R[core]["yT"]
        ya = yT.transpose(2, 1, 0).reshape(NYT, D)
        y_prompt[0, core * 2048:(core + 1) * 2048] = ya[:2048]
        y_sample[core * NSMP:(core + 1) * NSMP, 0] = ya[2048:]
    last = R[NCORE - 1]
    prompt_conv = np.zeros((2, 1, 2, D), f32)
    prompt_k = np.zeros((2, 1, 128, 4, 64), f32)
    prompt_v = np.zeros((2, 1, 128, 4, 64), f32)
    for l in range(2):
        prompt_conv[l, 0] = last["pconvT"][l].transpose(2, 1, 0).reshape(2, D)
        t = last["pkT"][l].reshape(2, 64, 2, 128)
        prompt_k[l, 0] = t.transpose(3, 2, 0, 1).reshape(128, 4, 64)
        prompt_v[l, 0] = last["pv"][l].reshape(128, 4, 64)
    sample_conv = np.zeros((2, 128, 2, D), f32)
    sample_k = np.zeros((2, 128, 128, 4, 64), f32)
    sample_v = np.zeros((2, 128, 128, 4, 64), f32)
    sample_cv = np.zeros((2, 128, 1, D), f32)
    for core in range(NCORE):
        r = R[core]
        sl = slice(core * NSMP, (core + 1) * NSMP)
        for l in range(2):
            sample_conv[l, sl, 0] = r["sconv_old"][l]
            sample_conv[l, sl, 1] = r["sconv_newT"][l].transpose(2, 1, 0).reshape(NSMP, D)
            sample_k[l, sl, 0:127] = r["sk_shift"][l].reshape(NSMP, 127, 4, 64)
            t = r["sk_newT"][l].reshape(2, 64, 2, NSMP)
            sample_k[l, sl, 127] = t.transpose(3, 2, 0, 1).reshape(NSMP, 4, 64)
            sample_v[l, sl, 0:127] = r["sv_shift"][l].reshape(NSMP, 127, 4, 64)
            sample_v[l, sl, 127] = r["sv_new"][l].reshape(NSMP, 4, 64)
            sample_cv[l, sl, 0] = r["scvT"][l].transpose(2, 1, 0).reshape(NSMP, D)
    return (y_prompt, y_sample, prompt_conv, prompt_k, prompt_v, sample_conv, sample_k, sample_v, sample_cv)
```

```python
import numpy as np
import ml_dtypes
from contextlib import ExitStack
import concourse.bass as bass
import concourse.mybir as mybir
from concourse.bass_utils import run_bass_kernel_spmd

F32 = mybir.dt.float32
BF16 = mybir.dt.bfloat16
AF = mybir.ActivationFunctionType
ALU = mybir.AluOpType

NCORE = 8
D = 1024
TP = 768
NGRP = 3
NSMP = 16
TMAX = TP + NSMP
NT = NGRP * TP + NSMP
NYT = 2048 + NSMP
NSLAB = 46
NB = 3
ATT_DEPTH = 2
PAST = 16384
EPS_RMS = 1e-6
EPS_LN = 1e-5
MASKNEG = -30000.0

K_PREMIX, K_POSTMIX, K_PREFFN, K_POSTFFN, K_LNG, K_LNB, K_CW0, K_CW1, K_CW2, K_CB = range(10)

QUEUES = ("pe", "act", "dve", "pool", "sp")
NDMASEM = 24
EPOCH = 400
NEPOCH = 8
KLIMIT = 9999
KGROUPS = ""
KSKIP = ""
KDBG = False


_CACHE = {}


class _Stop(Exception):
    pass


class Plan:
    def __init__(self):
        self.ops = {q: [] for q in QUEUES}
        self.state = {}
        self.dma_uses = [0] * NDMASEM
        self.tag = "setup"
        self.dma_rr = {"sp": 0, "pool": 0, "act": 0}
        self.dma_rng = {"sp": (0, 14), "pool": (14, 8), "act": (22, 2)}

    def _st(self, k):
        s = self.state.get(k)
        if s is None:
            s = [None, {}]
            self.state[k] = s
        return s

    def op(self, q, fn, reads=(), writes=(), dma=False, extra_deps=()):
        deps = []
        for k in reads:
            s = self._st(k)
            if s[0] is not None:
                deps.append(s[0])
        for k in writes:
            s = self._st(k)
            if s[0] is not None:
                deps.append(s[0])
            deps.extend(s[1].values())
        deps.extend(extra_deps)
        idx = len(self.ops[q])
        rec = dict(fn=fn, deps=deps, signal=False, dma=None, tag=self.tag)
        if dma:
            base, n = self.dma_rng[q]
            si = base + self.dma_rr[q] % n
            self.dma_rr[q] += 1
            prev = self.dma_uses[si]
            self.dma_uses[si] += 1
            if prev > 0:
                rec["deps"].append(("d", si, 16 * prev))
            rec["dma"] = si
            tok = ("d", si, 16 * (prev + 1))
        else:
            tok = ("e", q, idx)
        self.ops[q].append(rec)
        for k in reads:
            s = self._st(k)
            tk = tok[:2]
            old = s[1].get(tk)
            if old is None or old[2] < tok[2]:
                s[1][tk] = tok
        for k in writes:
            s = self._st(k)
            s[0] = tok
            s[1] = {}
        return tok

    def emit(self, block, sems, dsems):
        for q in QUEUES:
            for rec in self.ops[q]:
                for d in rec["deps"]:
                    if d[0] == "e":
                        if d[1] == q and q in ("pe", "sp"):
                            continue
                        self.ops[d[1]][d[2]]["signal"] = True
        cnt = {}
        for q in QUEUES:
            c = 0
            arr = []
            for rec in self.ops[q]:
                if rec["signal"]:
                    c += 1
                arr.append(c)
            cnt[q] = arr
        engs = dict(pe=block.tensor, act=block.scalar, dve=block.vector, pool=block.gpsimd, sp=block.sync)
        stats = {}
        for q in QUEUES:
            ops = self.ops[q]
            if not ops:
                continue

            def body(eng, q=q, ops=ops):
                waited = {}
                nw = 0
                mycnt = [0]
                for rec in ops:
                    need = {}
                    for d in rec["deps"]:
                        if d[0] == "e":
                            if d[1] == q and q in ("pe", "sp"):
                                continue
                            key = ("e", d[1])
                            val = cnt[d[1]][d[2]]
                        else:
                            key = ("d", d[1])
                            val = d[2]
                        if waited.get(key, 0) < val and need.get(key, 0) < val:
                            need[key] = val
                    for key, val in need.items():
                        if key[0] == "e":
                            eng.wait_ge(sems[key[1]][(val - 1) // EPOCH], (val - 1) % EPOCH + 1)
                        else:
                            eng.wait_ge(dsems[key[1]], val)
                        waited[key] = val
                        nw += 1
                    ins = rec["fn"](eng)
                    if rec["dma"] is not None:
                        ins.then_inc(dsems[rec["dma"]], 16)
                    elif rec["signal"]:
                        mycnt[0] += 1
                        ins.then_inc(sems[q][(mycnt[0] - 1) // EPOCH], 1)
                stats[q] = (len(ops), nw)

            engs[q](body)
        return stats


class Rot:
    def __init__(self, name, aps):
        self.name, self.aps, self.i = name, aps, 0

    def get(self):
        k = self.i % len(self.aps)
        self.i += 1
        return self.aps[k], (self.name, k)


def unit_sequence():
    seq = []
    seq += [("va", c) for c in range(8)]
    seq += [("ua", c) for c in range(8)]
    for c in range(8):
        seq += [("cg", c), ("hb", c), ("bg", c)]
    for c in range(8):
        seq += [("wa", c), ("ga", c)]
    seq += [("q", c) for c in range(8)]
    seq += [("k", 0), ("k", 1), ("v", 0), ("v", 1)]
    for c in range(8):
        seq += [("wb", c), ("gb", c)]
    for c in range(8):
        seq += [("wc", c), ("gc", c)]
    seq += [("wo", c) for c in range(8)]
    for j in range(22):
        seq += [("fg", j), ("fu", j)]
    return seq


def head_of(c, e):
    G2, r = c // 4, c % 4
    return 4 * (2 * G2 + e) + r


def build_program():
    nc = bass.Bass("TRN2", target_bir_lowering=False)

    def din(name, shape, dt=F32):
        return nc.dram_tensor(name, list(shape), dt, kind="ExternalInput").ap()

    def dout(name, shape, dt=F32):
        return nc.dram_tensor(name, list(shape), dt, kind="ExternalOutput").ap()

    xT = din("xT", [128, 8, NT])
    cosT = din("cosT", [128, NT])
    sinT = din("sinT", [128, NT])
    wst = din("wst", [2, NSLAB, 128, 4096])
    cvec = din("cvec", [128, 160])
    wsT_in = din("wsT_in", [2, 128, 8, 128])
    bs_bc = din("bs_bc", [2, 128, 8, 128])
    maskb_in = din("maskb_in", [128, 2, 1024])
    identf_in = din("identf_in", [128, 128])
    tri_in = din("tri_in", [128, 128])
    rotm_in = din("rotm_in", [128, 128], BF16)
    sinks_bc = din("sinks_bc", [128, 32])
    ws00_bc = din("ws00_bc", [16, 16])
    idm16_in = din("idm16_in", [16, 256])
    cm_in = din("cm_in", [128, 1024])
    histT = din("histT", [2, 128, 2, 8, NSMP])
    kcT_in = din("kcT_in", [2, 128, NSMP, 2, 128])
    ck_nat = din("ck_nat", [2, NSMP, 128, 256])
    cv_nat = din("cv_nat", [2, NSMP, 128, 256])
    sc_nat = din("sc_nat", [2, NSMP, 2, 1024])

    yT = dout("yT", [128, 8, NYT])
    pconvT = dout("pconvT", [2, 128, 8, 2])
    pkT = dout("pkT", [2, 128, 2, 128])
    pv = dout("pv", [2, 128, 256])
    sconv_newT = dout("sconv_newT", [2, 128, 8, NSMP])
    sconv_old = dout("sconv_old", [2, NSMP, 1024])
    sk_shift = dout("sk_shift", [2, NSMP, 127, 256])
    sk_newT = dout("sk_newT", [2, 128, 2, NSMP])
    sv_shift = dout("sv_shift", [2, NSMP, 127, 256])
    sv_new = dout("sv_new", [2, NSMP, 256])
    scvT = dout("scvT", [2, 128, 8, NSMP])
    if KDBG:
        dbgf = dout("dbgf", [3, 128, 8, TMAX])
        dbgb = dout("dbgb", [2, 128, 8, TMAX], BF16)

    P = Plan()
    out_toks = []

    with ExitStack() as es:
        def sb(name, shape, dt):
            return es.enter_context(nc.sbuf_tensor(name, list(shape), dt))

        xs = sb("xs", [128, 8, TMAX], F32)
        hb = sb("hb", [128, 8, TMAX], BF16)
        arena = sb("arena", [128, 22 * TMAX], BF16)
        merged = sb("merged", [128, 8, TMAX], F32)
        kcur = sb("kcur", [128, 2, TMAX], BF16)
        kprev = sb("kprev", [128, 2, 2, 128], BF16)
        vcur = sb("vcur", [128, 7, 256], BF16)
        vprev = sb("vprev", [128, 2, 256], BF16)
        kTf = sb("kTf", [128, 2, 144], F32)
        cxe_t = sb("cxe", [128, 2, 2 + TMAX], F32)
        cxh = sb("cxh", [128, 2, 8, 2], F32)
        scx = sb("scx", [128, 8, NSMP], F32)
        scv_sb = sb("scv_sb", [128, 8, NSMP], F32)
        cos_sb = sb("cos_sb", [128, TMAX], F32)
        sin_sb = sb("sin_sb", [128, TMAX], F32)
        rstd = sb("rstd", [128, TMAX], F32)
        sq_t = sb("sq_t", [128, 3, TMAX], BF16)
        tmpf_t = sb("tmpf_t", [128, 3, 1024], F32)
        pT_t = sb("pT_t", [128, ATT_DEPTH + 1, 1024], BF16)
        ring = sb("ring", [128, NB, 4096], BF16)
        cv_sb = sb("cv_sb", [128, 160], F32)
        wsTb = sb("wsTb", [128, 2, 8, 128], BF16)
        bias2 = sb("bias2", [128, 2, 8, 128], F32)
        maskb = sb("maskb", [128, 2, 1024], BF16)
        identf = sb("identf", [128, 128], F32)
        identb = sb("identb", [128, 128], BF16)
        tri = sb("tri", [128, 128], F32)
        ones = sb("ones", [128, 128], BF16)
        rotm = sb("rotm", [128, 128], BF16)
        sinkexp = sb("sinkexp", [128, 32], F32)
        ws00 = sb("ws00", [16, 16], F32)
        wsd = sb("wsd", [16, 2, 8, 16], BF16)
        idm16 = sb("idm16", [16, 256], F32)
        cst = sb("cst", [128, 4], F32)
        hist_sb = sb("hist_sb", [128, 2, 8, NSMP], F32)
        kcT = sb("kcT", [128, NSMP, 2, 128], BF16)
        vc = sb("vc", [128, NSMP, 256], BF16)
        lnst = sb("lnst", [128, 7, 2, 6], F32)
        lnmv = sb("lnmv", [128, 7, 2], F32)
        lnr = sb("lnr", [128, 7], F32)
        pd_sb = sb("pd_sb", [16, 256], BF16)
        cmb = sb("cmb", [128, 1024], BF16)

        ps = es.enter_context(nc.psum_tensor("ps", [128, 4096], F32))
        sems = {q: [es.enter_context(nc.semaphore("s_%s_%d" % (q, i))) for i in range(NEPOCH)] for q in QUEUES}
        dsems = [es.enter_context(nc.semaphore("d%d" % i)) for i in range(NDMASEM)]
        block = es.enter_context(nc.Block())

        buf1 = arena[:, 0:8 * TMAX].rearrange("p (c t) -> p c t", c=8)
        B2OFF = 8 * TMAX
        buf2 = arena[:, B2OFF:B2OFF + 8 * TMAX].rearrange("p (c t) -> p c t", c=8)
        vatm = arena[:, B2OFF:B2OFF + 7 * 1024].rearrange("p (i f) -> p i f", i=7)
        actb = arena[:, :].rearrange("p (c t) -> p c t", c=22)

        def arkeys(start, end):
            return [("ar", k) for k in range(start // TMAX, (end - 1) // TMAX + 1)]

        def vatm_keys(i):
            return arkeys(B2OFF + i * 1024, B2OFF + (i + 1) * 1024)

        tmpf = Rot("tmpf", [tmpf_t[:, i, :] for i in range(3)])
        sqb = Rot("sq", [sq_t[:, i, :] for i in range(3)])
        qbs = sqb
        pTs = Rot("pT", [pT_t[:, i, :] for i in range(ATT_DEPTH + 1)])
        cxes = Rot("cxe", [cxe_t[:, i, :] for i in range(2)])

        def psf(S):
            return ps[:, S * 1024:(S + 1) * 1024]

        CS = [0]
        slot_state = dict(rr=0, pinned=set())

        def alloc(pin=False):
            while True:
                s = slot_state["rr"] % 4
                slot_state["rr"] += 1
                if s not in slot_state["pinned"]:
                    break
            if pin:
                slot_state["pinned"].add(s)
            return s

        def unpin(s):
            slot_state["pinned"].discard(s)

        def cv(l, kind, c):
            i = l * 80 + kind * 8 + c
            return cv_sb[:, i:i + 1]

        def MM(out, lhsT, rhs, start, stop, reads, writes):
            P.op("pe", lambda e: e.matmul(out, lhsT=lhsT, rhs=rhs, start=start, stop=stop), reads, writes)

        def ACT(out, in_, func, reads, writes, scale=1.0, bias=None):
            if bias is None:
                P.op("act", lambda e: e.activation(out=out, in_=in_, func=func, scale=scale), reads, writes)
            else:
                P.op("act", lambda e: e.activation(out=out, in_=in_, func=func, scale=scale, bias=bias), reads, writes)

        def TT(out, in0, in1, op, reads, writes, q="dve"):
            if op == ALU.add and "ttadd" not in KSKIP:
                P.op(q, lambda e: e.scalar_tensor_tensor(out=out, in0=in0, scalar=1.0, in1=in1, op0=ALU.mult, op1=ALU.add), reads, writes)
            else:
                P.op(q, lambda e: e.tensor_tensor(out=out, in0=in0, in1=in1, op=op), reads, writes)

        def STT(out, in0, scalar, in1, op0, op1, reads, writes):
            P.op("dve", lambda e: e.scalar_tensor_tensor(out=out, in0=in0, scalar=scalar, in1=in1, op0=op0, op1=op1), reads, writes)

        def TS(out, in0, s1, s2, op0, op1, reads, writes, q="dve"):
            if s2 is None:
                P.op(q, lambda e: e.tensor_scalar(out=out, in0=in0, scalar1=s1, scalar2=None, op0=op0), reads, writes)
            else:
                P.op(q, lambda e: e.tensor_scalar(out=out, in0=in0, scalar1=s1, scalar2=s2, op0=op0, op1=op1), reads, writes)

        def CP(out, in_, reads, writes, q="dve"):
            if q == "act":
                P.op(q, lambda e: e.activation(out=out, in_=in_, func=AF.Identity), reads, writes)
            else:
                P.op(q, lambda e: e.tensor_copy(out=out, in_=in_), reads, writes)

        def DMA(out, in_, reads, writes, q="sp", is_out=False, **kw):
            t = P.op(q, lambda e: e.dma_start(out=out, in_=in_, **kw), reads, writes, dma=True)
            if is_out:
                out_toks.append(t)
            return t

        useq = unit_sequence()
        total_slabs = NGRP * 2 * NSLAB
        ws = dict(next_plan=0, oldest=0, upos=0, hold=False)

        def slab_coords(n):
            gl, s = divmod(n, NSLAB)
            return gl % 2, s

        def w_ensure():
            while ws["next_plan"] < min(ws["oldest"] + NB, total_slabs):
                n = ws["next_plan"]
                l, s = slab_coords(n)
                k = n % NB
                ncol = 4096 if s < 38 else 2816
                DMA(ring[:, k, 0:ncol], wst[l, s, :, 0:ncol], [], [("ring", k)], q="pool", max_dma_last_dim=8192)
                ws["next_plan"] += 1

        def w_unit(name, idx, glayer):
            u = ws["upos"]
            assert useq[u] == (name, idx), (useq[u], name, idx)
            ws["upos"] += 1
            n = glayer * NSLAB + u // 4
            if not ws["hold"] and n > ws["oldest"]:
                ws["oldest"] = n
            w_ensure()
            return n % NB, u % 4, ("ring", n % NB)

        def w_down(c, glayer):
            n = glayer * NSLAB + 38 + c
            if n > ws["oldest"]:
                ws["oldest"] = n
            w_ensure()
            return n % NB, ("ring", n % NB)

        def lhs_unit(uinfo, kc):
            k, j, _ = uinfo
            o = j * 1024 + kc * 128
            return ring[:, k, o:o + 128]

        def chunk_mm(S, uinfo, rhs_fn, rkey_fn, T):
            wkey = uinfo[2]
            for kc in range(8):
                lhsT = lhs_unit(uinfo, kc)
                r = rhs_fn(kc)
                for (c0, c1) in ((CS[0], 512), (512, T)):
                    MM(psf(S)[:, c0:c1], lhsT, r[:, c0:c1], kc == 0, kc == 7, [wkey] + rkey_fn(kc), [("ps", S)])

        DMA(cv_sb[:], cvec, [], ["cv"])
        DMA(identf[:], identf_in, [], ["identf"])
        DMA(tri[:], tri_in, [], ["tri"])
        DMA(rotm[:], rotm_in, [], ["rotm"])
        DMA(sinkexp[:], sinks_bc, [], ["sinkexp"])
        DMA(ws00[:], ws00_bc, [], ["ws00"])
        DMA(idm16[:], idm16_in, [], ["idm16"])
        DMA(bias2[:, 0], bs_bc[0], [], [("bias2", 0)])
        DMA(bias2[:, 1], bs_bc[1], [], [("bias2", 1)])
        DMA(maskb[:], maskb_in, [], ["maskb"], q="pool", max_dma_last_dim=8192)
        DMA(cmb[:], cm_in, [], ["cmb"], q="pool", max_dma_last_dim=8192)
        P.op("dve", lambda e: e.memset(ones[:], 1.0), [], ["ones"])
        P.op("dve", lambda e: e.memset(kprev[:], 0.0), [], [("kprev", 0), ("kprev", 1)])
        P.op("dve", lambda e: e.memset(vprev[:], 0.0), [], [("vprev", 0), ("vprev", 1)])
        P.op("dve", lambda e: e.memset(cxh[:], 0.0), [], [("cxh", 0), ("cxh", 1)])
        P.op("dve", lambda e: e.memset(cst[:, 0:1], EPS_RMS), [], ["cst"])
        P.op("dve", lambda e: e.memset(cst[:, 1:2], EPS_LN), [], ["cst"])
        CP(identb[:], identf[:], ["identf"], ["identb"])
        ACT(sinkexp[:], sinkexp[:], AF.Exp, ["sinkexp"], ["sinkexp"])
        for l in range(2):
            tw, twk = tmpf.get()
            twv = tw.rearrange("p (g t) -> p g t", g=8)
            DMA(twv, wsT_in[l], [], [twk])
            TT(wsTb[:, l], twv, tri[:].unsqueeze(1).broadcast_to([128, 8, 128]), ALU.mult, [twk, "tri"], [("wsTb", l)])
            S = alloc()
            for g in range(8):
                MM(psf(S)[:, g * 128:(g + 1) * 128], ones[:], wsTb[:, l, g, :], True, True, ["ones", ("wsTb", l)], [("ps", S)])
            for g in range(8):
                STT(bias2[:, l, g, :], psf(S)[:, g * 128:(g + 1) * 128], cv(l, K_LNB, g), bias2[:, l, g, :], ALU.mult, ALU.add,
                    [("ps", S), "cv", ("bias2", l)], [("bias2", l)])
            for g in range(8):
                TS(wsd[:, l, g, :], identf[0:16, 0:16], ws00[:, l * 8 + g:l * 8 + g + 1], None, ALU.mult, None, ["identf", "ws00"], ["wsd"])

        def stats_finish(SN, T):
            ACT(rstd[:, CS[0]:T], psf(SN)[:, CS[0]:T], AF.Ln, [("ps", SN), "cst"], ["rstd"], scale=1.0 / D, bias=cst[:, 0:1])
            ACT(rstd[:, CS[0]:T], rstd[:, CS[0]:T], AF.Exp, ["rstd"], ["rstd"], scale=-0.5)
            unpin(SN)

        def stats_sq(src, skeys, T):
            sq, sk = sqb.get()
            ACT(sq[:, CS[0]:T], src, AF.Square, skeys, [sk])
            return sq, sk

        def stats_mm(SN, sq, sk, c, T):
            for (c0, c1) in ((CS[0], 512), (512, T)):
                MM(psf(SN)[:, c0:c1], ones[:], sq[:, c0:c1], c == 0, c == 7, [sk, "ones"], [("ps", SN)])

        def stats_add(SN, src, skeys, c, T):
            sq, sk = stats_sq(src, skeys, T)
            stats_mm(SN, sq, sk, c, T)

        def norm_to_hb(l, kind, T):
            SN = alloc(pin=True)
            for c in range(8):
                stats_add(SN, xs[:, c, CS[0]:T], [("xs", c)], c, T)
            stats_finish(SN, T)
            for c in range(8):
                STT(hb[:, c, CS[0]:T], xs[:, c, CS[0]:T], cv(l, kind, c), rstd[:, CS[0]:T], ALU.mult, ALU.mult,
                    [("xs", c), "cv", "rstd"], [("hb", c)])

        def post_apply(l, kind, T):
            for c in range(8):
                t, tk = tmpf.get()
                STT(t[:, CS[0]:T], merged[:, c, CS[0]:T], cv(l, kind, c), rstd[:, CS[0]:T], ALU.mult, ALU.mult,
                    [("mg", c), "cv", "rstd"], [tk])
                TT(xs[:, c, CS[0]:T], xs[:, c, CS[0]:T], t[:, CS[0]:T], ALU.add, [("xs", c), tk], [("xs", c)])

        hbr = lambda kc: hb[:, kc, :]
        hbk = lambda kc: [("hb", kc)]

        def gated_branch(l, glayer, T, wname, gname, src, srckeys, mode):
            for c in range(8):
                uw = w_unit(wname, c, glayer)
                S = alloc()
                chunk_mm(S, uw, lambda kc: src[:, kc, :], lambda kc: srckeys(kc), T)
                ug = w_unit(gname, c, glayer)
                S2 = alloc()
                chunk_mm(S2, ug, hbr, hbk, T)
                tg, tgk = tmpf.get()
                ACT(tg[:, CS[0]:T], psf(S2)[:, CS[0]:T], AF.Sigmoid, [("ps", S2)], [tgk])
                if mode == 0:
                    TT(merged[:, c, CS[0]:T], psf(S)[:, CS[0]:T], tg[:, CS[0]:T], ALU.mult, [("ps", S), tgk], [("mg", c)])
                else:
                    TT(tg[:, CS[0]:T], psf(S)[:, CS[0]:T], tg[:, CS[0]:T], ALU.mult, [("ps", S), tgk], [tgk])
                    if mode == 1:
                        TT(merged[:, c, CS[0]:T], merged[:, c, CS[0]:T], tg[:, CS[0]:T], ALU.add, [("mg", c), tgk], [("mg", c)])
                    else:
                        TT(buf1[:, c, CS[0]:T], merged[:, c, CS[0]:T], tg[:, CS[0]:T], ALU.add, [("mg", c), tgk], [("ar", c)])

        phase = [0]

        def chk():
            phase[0] += 1
            if phase[0] > KLIMIT:
                raise _Stop()

        def main_loops():
          for gi in range(NGRP):
            if KGROUPS and str(gi) not in KGROUPS:
                continue
            if KGROUPS and ws["next_plan"] < gi * 2 * NSLAB:
                ws["next_plan"] = ws["oldest"] = gi * 2 * NSLAB
            if True:
                T = TP + (NSMP if gi == NGRP - 1 else 0)
                last = gi == NGRP - 1
                ntile = 7 if last else 6
                col0 = gi * TP
                for c in range(8):
                    DMA(xs[:, c, :T], xT[:, c, col0:col0 + T], [], [("xs", c)])
                DMA(cos_sb[:, :T], cosT[:, col0:col0 + T], [], ["cos"])
                DMA(sin_sb[:, :T], sinT[:, col0:col0 + T], [], ["sin"])

                for l in range(2):
                    glayer = gi * 2 + l
                    ws["upos"] = 0
                    ck = (0, 128)[l] if gi == 0 else 0
                    cs = (128, 256)[l] if gi == 0 else 0
                    i0 = cs // 128
                    i0k = ck // 128
                    if last:
                        DMA(hist_sb[:], histT[l], [], ["hist"])
                        DMA(kcT[:], kcT_in[l], [], ["kcT"], q="pool", max_dma_last_dim=8192)
                        DMA(vc[:], cv_nat[l].rearrange("s k f -> k s f"), [], ["vc"], q="pool", max_dma_last_dim=8192)
                        if gi == NGRP - 1:
                            DMA(sk_shift[l], ck_nat[l, :, 1:128, :], [], [], is_out=True)
                            DMA(sv_shift[l], cv_nat[l, :, 1:128, :], [], [], is_out=True)
                            DMA(sconv_old[l], sc_nat[l, :, 1, :], [], [], is_out=True)

                    chk()
                    P.tag = "P1"
                    CS[0] = ck
                    norm_to_hb(l, K_PREMIX, T)
                    CS[0] = cs

                    chk()
                    P.tag = "P2"
                    ws["hold"] = True
                    uva = [w_unit("va", c, glayer) for c in range(8)]
                    for i in range(i0, ntile):
                        M = 128 if i < 6 else NSMP
                        tc0 = i * 128
                        S = alloc()
                        for h2 in range(2):
                            u0 = uva[h2 * 4]
                            slab = ring[:, u0[0], :].rearrange("p (u k c) -> p u k c", u=4, k=8)
                            for kc in range(8):
                                MM(psf(S)[0:M, h2 * 512:(h2 + 1) * 512], hb[:, kc, tc0:tc0 + M], slab[:, :, kc, :], kc == 0, kc == 7,
                                   [u0[2], ("hb", kc)], [("ps", S)])
                        tg, tgk = tmpf.get()
                        ACT(tg[0:M, :], psf(S)[0:M, :], AF.Gelu_apprx_tanh, [("ps", S)], [tgk])
                        for h2 in range(2):
                            P.op("dve", lambda e, tg=tg, M=M, i=i, h2=h2: e.bn_stats(out=lnst[0:M, i, h2, :], in_=tg[0:M, h2 * 512:(h2 + 1) * 512]),
                                 [tgk], [("lnst", i)])
                        P.op("dve", lambda e, M=M, i=i: e.bn_aggr(out=lnmv[0:M, i, :], in_=lnst[0:M, i, :, :].rearrange("p a b -> p (a b)")),
                             [("lnst", i)], [("lnmv", i)])
                        CP(vatm[0:M, i, :], tg[0:M, :], [tgk], vatm_keys(i))
                        if i == 6:
                            tlast, tlastk = tg, tgk
                    ws["hold"] = False
                    nl = ntile
                    ACT(lnr[:, i0:nl], lnmv[:, i0:nl, 1], AF.Ln, [("lnmv", i) for i in range(i0, nl)] + ["cst"], ["lnr"], bias=cst[:, 1:2])
                    ACT(lnr[:, i0:nl], lnr[:, i0:nl], AF.Exp, ["lnr"], ["lnr"], scale=-0.5)
                    for i in range(i0, ntile):
                        M = 128 if i < 6 else NSMP
                        TS(vatm[0:M, i, :], vatm[0:M, i, :], lnmv[0:M, i, 0:1], lnr[0:M, i:i + 1], ALU.subtract, ALU.mult,
                           vatm_keys(i) + [("lnmv", i), "lnr"], vatm_keys(i))
                    for c in range(8):
                        u = w_unit("ua", c, glayer)
                        S = alloc()
                        chunk_mm(S, u, hbr, hbk, T)
                        ACT(buf1[:, c, CS[0]:T], psf(S)[:, CS[0]:T], AF.Gelu_apprx_tanh, [("ps", S)], [("ar", c)])
                    if last:
                        TS(tlast[0:NSMP, :], tlast[0:NSMP, :], lnmv[0:NSMP, 6, 0:1], lnr[0:NSMP, 6:7], ALU.subtract, ALU.mult,
                           [tlastk, ("lnmv", 6), "lnr"], [tlastk])
                        S = alloc()
                        for c in range(8):
                            P.op("pe", lambda e, S=S, c=c, tlast=tlast: e.transpose(psf(S)[:, c * NSMP:(c + 1) * NSMP], tlast[0:NSMP, c * 128:(c + 1) * 128], identf[0:NSMP, 0:NSMP]),
                                 [tlastk, "identf"], [("ps", S)])
                        for c in range(8):
                            ACT(scv_sb[:, c, :], psf(S)[:, c * NSMP:(c + 1) * NSMP], AF.Identity, [("ps", S), "cv"], ["scv"],
                                scale=cv(l, K_LNG, c), bias=cv(l, K_LNB, c))
                        DMA(scvT[l], scv_sb[:], ["scv"], [], is_out=True)

                    chk()
                    P.tag = "spatial"
                    for g in range(8):
                        S = alloc()
                        for i in range(i0, 6):
                            MM(psf(S)[:, i * 128:(i + 1) * 128], vatm[:, i, g * 128:(g + 1) * 128], wsTb[:, l, g, :], True, True,
                               vatm_keys(i) + [("wsTb", l)], [("ps", S)])
                        if last:
                            MM(psf(S)[:, TP:T], vatm[0:NSMP, 6, g * 128:(g + 1) * 128], wsd[:, l, g, :], True, True,
                               vatm_keys(6) + ["wsd"], [("ps", S)])
                        t, tk = tmpf.get()
                        STT(t[:, cs:TP].rearrange("p (i t) -> p i t", i=6 - i0), psf(S)[:, cs:TP].rearrange("p (i t) -> p i t", i=6 - i0), cv(l, K_LNG, g),
                            bias2[:, l, g, :].unsqueeze(1).broadcast_to([128, 6 - i0, 128]), ALU.mult, ALU.add,
                            [("ps", S), "cv", ("bias2", l)], [tk])
                        if last:
                            STT(t[:, TP:T], psf(S)[:, TP:T], cv(l, K_LNG, g), bias2[:, l, g, 0:1].broadcast_to([128, NSMP]), ALU.mult, ALU.add,
                                [("ps", S), "cv", ("bias2", l)], [tk])
                        TT(buf1[:, g, CS[0]:T], buf1[:, g, CS[0]:T], t[:, CS[0]:T], ALU.mult, [("ar", g), tk], [("ar", g)])

                    chk()
                    P.tag = "P3"
                    for c in range(8):
                        CS[0] = ck
                        uc = w_unit("cg", c, glayer)
                        Sc = alloc()
                        chunk_mm(Sc, uc, hbr, hbk, T)
                        uh = w_unit("hb", c, glayer)
                        Sh = alloc()
                        chunk_mm(Sh, uh, hbr, hbk, T)
                        CS[0] = cs
                        ub = w_unit("bg", c, glayer)
                        Sb = alloc()
                        chunk_mm(Sb, ub, hbr, hbk, T)
                        CS[0] = ck
                        th, thk = tmpf.get()
                        ACT(th[:, CS[0]:T], psf(Sh)[:, CS[0]:T], AF.Identity, [("ps", Sh)], [thk])
                        cxe, cxk = cxes.get()
                        CP(cxe[:, 0:2], cxh[:, l, c, :], [("cxh", l)], [cxk], q="act")
                        TT(cxe[:, 2 + ck:2 + T], psf(Sc)[:, CS[0]:T], th[:, CS[0]:T], ALU.mult, [("ps", Sc), thk], [cxk])
                        CS[0] = cs
                        ACT(th[:, cs:TP], cxe[:, 2 + cs:2 + TP], AF.Identity, [cxk, "cv"], [thk], scale=cv(l, K_CW2, c), bias=cv(l, K_CB, c))
                        STT(th[:, cs:TP], cxe[:, 1 + cs:1 + TP], cv(l, K_CW1, c), th[:, cs:TP], ALU.mult, ALU.add, [cxk, "cv", thk], [thk])
                        STT(th[:, cs:TP], cxe[:, cs:TP], cv(l, K_CW0, c), th[:, cs:TP], ALU.mult, ALU.add, [cxk, "cv", thk], [thk])
                        TT(buf2[:, c, cs:TP], psf(Sb)[:, cs:TP], th[:, cs:TP], ALU.mult, [("ps", Sb), thk], [("ar", 8 + c)])
                        CP(cxh[:, l, c, :], cxe[:, TP:TP + 2], [cxk], [("cxh", l)], q="act")
                        if last:
                            sc_ = slice(TP, T)
                            ACT(th[:, sc_], cxe[:, 2 + TP:2 + T], AF.Identity, [cxk, "cv"], [thk], scale=cv(l, K_CW2, c), bias=cv(l, K_CB, c))
                            STT(th[:, sc_], hist_sb[:, 1, c, :], cv(l, K_CW1, c), th[:, sc_], ALU.mult, ALU.add, ["hist", "cv", thk], [thk])
                            STT(th[:, sc_], hist_sb[:, 0, c, :], cv(l, K_CW0, c), th[:, sc_], ALU.mult, ALU.add, ["hist", "cv", thk], [thk])
                            TT(buf2[:, c, sc_], psf(Sb)[:, sc_], th[:, sc_], ALU.mult, [("ps", Sb), thk], [("ar", 8 + c)])
                            CP(scx[:, c, :], cxe[:, 2 + TP:2 + T], [cxk], ["scx"], q="act")
                    if last:
                        DMA(pconvT[l], cxh[:, l], [("cxh", l)], [], is_out=True)
                        DMA(sconv_newT[l], scx[:], ["scx"], [], is_out=True)

                    chk()
                    P.tag = "P4"
                    gated_branch(l, glayer, T, "wa", "ga", buf1, lambda kc: [("ar", kc)], 0)
                    if KDBG and gi == 0 and l == 0:
                        DMA(dbgf[0], merged[:], [("mg", c) for c in range(8)], [], is_out=True)

                    chk()
                    P.tag = "P5"
                    def rope_tail(c, qb, qbk):
                        isq = c < 8
                        CS[0] = cs if isq else ck
                        S2 = alloc()
                        for (c0, c1) in ((CS[0], 512), (512, T)):
                            MM(psf(S2)[:, c0:c1], rotm[:], qb[:, c0:c1], True, True, ["rotm", qbk], [("ps", S2)])
                        t1, t1k = tmpf.get()
                        t2, t2k = tmpf.get()
                        TT(t1[:, CS[0]:T], qb[:, CS[0]:T], cos_sb[:, CS[0]:T], ALU.mult, [qbk, "cos"], [t1k])
                        TT(t2[:, CS[0]:T], psf(S2)[:, CS[0]:T], sin_sb[:, CS[0]:T], ALU.mult, [("ps", S2), "sin"], [t2k])
                        if isq:
                            TT(buf1[:, c, CS[0]:T], t1[:, CS[0]:T], t2[:, CS[0]:T], ALU.add, [t1k, t2k], [("ar", c)])
                        else:
                            G2 = c - 8
                            TT(kcur[:, G2, CS[0]:T], t1[:, CS[0]:T], t2[:, CS[0]:T], ALU.add, [t1k, t2k], [("kcur", G2)])
                            if last:
                                TT(kTf[:, G2, 0:T - 640], t1[:, 640:T], t2[:, 640:T], ALU.add, [t1k, t2k], ["kTf"])

                    pend = None
                    for c in range(10):
                        isq = c < 8
                        CS[0] = cs if isq else ck
                        u = w_unit("q" if isq else "k", c if isq else c - 8, glayer)
                        S = alloc()
                        chunk_mm(S, u, hbr, hbk, T)
                        qb, qbk = qbs.get()
                        ACT(qb[:, CS[0]:T], psf(S)[:, CS[0]:T], AF.Identity, [("ps", S)], [qbk])
                        if pend is not None:
                            rope_tail(*pend)
                        pend = (c, qb, qbk)
                    rope_tail(*pend)
                    CS[0] = cs
                    if last:
                        DMA(pkT[l], kTf[:, :, 0:128], ["kTf"], [], is_out=True)
                        DMA(sk_newT[l], kTf[:, :, 128:144], ["kTf"], [], is_out=True)
                    uv0 = w_unit("v", 0, glayer)
                    uv1 = w_unit("v", 1, glayer)
                    vslab = ring[:, uv0[0], :].rearrange("p (u k c) -> p u k c", u=4, k=8)
                    Sv = None
                    for i in range(i0k, ntile):
                        M = 128 if i < 6 else NSMP
                        j = (i - i0k) % 4
                        if j == 0:
                            Sv = alloc()
                        for kc in range(8):
                            MM(psf(Sv)[0:M, j * 256:(j + 1) * 256], hb[:, kc, i * 128:i * 128 + M], vslab[:, 2:4, kc, :], kc == 0, kc == 7,
                               [uv0[2], ("hb", kc)], [("ps", Sv)])
                        ACT(vcur[0:M, i, :], psf(Sv)[0:M, j * 256:(j + 1) * 256], AF.Identity, [("ps", Sv)], [("vcur", i)])
                        if last and i == 5:
                            vf, vfk = tmpf.get()
                            ACT(vf[:, 0:256], psf(Sv)[:, j * 256:(j + 1) * 256], AF.Identity, [("ps", Sv)], [vfk])
                            DMA(pv[l], vf[:, 0:256], [vfk], [], is_out=True)
                        if last and i == 6:
                            vf, vfk = tmpf.get()
                            ACT(vf[0:NSMP, 0:256], psf(Sv)[0:NSMP, j * 256:(j + 1) * 256], AF.Identity, [("ps", Sv)], [vfk])
                            DMA(sv_new[l], vf[0:NSMP, 0:256], [vfk], [], is_out=True)

                    chk()
                    P.tag = "P6"
                    gated_branch(l, glayer, T, "wb", "gb", buf2, lambda kc: [("ar", 8 + kc)], 1)
                    if KDBG and gi == 0 and l == 0:
                        DMA(dbgf[1], merged[:], [("mg", c) for c in range(8)], [], is_out=True)

                    chk()
                    P.tag = "P7"
                    qkeys = [("ar", c) for c in range(8)]
                    def att_a(i, g):
                        ti = gi * 6 + i
                        mprev = 2 if ti == 2 else 1
                        tsl = slice(i * 128, (i + 1) * 128)
                        G2, e = g // 2, g % 2
                        rows = slice(e * 64, (e + 1) * 64)
                        SA = alloc()
                        qr = buf1[rows, G2 * 4:(G2 + 1) * 4, tsl]
                        if i == 0:
                            kp, kpk = kprev[rows, l, G2, :], ("kprev", l)
                        else:
                            kp, kpk = kcur[rows, G2, (i - 1) * 128:i * 128], ("kcur", G2)
                        kc_ = kcur[rows, G2, tsl]
                        MM(psf(SA)[:, 0:512], kp, qr, True, False, [kpk] + qkeys[G2 * 4:(G2 + 1) * 4], [("ps", SA)])
                        MM(psf(SA)[:, 0:512], identb[:], maskb[:, mprev - 1, 0:512], False, True, ["identb", "maskb"], [("ps", SA)])
                        MM(psf(SA)[:, 512:1024], kc_, qr, True, False, [("kcur", G2)] + qkeys[G2 * 4:(G2 + 1) * 4], [("ps", SA)])
                        MM(psf(SA)[:, 512:1024], identb[:], maskb[:, mprev - 1, 512:1024], False, True, ["identb", "maskb"], [("ps", SA)])
                        pt, ptk = pTs.get()
                        ACT(pt[:, :], psf(SA)[:, :], AF.Exp, [("ps", SA)], [ptk], scale=0.125)
                        return (i, g, pt, ptk)

                    def att_b(i, g, pt, ptk):
                        tsl = slice(i * 128, (i + 1) * 128)
                        G2, e = g // 2, g % 2
                        rows = slice(e * 64, (e + 1) * 64)
                        SB = alloc()
                        if i == 0:
                            vp, vpk = vprev[:, l, g * 64:(g + 1) * 64], ("vprev", l)
                        else:
                            vp, vpk = vcur[:, i - 1, g * 64:(g + 1) * 64], ("vcur", i - 1)
                        MM(psf(SB)[rows, 0:512], vp, pt[:, 0:512], True, False, [vpk, ptk], [("ps", SB)])
                        MM(psf(SB)[rows, 0:512], vcur[:, i, g * 64:(g + 1) * 64], pt[:, 512:1024], False, True, [("vcur", i), ptk], [("ps", SB)])
                        MM(psf(SB)[:, 512:1024], ones[:], pt[:, 0:512], True, False, ["ones", ptk], [("ps", SB)])
                        MM(psf(SB)[:, 512:1024], ones[:], pt[:, 512:1024], False, True, ["ones", ptk], [("ps", SB)])
                        tl, tlk = tmpf.get()
                        hc0 = l * 16 + 4 * g
                        STT(tl[:, 0:512].rearrange("p (r t) -> p r t", r=4), psf(SB)[:, 512:1024].rearrange("p (r t) -> p r t", r=4), 1.0,
                            sinkexp[:, hc0:hc0 + 4].unsqueeze(2).broadcast_to([128, 4, 128]), ALU.mult, ALU.add,
                            [("ps", SB), "sinkexp"], [tlk])
                        ACT(tl[:, 0:512], tl[:, 0:512], AF.Ln, [tlk], [tlk])
                        ACT(tl[:, 0:512], tl[:, 0:512], AF.Exp, [tlk], [tlk], scale=-1.0)
                        TT(buf2[rows, G2 * 4:(G2 + 1) * 4, tsl], psf(SB)[rows, 0:512].rearrange("p (r t) -> p r t", r=4),
                           tl[rows, 0:512].rearrange("p (r t) -> p r t", r=4), ALU.mult,
                           [("ps", SB), tlk], [("ar", 8 + G2 * 4 + r) for r in range(4)])

                    pend = []
                    for i in range(i0, 6):
                        for g in range(4):
                            pend.append(att_a(i, g))
                            if len(pend) > ATT_DEPTH:
                                att_b(*pend.pop(0))
                    while pend:
                        att_b(*pend.pop(0))
                    if last:
                        chk()
                        SB = alloc(pin=True)
                        for g in range(4):
                            G2, e = g // 2, g % 2
                            rows = slice(e * 64, (e + 1) * 64)
                            qg = buf1[rows, G2 * 4:(G2 + 1) * 4, TP:T]
                            SA = alloc()
                            for s in range(NSMP):
                                MM(psf(SA)[:, s * 64:(s + 1) * 64], kcT[rows, s, G2, :], qg, True, True,
                                   ["kcT"] + qkeys[G2 * 4:(G2 + 1) * 4], [("ps", SA)])
                            pt, ptk = pTs.get()
                            ACT(pt[:, :], psf(SA)[:, :], AF.Exp, [("ps", SA)], [ptk], scale=0.125)
                            TT(pt[:, :], pt[:, :], cmb[:, :], ALU.mult, [ptk, "cmb"], [ptk])
                            SN2 = alloc()
                            MM(psf(SN2)[0:NSMP, 0:64], kcur[rows, G2, TP:T], qg, True, True,
                               [("kcur", G2)] + qkeys[G2 * 4:(G2 + 1) * 4], [("ps", SN2)])
                            tn, tnk = tmpf.get()
                            ACT(tn[0:NSMP, 0:64], psf(SN2)[0:NSMP, 0:64], AF.Exp, [("ps", SN2)], [tnk], scale=0.125)
                            TT(pd_sb[:, g * 64:(g + 1) * 64], tn[0:NSMP, 0:64], idm16[:, 0:64], ALU.mult, [tnk, "idm16"], [("pd", g)])
                            for s in range(NSMP):
                                MM(psf(SB)[rows, g * 64:(g + 1) * 64], vc[:, s, g * 64:(g + 1) * 64], pt[:, s * 64:(s + 1) * 64], s == 0, False,
                                   ["vc", ptk], [("ps", SB)])
                            MM(psf(SB)[rows, g * 64:(g + 1) * 64], vcur[0:NSMP, 6, g * 64:(g + 1) * 64], pd_sb[:, g * 64:(g + 1) * 64], False, True,
                               [("vcur", 6), ("pd", g)], [("ps", SB)])
                            for s in range(NSMP):
                                MM(psf(SB)[:, 512 + g * 64:512 + (g + 1) * 64], ones[:], pt[:, s * 64:(s + 1) * 64], s == 0, False,
                                   ["ones", ptk], [("ps", SB)])
                            MM(psf(SB)[:, 512 + g * 64:512 + (g + 1) * 64], ones[0:NSMP, :], pd_sb[:, g * 64:(g + 1) * 64], False, True,
                               ["ones", ("pd", g)], [("ps", SB)])
                        tl, tlk = tmpf.get()
                        for g in range(4):
                            hc0 = l * 16 + 4 * g
                            STT(tl[:, g * 64:(g + 1) * 64].rearrange("p (r s) -> p r s", r=4),
                                psf(SB)[:, 512 + g * 64:512 + (g + 1) * 64].rearrange("p (r s) -> p r s", r=4), 1.0,
                                sinkexp[:, hc0:hc0 + 4].unsqueeze(2).broadcast_to([128, 4, NSMP]), ALU.mult, ALU.add,
                                [("ps", SB), "sinkexp"], [tlk])
                        ACT(tl[:, 0:256], tl[:, 0:256], AF.Ln, [tlk], [tlk])
                        ACT(tl[:, 0:256], tl[:, 0:256], AF.Exp, [tlk], [tlk], scale=-1.0)
                        for g in range(4):
                            G2, e = g // 2, g % 2
                            rows = slice(e * 64, (e + 1) * 64)
                            TT(buf2[rows, G2 * 4:(G2 + 1) * 4, TP:T], psf(SB)[rows, g * 64:(g + 1) * 64].rearrange("p (r s) -> p r s", r=4),
                               tl[rows, g * 64:(g + 1) * 64].rearrange("p (r s) -> p r s", r=4), ALU.mult,
                               [("ps", SB), tlk], [("ar", 8 + G2 * 4 + r) for r in range(4)])
                        unpin(SB)
                    CP(kprev[:, l], kcur[:, :, 640:768], [("kcur", 0), ("kcur", 1)], [("kprev", l)], q="act")
                    CP(vprev[:, l, :], vcur[:, 5, :], [("vcur", 5)], [("vprev", l)], q="act")

                    chk()
                    P.tag = "P8"
                    if KDBG and gi == 0 and l == 0:
                        DMA(dbgb[0], buf2[:], [("ar", 8 + c) for c in range(8)], [], is_out=True)
                    gated_branch(l, glayer, T, "wc", "gc", buf2, lambda kc: [("ar", 8 + kc)], 2)
                    if KDBG and gi == 0 and l == 0:
                        DMA(dbgb[1], buf1[:], [("ar", c) for c in range(8)], [], is_out=True)

                    chk()
                    P.tag = "P9"
                    SN = alloc(pin=True)
                    for c in range(8):
                        u = w_unit("wo", c, glayer)
                        S = alloc()
                        chunk_mm(S, u, lambda kc: buf1[:, kc, :], lambda kc: [("ar", kc)], T)
                        ACT(merged[:, c, CS[0]:T], psf(S)[:, CS[0]:T], AF.Identity, [("ps", S)], [("mg", c)])
                        sq, sk = stats_sq(psf(S)[:, CS[0]:T], [("ps", S)], T)
                        if c > 0:
                            stats_mm(SN, *pend_sq, c - 1, T)
                        pend_sq = (sq, sk)
                    stats_mm(SN, *pend_sq, 7, T)
                    stats_finish(SN, T)
                    post_apply(l, K_POSTMIX, T)

                    chk()
                    P.tag = "P10..12"
                    norm_to_hb(l, K_PREFFN, T)
                    for j in range(22):
                        ug = w_unit("fg", j, glayer)
                        S1 = alloc()
                        chunk_mm(S1, ug, hbr, hbk, T)
                        uu = w_unit("fu", j, glayer)
                        S2 = alloc()
                        chunk_mm(S2, uu, hbr, hbk, T)
                        t, tk = tmpf.get()
                        ACT(t[:, CS[0]:T], psf(S1)[:, CS[0]:T], AF.Silu, [("ps", S1)], [tk])
                        TT(actb[:, j, CS[0]:T], psf(S2)[:, CS[0]:T], t[:, CS[0]:T], ALU.mult, [("ps", S2), tk], [("ar", j)])
                    SN = alloc(pin=True)
                    for c in range(8):
                        k, wkey = w_down(c, glayer)
                        S = alloc()
                        for kc in range(22):
                            lhsT = ring[:, k, kc * 128:(kc + 1) * 128]
                            for (c0, c1) in ((CS[0], 512), (512, T)):
                                MM(psf(S)[:, c0:c1], lhsT, actb[:, kc, c0:c1], kc == 0, kc == 21, [wkey, ("ar", kc)], [("ps", S)])
                        ACT(merged[:, c, CS[0]:T], psf(S)[:, CS[0]:T], AF.Identity, [("ps", S)], [("mg", c)])
                        sq, sk = stats_sq(psf(S)[:, CS[0]:T], [("ps", S)], T)
                        if c > 0:
                            stats_mm(SN, *pend_sq, c - 1, T)
                        pend_sq = (sq, sk)
                    stats_mm(SN, *pend_sq, 7, T)
                    stats_finish(SN, T)
                    post_apply(l, K_POSTFFN, T)

                chk()
                if gi == 0:
                    DMA(yT[:, :, 0:512], xs[:, :, 256:768], [("xs", c) for c in range(8)], [], is_out=True)
                elif gi == 1:
                    DMA(yT[:, :, 512:1280], xs[:, :, 0:768], [("xs", c) for c in range(8)], [], is_out=True)
                else:
                    DMA(yT[:, :, 1280:NYT], xs[:, :, 0:T], [("xs", c) for c in range(8)], [], is_out=True)

        try:
            main_loops()
        except _Stop:
            pass
        P.op("sp", lambda e: e.nop(), extra_deps=list(out_toks))
        stats = P.emit(block, sems, dsems)
    _CACHE['plan'] = P
    return nc, stats


IN_OFF = dict(ua=0, va=1024, bg=2048, cg=3072, hb=4096, q=5120, k=6144, v=6400, ga=6656, gb=7680, gc=8704)


def _unit(W, cols, rows=None):
    Wr = W if rows is None else W[rows]
    blk = Wr[:, cols]
    KC = blk.shape[0] // 128
    return blk.reshape(KC, 128, 128).transpose(1, 0, 2).reshape(128, KC * 128)


def pack_weights(inp):
    out = np.zeros((2, NSLAB, 128, 4096), np.float32)
    ar = np.arange(128)
    qcols = {}
    for c in range(8):
        cols = np.empty(128, np.int64)
        for e in range(2):
            h = head_of(c, e)
            cols[e * 64:(e + 1) * 64] = h * 64 + np.arange(64)
        qcols[c] = cols
    rows_c = np.concatenate([qcols[c] for c in range(8)])
    for l in range(2):
        w_in = np.asarray(inp["w_in"][l])
        mats = dict(wa=np.asarray(inp["w_br_a"][l]), wb=np.asarray(inp["w_br_b"][l]), wc=np.asarray(inp["w_br_c"][l]),
                    wo=np.asarray(inp["w_out"][l]), fg=np.asarray(inp["w_ffn_gate"][l]), fu=np.asarray(inp["w_ffn_up"][l]))
        fd = np.asarray(inp["w_ffn_down"][l])
        units = []
        for (name, idx) in unit_sequence():
            if name in ("ua", "va", "bg", "cg", "hb", "k", "v", "ga", "gb", "gc"):
                units.append(_unit(w_in, IN_OFF[name] + idx * 128 + ar))
            elif name == "q":
                units.append(_unit(w_in, IN_OFF["q"] + qcols[idx]))
            elif name == "wc":
                units.append(_unit(mats["wc"], idx * 128 + ar, rows=rows_c))
            else:
                units.append(_unit(mats[name], idx * 128 + ar))
        for s in range(38):
            out[l, s] = np.concatenate(units[s * 4:(s + 1) * 4], axis=1)
        for c in range(8):
            out[l, 38 + c, :, 0:2816] = _unit(fd, c * 128 + ar)
    return out


def fm(v):
    v = np.asarray(v)
    return np.moveaxis(v.reshape(v.shape[:-1] + (8, 128)), -1, 0)


def prep(inp):
    f32 = np.float32
    x_prompt = np.asarray(inp["x_prompt"], f32)[0]
    x_sample = np.asarray(inp["x_sample"], f32)[:, 0]
    state_conv = np.asarray(inp["state_conv"], f32)
    ck = np.asarray(inp["cache_win_k"], f32)
    cvv = np.asarray(inp["cache_win_v"], f32)
    shared = {}
    shared["wst"] = pack_weights(inp)
    cvec = np.zeros((128, 160), f32)
    for l in range(2):
        vecs = [inp["norm_pre_mix"][l], inp["norm_post_mix"][l], inp["norm_pre_ffn"][l], inp["norm_post_ffn"][l],
                inp["chunk_ln_g"][l], inp["chunk_ln_b"][l], inp["conv_w"][l][0], inp["conv_w"][l][1], inp["conv_w"][l][2],
                inp["conv_b"][l]]
        for kd, v in enumerate(vecs):
            cvec[:, l * 80 + kd * 8:l * 80 + kd * 8 + 8] = np.asarray(v, f32).reshape(8, 128).T
    shared["cvec"] = cvec
    wsp = np.asarray(inp["w_spatial"], f32)
    shared["wsT_in"] = np.ascontiguousarray(wsp.transpose(0, 3, 1, 2))
    bsp = np.asarray(inp["b_spatial"], f32)
    shared["bs_bc"] = np.ascontiguousarray(np.broadcast_to(bsp[:, None], (2, 128, 8, 128)))
    shared["identf_in"] = np.eye(128, dtype=f32)
    shared["tri_in"] = np.triu(np.ones((128, 128), f32))
    rot = np.zeros((128, 128), f32)
    for m in range(128):
        if m % 64 < 32:
            rot[m + 32, m] = -1.0
        else:
            rot[m - 32, m] = 1.0
    shared["rotm_in"] = rot.astype(ml_dtypes.bfloat16)
    sinks = np.asarray(inp["attn_sinks"], f32)
    shared["sinks_bc"] = np.ascontiguousarray(np.broadcast_to(sinks.reshape(1, 32), (128, 32)))
    shared["ws00_bc"] = np.ascontiguousarray(np.broadcast_to(wsp[:, :, 0, 0].reshape(1, 16), (16, 16)))
    idm = np.zeros((16, 4, 4, 16), f32)
    for s in range(16):
        idm[s, :, :, s] = 1.0
    shared["idm16_in"] = idm.reshape(16, 256)
    cm = np.zeros((128, 16, 4, 16), f32)
    for s in range(16):
        cm[:, s, :, s] = 1.0
    shared["cm_in"] = cm.reshape(128, 1024)
    kk = np.arange(128)[:, None]
    qq = np.arange(128)[None, :]
    m_cur = np.where(kk <= qq, 0.0, MASKNEG).astype(f32)
    m_prev = np.where(kk >= qq, 0.0, MASKNEG).astype(f32)
    m_none = np.full((128, 128), MASKNEG, f32)
    inv = np.power(np.float32(10000.0), -np.arange(32, dtype=f32) * np.float32(2.0 / 64)).astype(f32)
    invp = inv[(np.arange(128) % 64) % 32]

    xpad = np.concatenate([np.zeros((256, D), f32), x_prompt], axis=0)
    per_core = []
    for core in range(NCORE):
        d = {}
        xp = xpad[core * 2048:core * 2048 + 2304]
        xsmp = x_sample[core * NSMP:(core + 1) * NSMP]
        xa = np.concatenate([xp, xsmp], axis=0)
        d["xT"] = np.ascontiguousarray(xa.reshape(NT, 8, 128).transpose(2, 1, 0))
        pos = np.concatenate([np.arange(core * 2048 - 256, core * 2048 + 2048), np.full(NSMP, PAST)]).astype(np.int32)
        ang = pos.astype(f32)[None, :] * invp[:, None]
        d["cosT"] = np.cos(ang).astype(f32)
        d["sinT"] = np.sin(ang).astype(f32)
        mp0 = m_none if core == 0 else m_prev
        def rep4(m):
            return np.broadcast_to(m[:, None, :], (128, 4, 128)).reshape(128, 512)
        d["maskb_in"] = np.ascontiguousarray(np.stack([np.concatenate([rep4(m_prev), rep4(m_cur)], axis=1),
                                                       np.concatenate([rep4(mp0), rep4(m_cur)], axis=1)], axis=1))
        sl = slice(core * NSMP, (core + 1) * NSMP)
        sc = state_conv[:, sl]
        d["sc_nat"] = np.ascontiguousarray(sc)
        d["histT"] = np.ascontiguousarray(sc.reshape(2, NSMP, 2, 8, 128).transpose(0, 4, 2, 3, 1))
        ckc = ck[:, sl]
        d["ck_nat"] = np.ascontiguousarray(ckc.reshape(2, NSMP, 128, 256))
        d["cv_nat"] = np.ascontiguousarray(cvv[:, sl].reshape(2, NSMP, 128, 256))
        t = ckc.reshape(2, NSMP, 128, 2, 2, 64)
        d["kcT_in"] = np.ascontiguousarray(t.transpose(0, 4, 5, 1, 3, 2).reshape(2, 128, NSMP, 2, 128))
        d.update(shared)
        per_core.append(d)
    return per_core


def kernel(**inputs):
    if "nc" not in _CACHE:
        _CACHE["nc"] = build_program()
    nc, _ = _CACHE["nc"]
    in_maps = prep(inputs)
    res = run_bass_kernel_spmd(nc, in_maps, core_ids=list(range(NCORE)))
    R = res.results
    f32 = np.float32
    y_prompt = np.zeros((1, 16384, D), f32)
    y_sample = np.zeros((128, 1, D), f32)
    for core in range(NCORE):
        yT = R[core]["yT"]
        ya = yT.transpose(2, 1, 0).reshape(NYT, D)
        y_prompt[0, core * 2048:(core + 1) * 2048] = ya[:2048]
        y_sample[core * NSMP:(core + 1) * NSMP, 0] = ya[2048:]
    last = R[NCORE - 1]
    prompt_conv = np.zeros((2, 1, 2, D), f32)
    prompt_k = np.zeros((2, 1, 128, 4, 64), f32)
    prompt_v = np.zeros((2, 1, 128, 4, 64), f32)
    for l in range(2):
        prompt_conv[l, 0] = last["pconvT"][l].transpose(2, 1, 0).reshape(2, D)
        t = last["pkT"][l].reshape(2, 64, 2, 128)
        prompt_k[l, 0] = t.transpose(3, 2, 0, 1).reshape(128, 4, 64)
        prompt_v[l, 0] = last["pv"][l].reshape(128, 4, 64)
    sample_conv = np.zeros((2, 128, 2, D), f32)
    sample_k = np.zeros((2, 128, 128, 4, 64), f32)
    sample_v = np.zeros((2, 128, 128, 4, 64), f32)
    sample_cv = np.zeros((2, 128, 1, D), f32)
    for core in range(NCORE):
        r = R[core]
        sl = slice(core * NSMP, (core + 1) * NSMP)
        for l in range(2):
            sample_conv[l, sl, 0] = r["sconv_old"][l]
            sample_conv[l, sl, 1] = r["sconv_newT"][l].transpose(2, 1, 0).reshape(NSMP, D)
            sample_k[l, sl, 0:127] = r["sk_shift"][l].reshape(NSMP, 127, 4, 64)
            t = r["sk_newT"][l].reshape(2, 64, 2, NSMP)
            sample_k[l, sl, 127] = t.transpose(3, 2, 0, 1).reshape(NSMP, 4, 64)
            sample_v[l, sl, 0:127] = r["sv_shift"][l].reshape(NSMP, 127, 4, 64)
            sample_v[l, sl, 127] = r["sv_new"][l].reshape(NSMP, 4, 64)
            sample_cv[l, sl, 0] = r["scvT"][l].transpose(2, 1, 0).reshape(NSMP, D)
    return (y_prompt, y_sample, prompt_conv, prompt_k, prompt_v, sample_conv, sample_k, sample_v, sample_cv)
```

```python
import numpy as np
import ml_dtypes
from contextlib import ExitStack
import concourse.bass as bass
import concourse.mybir as mybir
from concourse.bass_utils import run_bass_kernel_spmd

F32 = mybir.dt.float32
BF16 = mybir.dt.bfloat16
AF = mybir.ActivationFunctionType
ALU = mybir.AluOpType

NCORE = 8
D = 1024
TP = 768
NGRP = 3
NSMP = 16
TMAX = TP + NSMP
NT = NGRP * TP + NSMP
NYT = 2048 + NSMP
NSLAB = 46
NB = 3
ATT_DEPTH = 2
PAST = 16384
EPS_RMS = 1e-6
EPS_LN = 1e-5
MASKNEG = -30000.0

K_PREMIX, K_POSTMIX, K_PREFFN, K_POSTFFN, K_LNG, K_LNB, K_CW0, K_CW1, K_CW2, K_CB = range(10)

QUEUES = ("pe", "act", "dve", "pool", "sp")
NDMASEM = 24
EPOCH = 400
NEPOCH = 8
KLIMIT = 9999
KGROUPS = ""
KSKIP = ""
KDBG = False


_CACHE = {}


class _Stop(Exception):
    pass


class Plan:
    def __init__(self):
        self.ops = {q: [] for q in QUEUES}
        self.state = {}
        self.dma_uses = [0] * NDMASEM
        self.tag = "setup"
        self.dma_rr = {"sp": 0, "pool": 0, "act": 0}
        self.dma_rng = {"sp": (0, 14), "pool": (14, 8), "act": (22, 2)}

    def _st(self, k):
        s = self.state.get(k)
        if s is None:
            s = [None, {}]
            self.state[k] = s
        return s

    def op(self, q, fn, reads=(), writes=(), dma=False, extra_deps=()):
        deps = []
        for k in reads:
            s = self._st(k)
            if s[0] is not None:
                deps.append(s[0])
        for k in writes:
            s = self._st(k)
            if s[0] is not None:
                deps.append(s[0])
            deps.extend(s[1].values())
        deps.extend(extra_deps)
        idx = len(self.ops[q])
        rec = dict(fn=fn, deps=deps, signal=False, dma=None, tag=self.tag)
        if dma:
            base, n = self.dma_rng[q]
            si = base + self.dma_rr[q] % n
            self.dma_rr[q] += 1
            prev = self.dma_uses[si]
            self.dma_uses[si] += 1
            if prev > 0:
                rec["deps"].append(("d", si, 16 * prev))
            rec["dma"] = si
            tok = ("d", si, 16 * (prev + 1))
        else:
            tok = ("e", q, idx)
        self.ops[q].append(rec)
        for k in reads:
            s = self._st(k)
            tk = tok[:2]
            old = s[1].get(tk)
            if old is None or old[2] < tok[2]:
                s[1][tk] = tok
        for k in writes:
            s = self._st(k)
            s[0] = tok
            s[1] = {}
        return tok

    def emit(self, block, sems, dsems):
        for q in QUEUES:
            for rec in self.ops[q]:
                for d in rec["deps"]:
                    if d[0] == "e":
                        if d[1] == q and q in ("pe", "sp"):
                            continue
                        self.ops[d[1]][d[2]]["signal"] = True
        cnt = {}
        for q in QUEUES:
            c = 0
            arr = []
            for rec in self.ops[q]:
                if rec["signal"]:
                    c += 1
                arr.append(c)
            cnt[q] = arr
        engs = dict(pe=block.tensor, act=block.scalar, dve=block.vector, pool=block.gpsimd, sp=block.sync)
        stats = {}
        for q in QUEUES:
            ops = self.ops[q]
            if not ops:
                continue

            def body(eng, q=q, ops=ops):
                waited = {}
                nw = 0
                mycnt = [0]
                for rec in ops:
                    need = {}
                    for d in rec["deps"]:
                        if d[0] == "e":
                            if d[1] == q and q in ("pe", "sp"):
                                continue
                            key = ("e", d[1])
                            val = cnt[d[1]][d[2]]
                        else:
                            key = ("d", d[1])
                            val = d[2]
                        if waited.get(key, 0) < val and need.get(key, 0) < val:
                            need[key] = val
                    for key, val in need.items():
                        if key[0] == "e":
                            eng.wait_ge(sems[key[1]][(val - 1) // EPOCH], (val - 1) % EPOCH + 1)
                        else:
                            eng.wait_ge(dsems[key[1]], val)
                        waited[key] = val
                        nw += 1
                    ins = rec["fn"](eng)
                    if rec["dma"] is not None:
                        ins.then_inc(dsems[rec["dma"]], 16)
                    elif rec["signal"]:
                        mycnt[0] += 1
                        ins.then_inc(sems[q][(mycnt[0] - 1) // EPOCH], 1)
                stats[q] = (len(ops), nw)

            engs[q](body)
        return stats


class Rot:
    def __init__(self, name, aps):
        self.name, self.aps, self.i = name, aps, 0

    def get(self):
        k = self.i % len(self.aps)
        self.i += 1
        return self.aps[k], (self.name, k)


def unit_sequence():
    seq = []
    seq += [("va", c) for c in range(8)]
    seq += [("ua", c) for c in range(8)]
    for c in range(8):
        seq += [("cg", c), ("hb", c), ("bg", c)]
    for c in range(8):
        seq += [("wa", c), ("ga", c)]
    seq += [("q", c) for c in range(8)]
    seq += [("k", 0), ("k", 1), ("v", 0), ("v", 1)]
    for c in range(8):
        seq += [("wb", c), ("gb", c)]
    for c in range(8):
        seq += [("wc", c), ("gc", c)]
    seq += [("wo", c) for c in range(8)]
    for j in range(22):
        seq += [("fg", j), ("fu", j)]
    return seq


def head_of(c, e):
    G2, r = c // 4, c % 4
    return 4 * (2 * G2 + e) + r


def build_program():
    nc = bass.Bass("TRN2", target_bir_lowering=False)

    def din(name, shape, dt=F32):
        return nc.dram_tensor(name, list(shape), dt, kind="ExternalInput").ap()

    def dout(name, shape, dt=F32):
        return nc.dram_tensor(name, list(shape), dt, kind="ExternalOutput").ap()

    xT = din("xT", [128, 8, NT])
    cosT = din("cosT", [128, NT])
    sinT = din("sinT", [128, NT])
    wst = din("wst", [2, NSLAB, 128, 4096])
    cvec = din("cvec", [128, 160])
    wsT_in = din("wsT_in", [2, 128, 8, 128])
    bs_bc = din("bs_bc", [2, 128, 8, 128])
    maskb_in = din("maskb_in", [128, 2, 1024])
    identf_in = din("identf_in", [128, 128])
    tri_in = din("tri_in", [128, 128])
    rotm_in = din("rotm_in", [128, 128], BF16)
    sinks_bc = din("sinks_bc", [128, 32])
    ws00_bc = din("ws00_bc", [16, 16])
    idm16_in = din("idm16_in", [16, 256])
    cm_in = din("cm_in", [128, 1024])
    histT = din("histT", [2, 128, 2, 8, NSMP])
    kcT_in = din("kcT_in", [2, 128, NSMP, 2, 128])
    ck_nat = din("ck_nat", [2, NSMP, 128, 256])
    cv_nat = din("cv_nat", [2, NSMP, 128, 256])
    sc_nat = din("sc_nat", [2, NSMP, 2, 1024])

    yT = dout("yT", [128, 8, NYT])
    pconvT = dout("pconvT", [2, 128, 8, 2])
    pkT = dout("pkT", [2, 128, 2, 128])
    pv = dout("pv", [2, 128, 256])
    sconv_newT = dout("sconv_newT", [2, 128, 8, NSMP])
    sconv_old = dout("sconv_old", [2, NSMP, 1024])
    sk_shift = dout("sk_shift", [2, NSMP, 127, 256])
    sk_newT = dout("sk_newT", [2, 128, 2, NSMP])
    sv_shift = dout("sv_shift", [2, NSMP, 127, 256])
    sv_new = dout("sv_new", [2, NSMP, 256])
    scvT = dout("scvT", [2, 128, 8, NSMP])
    if KDBG:
        dbgf = dout("dbgf", [3, 128, 8, TMAX])
        dbgb = dout("dbgb", [2, 128, 8, TMAX], BF16)

    P = Plan()
    out_toks = []

    with ExitStack() as es:
        def sb(name, shape, dt):
            return es.enter_context(nc.sbuf_tensor(name, list(shape), dt))

        xs = sb("xs", [128, 8, TMAX], F32)
        hb = sb("hb", [128, 8, TMAX], BF16)
        arena = sb("arena", [128, 22 * TMAX], BF16)
        merged = sb("merged", [128, 8, TMAX], F32)
        kcur = sb("kcur", [128, 2, TMAX], BF16)
        kprev = sb("kprev", [128, 2, 2, 128], BF16)
        vcur = sb("vcur", [128, 7, 256], BF16)
        vprev = sb("vprev", [128, 2, 256], BF16)
        kTf = sb("kTf", [128, 2, 144], F32)
        cxe_t = sb("cxe", [128, 2, 2 + TMAX], F32)
        cxh = sb("cxh", [128, 2, 8, 2], F32)
        scx = sb("scx", [128, 8, NSMP], F32)
        scv_sb = sb("scv_sb", [128, 8, NSMP], F32)
        cos_sb = sb("cos_sb", [128, TMAX], F32)
        sin_sb = sb("sin_sb", [128, TMAX], F32)
        rstd = sb("rstd", [128, TMAX], F32)
        sq_t = sb("sq_t", [128, 3, TMAX], BF16)
        tmpf_t = sb("tmpf_t", [128, 3, 1024], F32)
        pT_t = sb("pT_t", [128, ATT_DEPTH + 1, 1024], BF16)
        ring = sb("ring", [128, NB, 4096], BF16)
        cv_sb = sb("cv_sb", [128, 160], F32)
        wsTb = sb("wsTb", [128, 2, 8, 128], BF16)
        bias2 = sb("bias2", [128, 2, 8, 128], F32)
        maskb = sb("maskb", [128, 2, 1024], BF16)
        identf = sb("identf", [128, 128], F32)
        identb = sb("identb", [128, 128], BF16)
        tri = sb("tri", [128, 128], F32)
        ones = sb("ones", [128, 128], BF16)
        rotm = sb("rotm", [128, 128], BF16)
        sinkexp = sb("sinkexp", [128, 32], F32)
        ws00 = sb("ws00", [16, 16], F32)
        wsd = sb("wsd", [16, 2, 8, 16], BF16)
        idm16 = sb("idm16", [16, 256], F32)
        cst = sb("cst", [128, 4], F32)
        hist_sb = sb("hist_sb", [128, 2, 8, NSMP], F32)
        kcT = sb("kcT", [128, NSMP, 2, 128], BF16)
        vc = sb("vc", [128, NSMP, 256], BF16)
        lnst = sb("lnst", [128, 7, 2, 6], F32)
        lnmv = sb("lnmv", [128, 7, 2], F32)
        lnr = sb("lnr", [128, 7], F32)
        pd_sb = sb("pd_sb", [16, 256], BF16)
        cmb = sb("cmb", [128, 1024], BF16)

        ps = es.enter_context(nc.psum_tensor("ps", [128, 4096], F32))
        sems = {q: [es.enter_context(nc.semaphore("s_%s_%d" % (q, i))) for i in range(NEPOCH)] for q in QUEUES}
        dsems = [es.enter_context(nc.semaphore("d%d" % i)) for i in range(NDMASEM)]
        block = es.enter_context(nc.Block())

        buf1 = arena[:, 0:8 * TMAX].rearrange("p (c t) -> p c t", c=8)
        B2OFF = 8 * TMAX
        buf2 = arena[:, B2OFF:B2OFF + 8 * TMAX].rearrange("p (c t) -> p c t", c=8)
        vatm = arena[:, B2OFF:B2OFF + 7 * 1024].rearrange("p (i f) -> p i f", i=7)
        actb = arena[:, :].rearrange("p (c t) -> p c t", c=22)

        def arkeys(start, end):
            return [("ar", k) for k in range(start // TMAX, (end - 1) // TMAX + 1)]

        def vatm_keys(i):
            return arkeys(B2OFF + i * 1024, B2OFF + (i + 1) * 1024)

        tmpf = Rot("tmpf", [tmpf_t[:, i, :] for i in range(3)])
        sqb = Rot("sq", [sq_t[:, i, :] for i in range(3)])
        qbs = sqb
        pTs = Rot("pT", [pT_t[:, i, :] for i in range(ATT_DEPTH + 1)])
        cxes = Rot("cxe", [cxe_t[:, i, :] for i in range(2)])

        def psf(S):
            return ps[:, S * 1024:(S + 1) * 1024]

        CS = [0]
        slot_state = dict(rr=0, pinned=set())

        def alloc(pin=False):
            while True:
                s = slot_state["rr"] % 4
                slot_state["rr"] += 1
                if s not in slot_state["pinned"]:
                    break
            if pin:
                slot_state["pinned"].add(s)
            return s

        def unpin(s):
            slot_state["pinned"].discard(s)

        def cv(l, kind, c):
            i = l * 80 + kind * 8 + c
            return cv_sb[:, i:i + 1]

        def MM(out, lhsT, rhs, start, stop, reads, writes):
            P.op("pe", lambda e: e.matmul(out, lhsT=lhsT, rhs=rhs, start=start, stop=stop), reads, writes)

        def ACT(out, in_, func, reads, writes, scale=1.0, bias=None):
            if bias is None:
                P.op("act", lambda e: e.activation(out=out, in_=in_, func=func, scale=scale), reads, writes)
            else:
                P.op("act", lambda e: e.activation(out=out, in_=in_, func=func, scale=scale, bias=bias), reads, writes)

        def TT(out, in0, in1, op, reads, writes, q="dve"):
            if op == ALU.add and "ttadd" not in KSKIP:
                P.op(q, lambda e: e.scalar_tensor_tensor(out=out, in0=in0, scalar=1.0, in1=in1, op0=ALU.mult, op1=ALU.add), reads, writes)
            else:
                P.op(q, lambda e: e.tensor_tensor(out=out, in0=in0, in1=in1, op=op), reads, writes)

        def STT(out, in0, scalar, in1, op0, op1, reads, writes):
            P.op("dve", lambda e: e.scalar_tensor_tensor(out=out, in0=in0, scalar=scalar, in1=in1, op0=op0, op1=op1), reads, writes)

        def TS(out, in0, s1, s2, op0, op1, reads, writes, q="dve"):
            if s2 is None:
                P.op(q, lambda e: e.tensor_scalar(out=out, in0=in0, scalar1=s1, scalar2=None, op0=op0), reads, writes)
            else:
                P.op(q, lambda e: e.tensor_scalar(out=out, in0=in0, scalar1=s1, scalar2=s2, op0=op0, op1=op1), reads, writes)

        def CP(out, in_, reads, writes, q="dve"):
            if q == "act":
                P.op(q, lambda e: e.activation(out=out, in_=in_, func=AF.Identity), reads, writes)
            else:
                P.op(q, lambda e: e.tensor_copy(out=out, in_=in_), reads, writes)

        def DMA(out, in_, reads, writes, q="sp", is_out=False, **kw):
            t = P.op(q, lambda e: e.dma_start(out=out, in_=in_, **kw), reads, writes, dma=True)
            if is_out:
                out_toks.append(t)
            return t

        useq = unit_sequence()
        total_slabs = NGRP * 2 * NSLAB
        ws = dict(next_plan=0, oldest=0, upos=0, hold=False)

        def slab_coords(n):
            gl, s = divmod(n, NSLAB)
            return gl % 2, s

        def w_ensure():
            while ws["next_plan"] < min(ws["oldest"] + NB, total_slabs):
                n = ws["next_plan"]
                l, s = slab_coords(n)
                k = n % NB
                ncol = 4096 if s < 38 else 2816
                DMA(ring[:, k, 0:ncol], wst[l, s, :, 0:ncol], [], [("ring", k)], q="pool", max_dma_last_dim=8192)
                ws["next_plan"] += 1

        def w_unit(name, idx, glayer):
            u = ws["upos"]
            assert useq[u] == (name, idx), (useq[u], name, idx)
            ws["upos"] += 1
            n = glayer * NSLAB + u // 4
            if not ws["hold"] and n > ws["oldest"]:
                ws["oldest"] = n
            w_ensure()
            return n % NB, u % 4, ("ring", n % NB)

        def w_down(c, glayer):
            n = glayer * NSLAB + 38 + c
            if n > ws["oldest"]:
                ws["oldest"] = n
            w_ensure()
            return n % NB, ("ring", n % NB)

        def lhs_unit(uinfo, kc):
            k, j, _ = uinfo
            o = j * 1024 + kc * 128
            return ring[:, k, o:o + 128]

        def chunk_mm(S, uinfo, rhs_fn, rkey_fn, T):
            wkey = uinfo[2]
            for kc in range(8):
                lhsT = lhs_unit(uinfo, kc)
                r = rhs_fn(kc)
                for (c0, c1) in ((CS[0], 512), (512, T)):
                    MM(psf(S)[:, c0:c1], lhsT, r[:, c0:c1], kc == 0, kc == 7, [wkey] + rkey_fn(kc), [("ps", S)])

        DMA(cv_sb[:], cvec, [], ["cv"])
        DMA(identf[:], identf_in, [], ["identf"])
        DMA(tri[:], tri_in, [], ["tri"])
        DMA(rotm[:], rotm_in, [], ["rotm"])
        DMA(sinkexp[:], sinks_bc, [], ["sinkexp"])
        DMA(ws00[:], ws00_bc, [], ["ws00"])
        DMA(idm16[:], idm16_in, [], ["idm16"])
        DMA(bias2[:, 0], bs_bc[0], [], [("bias2", 0)])
        DMA(bias2[:, 1], bs_bc[1], [], [("bias2", 1)])
        DMA(maskb[:], maskb_in, [], ["maskb"], q="pool", max_dma_last_dim=8192)
        DMA(cmb[:], cm_in, [], ["cmb"], q="pool", max_dma_last_dim=8192)
        P.op("dve", lambda e: e.memset(ones[:], 1.0), [], ["ones"])
        P.op("dve", lambda e: e.memset(kprev[:], 0.0), [], [("kprev", 0), ("kprev", 1)])
        P.op("dve", lambda e: e.memset(vprev[:], 0.0), [], [("vprev", 0), ("vprev", 1)])
        P.op("dve", lambda e: e.memset(cxh[:], 0.0), [], [("cxh", 0), ("cxh", 1)])
        P.op("dve", lambda e: e.memset(cst[:, 0:1], EPS_RMS), [], ["cst"])
        P.op("dve", lambda e: e.memset(cst[:, 1:2], EPS_LN), [], ["cst"])
        CP(identb[:], identf[:], ["identf"], ["identb"])
        ACT(sinkexp[:], sinkexp[:], AF.Exp, ["sinkexp"], ["sinkexp"])
        for l in range(2):
            tw, twk = tmpf.get()
            twv = tw.rearrange("p (g t) -> p g t", g=8)
            DMA(twv, wsT_in[l], [], [twk])
            TT(wsTb[:, l], twv, tri[:].unsqueeze(1).broadcast_to([128, 8, 128]), ALU.mult, [twk, "tri"], [("wsTb", l)])
            S = alloc()
            for g in range(8):
                MM(psf(S)[:, g * 128:(g + 1) * 128], ones[:], wsTb[:, l, g, :], True, True, ["ones", ("wsTb", l)], [("ps", S)])
            for g in range(8):
                STT(bias2[:, l, g, :], psf(S)[:, g * 128:(g + 1) * 128], cv(l, K_LNB, g), bias2[:, l, g, :], ALU.mult, ALU.add,
                    [("ps", S), "cv", ("bias2", l)], [("bias2", l)])
            for g in range(8):
                TS(wsd[:, l, g, :], identf[0:16, 0:16], ws00[:, l * 8 + g:l * 8 + g + 1], None, ALU.mult, None, ["identf", "ws00"], ["wsd"])

        def stats_finish(SN, T):
            ACT(rstd[:, CS[0]:T], psf(SN)[:, CS[0]:T], AF.Ln, [("ps", SN), "cst"], ["rstd"], scale=1.0 / D, bias=cst[:, 0:1])
            ACT(rstd[:, CS[0]:T], rstd[:, CS[0]:T], AF.Exp, ["rstd"], ["rstd"], scale=-0.5)
            unpin(SN)

        def stats_sq(src, skeys, T):
            sq, sk = sqb.get()
            ACT(sq[:, CS[0]:T], src, AF.Square, skeys, [sk])
            return sq, sk

        def stats_mm(SN, sq, sk, c, T):
            for (c0, c1) in ((CS[0], 512), (512, T)):
                MM(psf(SN)[:, c0:c1], ones[:], sq[:, c0:c1], c == 0, c == 7, [sk, "ones"], [("ps", SN)])

        def stats_add(SN, src, skeys, c, T):
            sq, sk = stats_sq(src, skeys, T)
            stats_mm(SN, sq, sk, c, T)

        def norm_to_hb(l, kind, T):
            SN = alloc(pin=True)
            for c in range(8):
                stats_add(SN, xs[:, c, CS[0]:T], [("xs", c)], c, T)
            stats_finish(SN, T)
            for c in range(8):
                STT(hb[:, c, CS[0]:T], xs[:, c, CS[0]:T], cv(l, kind, c), rstd[:, CS[0]:T], ALU.mult, ALU.mult,
                    [("xs", c), "cv", "rstd"], [("hb", c)])

        def post_apply(l, kind, T):
            c0 = CS[0]
            mkeys = [("mg", c) for c in range(8)]
            TT(merged[:, :, c0:T], merged[:, :, c0:T], rstd[:, c0:T].unsqueeze(1).broadcast_to([128, 8, T - c0]), ALU.mult,
               mkeys + ["rstd"], mkeys)
            for c in range(8):
                STT(xs[:, c, c0:T], merged[:, c, c0:T], cv(l, kind, c), xs[:, c, c0:T], ALU.mult, ALU.add,
                    [("mg", c), "cv", ("xs", c)], [("xs", c)])

        hbr = lambda kc: hb[:, kc, :]
        hbk = lambda kc: [("hb", kc)]

        def gated_branch(l, glayer, T, wname, gname, src, srckeys, mode):
            for c in range(8):
                uw = w_unit(wname, c, glayer)
                S = alloc()
                chunk_mm(S, uw, lambda kc: src[:, kc, :], lambda kc: srckeys(kc), T)
                ug = w_unit(gname, c, glayer)
                S2 = alloc()
                chunk_mm(S2, ug, hbr, hbk, T)
                tg, tgk = tmpf.get()
                ACT(tg[:, CS[0]:T], psf(S2)[:, CS[0]:T], AF.Sigmoid, [("ps", S2)], [tgk])
                if mode == 0:
                    TT(merged[:, c, CS[0]:T], psf(S)[:, CS[0]:T], tg[:, CS[0]:T], ALU.mult, [("ps", S), tgk], [("mg", c)])
                else:
                    TT(tg[:, CS[0]:T], psf(S)[:, CS[0]:T], tg[:, CS[0]:T], ALU.mult, [("ps", S), tgk], [tgk])
                    if mode == 1:
                        TT(merged[:, c, CS[0]:T], merged[:, c, CS[0]:T], tg[:, CS[0]:T], ALU.add, [("mg", c), tgk], [("mg", c)])
                    else:
                        TT(buf1[:, c, CS[0]:T], merged[:, c, CS[0]:T], tg[:, CS[0]:T], ALU.add, [("mg", c), tgk], [("ar", c)])

        phase = [0]

        def chk():
            phase[0] += 1
            if phase[0] > KLIMIT:
                raise _Stop()

        def main_loops():
          for gi in range(NGRP):
            if KGROUPS and str(gi) not in KGROUPS:
                continue
            if KGROUPS and ws["next_plan"] < gi * 2 * NSLAB:
                ws["next_plan"] = ws["oldest"] = gi * 2 * NSLAB
            if True:
                T = TP + (NSMP if gi == NGRP - 1 else 0)
                last = gi == NGRP - 1
                ntile = 7 if last else 6
                col0 = gi * TP
                for c in range(8):
                    DMA(xs[:, c, :T], xT[:, c, col0:col0 + T], [], [("xs", c)])
                DMA(cos_sb[:, :T], cosT[:, col0:col0 + T], [], ["cos"])
                DMA(sin_sb[:, :T], sinT[:, col0:col0 + T], [], ["sin"])

                for l in range(2):
                    glayer = gi * 2 + l
                    ws["upos"] = 0
                    ck = (0, 128)[l] if gi == 0 else 0
                    cs = (128, 256)[l] if gi == 0 else 0
                    i0 = cs // 128
                    i0k = ck // 128
                    if last:
                        DMA(hist_sb[:], histT[l], [], ["hist"])
                        DMA(kcT[:], kcT_in[l], [], ["kcT"], q="pool", max_dma_last_dim=8192)
                        DMA(vc[:], cv_nat[l].rearrange("s k f -> k s f"), [], ["vc"], q="pool", max_dma_last_dim=8192)
                        if gi == NGRP - 1:
                            DMA(sk_shift[l], ck_nat[l, :, 1:128, :], [], [], is_out=True)
                            DMA(sv_shift[l], cv_nat[l, :, 1:128, :], [], [], is_out=True)
                            DMA(sconv_old[l], sc_nat[l, :, 1, :], [], [], is_out=True)

                    chk()
                    P.tag = "P1"
                    CS[0] = ck
                    norm_to_hb(l, K_PREMIX, T)
                    CS[0] = cs

                    chk()
                    P.tag = "P2"
                    ws["hold"] = True
                    uva = [w_unit("va", c, glayer) for c in range(8)]
                    for i in range(i0, ntile):
                        M = 128 if i < 6 else NSMP
                        tc0 = i * 128
                        S = alloc()
                        for h2 in range(2):
                            u0 = uva[h2 * 4]
                            slab = ring[:, u0[0], :].rearrange("p (u k c) -> p u k c", u=4, k=8)
                            for kc in range(8):
                                MM(psf(S)[0:M, h2 * 512:(h2 + 1) * 512], hb[:, kc, tc0:tc0 + M], slab[:, :, kc, :], kc == 0, kc == 7,
                                   [u0[2], ("hb", kc)], [("ps", S)])
                        tg, tgk = tmpf.get()
                        ACT(tg[0:M, :], psf(S)[0:M, :], AF.Gelu_apprx_tanh, [("ps", S)], [tgk])
                        for h2 in range(2):
                            P.op("dve", lambda e, tg=tg, M=M, i=i, h2=h2: e.bn_stats(out=lnst[0:M, i, h2, :], in_=tg[0:M, h2 * 512:(h2 + 1) * 512]),
                                 [tgk], [("lnst", i)])
                        P.op("dve", lambda e, M=M, i=i: e.bn_aggr(out=lnmv[0:M, i, :], in_=lnst[0:M, i, :, :].rearrange("p a b -> p (a b)")),
                             [("lnst", i)], [("lnmv", i)])
                        CP(vatm[0:M, i, :], tg[0:M, :], [tgk], vatm_keys(i))
                        if i == 6:
                            tlast, tlastk = tg, tgk
                    ws["hold"] = False
                    nl = ntile
                    ACT(lnr[:, i0:nl], lnmv[:, i0:nl, 1], AF.Ln, [("lnmv", i) for i in range(i0, nl)] + ["cst"], ["lnr"], bias=cst[:, 1:2])
                    ACT(lnr[:, i0:nl], lnr[:, i0:nl], AF.Exp, ["lnr"], ["lnr"], scale=-0.5)
                    for i in range(i0, ntile):
                        M = 128 if i < 6 else NSMP
                        TS(vatm[0:M, i, :], vatm[0:M, i, :], lnmv[0:M, i, 0:1], lnr[0:M, i:i + 1], ALU.subtract, ALU.mult,
                           vatm_keys(i) + [("lnmv", i), "lnr"], vatm_keys(i))
                    if last:
                        TS(tlast[0:NSMP, :], tlast[0:NSMP, :], lnmv[0:NSMP, 6, 0:1], lnr[0:NSMP, 6:7], ALU.subtract, ALU.mult,
                           [tlastk, ("lnmv", 6), "lnr"], [tlastk])
                        S = alloc()
                        for c in range(8):
                            P.op("pe", lambda e, S=S, c=c, tlast=tlast: e.transpose(psf(S)[:, c * NSMP:(c + 1) * NSMP], tlast[0:NSMP, c * 128:(c + 1) * 128], identf[0:NSMP, 0:NSMP]),
                                 [tlastk, "identf"], [("ps", S)])
                        for c in range(8):
                            ACT(scv_sb[:, c, :], psf(S)[:, c * NSMP:(c + 1) * NSMP], AF.Identity, [("ps", S), "cv"], ["scv"],
                                scale=cv(l, K_LNG, c), bias=cv(l, K_LNB, c))
                        DMA(scvT[l], scv_sb[:], ["scv"], [], is_out=True)

                    def spatial_g(g):
                        S = alloc()
                        for i in range(i0, 6):
                            MM(psf(S)[:, i * 128:(i + 1) * 128], vatm[:, i, g * 128:(g + 1) * 128], wsTb[:, l, g, :], True, True,
                               vatm_keys(i) + [("wsTb", l)], [("ps", S)])
                        if last:
                            MM(psf(S)[:, TP:T], vatm[0:NSMP, 6, g * 128:(g + 1) * 128], wsd[:, l, g, :], True, True,
                               vatm_keys(6) + ["wsd"], [("ps", S)])
                        t, tk = tmpf.get()
                        STT(t[:, cs:TP].rearrange("p (i t) -> p i t", i=6 - i0), psf(S)[:, cs:TP].rearrange("p (i t) -> p i t", i=6 - i0), cv(l, K_LNG, g),
                            bias2[:, l, g, :].unsqueeze(1).broadcast_to([128, 6 - i0, 128]), ALU.mult, ALU.add,
                            [("ps", S), "cv", ("bias2", l)], [tk])
                        if last:
                            STT(t[:, TP:T], psf(S)[:, TP:T], cv(l, K_LNG, g), bias2[:, l, g, 0:1].broadcast_to([128, NSMP]), ALU.mult, ALU.add,
                                [("ps", S), "cv", ("bias2", l)], [tk])
                        TT(buf1[:, g, CS[0]:T], buf1[:, g, CS[0]:T], t[:, CS[0]:T], ALU.mult, [("ar", g), tk], [("ar", g)])

                    for c in range(8):
                        u = w_unit("ua", c, glayer)
                        S = alloc()
                        chunk_mm(S, u, hbr, hbk, T)
                        ACT(buf1[:, c, CS[0]:T], psf(S)[:, CS[0]:T], AF.Gelu_apprx_tanh, [("ps", S)], [("ar", c)])
                        if c >= 1:
                            spatial_g(c - 1)
                    spatial_g(7)
                    chk()
                    P.tag = "spatial"
                    chk()
                    P.tag = "P3"
                    for c in range(8):
                        CS[0] = ck
                        uc = w_unit("cg", c, glayer)
                        Sc = alloc()
                        chunk_mm(Sc, uc, hbr, hbk, T)
                        uh = w_unit("hb", c, glayer)
                        Sh = alloc()
                        chunk_mm(Sh, uh, hbr, hbk, T)
                        CS[0] = cs
                        ub = w_unit("bg", c, glayer)
                        Sb = alloc()
                        chunk_mm(Sb, ub, hbr, hbk, T)
                        CS[0] = ck
                        th, thk = tmpf.get()
                        ACT(th[:, CS[0]:T], psf(Sh)[:, CS[0]:T], AF.Identity, [("ps", Sh)], [thk])
                        cxe, cxk = cxes.get()
                        CP(cxe[:, 0:2], cxh[:, l, c, :], [("cxh", l)], [cxk], q="act")
                        TT(cxe[:, 2 + ck:2 + T], psf(Sc)[:, CS[0]:T], th[:, CS[0]:T], ALU.mult, [("ps", Sc), thk], [cxk])
                        CS[0] = cs
                        ACT(th[:, cs:TP], cxe[:, 2 + cs:2 + TP], AF.Identity, [cxk, "cv"], [thk], scale=cv(l, K_CW2, c), bias=cv(l, K_CB, c))
                        STT(th[:, cs:TP], cxe[:, 1 + cs:1 + TP], cv(l, K_CW1, c), th[:, cs:TP], ALU.mult, ALU.add, [cxk, "cv", thk], [thk])
                        STT(th[:, cs:TP], cxe[:, cs:TP], cv(l, K_CW0, c), th[:, cs:TP], ALU.mult, ALU.add, [cxk, "cv", thk], [thk])
                        TT(buf2[:, c, cs:TP], psf(Sb)[:, cs:TP], th[:, cs:TP], ALU.mult, [("ps", Sb), thk], [("ar", 8 + c)])
                        CP(cxh[:, l, c, :], cxe[:, TP:TP + 2], [cxk], [("cxh", l)], q="act")
                        if last:
                            sc_ = slice(TP, T)
                            ACT(th[:, sc_], cxe[:, 2 + TP:2 + T], AF.Identity, [cxk, "cv"], [thk], scale=cv(l, K_CW2, c), bias=cv(l, K_CB, c))
                            STT(th[:, sc_], hist_sb[:, 1, c, :], cv(l, K_CW1, c), th[:, sc_], ALU.mult, ALU.add, ["hist", "cv", thk], [thk])
                            STT(th[:, sc_], hist_sb[:, 0, c, :], cv(l, K_CW0, c), th[:, sc_], ALU.mult, ALU.add, ["hist", "cv", thk], [thk])
                            TT(buf2[:, c, sc_], psf(Sb)[:, sc_], th[:, sc_], ALU.mult, [("ps", Sb), thk], [("ar", 8 + c)])
                            CP(scx[:, c, :], cxe[:, 2 + TP:2 + T], [cxk], ["scx"], q="act")
                    if last:
                        DMA(pconvT[l], cxh[:, l], [("cxh", l)], [], is_out=True)
                        DMA(sconv_newT[l], scx[:], ["scx"], [], is_out=True)

                    chk()
                    P.tag = "P4"
                    gated_branch(l, glayer, T, "wa", "ga", buf1, lambda kc: [("ar", kc)], 0)
                    if KDBG and gi == 0 and l == 0:
                        DMA(dbgf[0], merged[:], [("mg", c) for c in range(8)], [], is_out=True)

                    chk()
                    P.tag = "P5"
                    def rope_tail(c, qb, qbk):
                        isq = c < 8
                        CS[0] = cs if isq else ck
                        S2 = alloc()
                        for (c0, c1) in ((CS[0], 512), (512, T)):
                            MM(psf(S2)[:, c0:c1], rotm[:], qb[:, c0:c1], True, True, ["rotm", qbk], [("ps", S2)])
                        t1, t1k = tmpf.get()
                        t2, t2k = tmpf.get()
                        TT(t1[:, CS[0]:T], qb[:, CS[0]:T], cos_sb[:, CS[0]:T], ALU.mult, [qbk, "cos"], [t1k])
                        TT(t2[:, CS[0]:T], psf(S2)[:, CS[0]:T], sin_sb[:, CS[0]:T], ALU.mult, [("ps", S2), "sin"], [t2k])
                        if isq:
                            TT(buf1[:, c, CS[0]:T], t1[:, CS[0]:T], t2[:, CS[0]:T], ALU.add, [t1k, t2k], [("ar", c)])
                        else:
                            G2 = c - 8
                            TT(kcur[:, G2, CS[0]:T], t1[:, CS[0]:T], t2[:, CS[0]:T], ALU.add, [t1k, t2k], [("kcur", G2)])
                            if last:
                                TT(kTf[:, G2, 0:T - 640], t1[:, 640:T], t2[:, 640:T], ALU.add, [t1k, t2k], ["kTf"])

                    pend = None
                    for c in range(10):
                        isq = c < 8
                        CS[0] = cs if isq else ck
                        u = w_unit("q" if isq else "k", c if isq else c - 8, glayer)
                        S = alloc()
                        chunk_mm(S, u, hbr, hbk, T)
                        qb, qbk = qbs.get()
                        ACT(qb[:, CS[0]:T], psf(S)[:, CS[0]:T], AF.Identity, [("ps", S)], [qbk])
                        if pend is not None:
                            rope_tail(*pend)
                        pend = (c, qb, qbk)
                    rope_tail(*pend)
                    CS[0] = cs
                    if last:
                        DMA(pkT[l], kTf[:, :, 0:128], ["kTf"], [], is_out=True)
                        DMA(sk_newT[l], kTf[:, :, 128:144], ["kTf"], [], is_out=True)
                    uv0 = w_unit("v", 0, glayer)
                    uv1 = w_unit("v", 1, glayer)
                    vslab = ring[:, uv0[0], :].rearrange("p (u k c) -> p u k c", u=4, k=8)
                    Sv = None
                    for i in range(i0k, ntile):
                        M = 128 if i < 6 else NSMP
                        j = (i - i0k) % 4
                        if j == 0:
                            Sv = alloc()
                        for kc in range(8):
                            MM(psf(Sv)[0:M, j * 256:(j + 1) * 256], hb[:, kc, i * 128:i * 128 + M], vslab[:, 2:4, kc, :], kc == 0, kc == 7,
                               [uv0[2], ("hb", kc)], [("ps", Sv)])
                        ACT(vcur[0:M, i, :], psf(Sv)[0:M, j * 256:(j + 1) * 256], AF.Identity, [("ps", Sv)], [("vcur", i)])
                        if last and i == 5:
                            vf, vfk = tmpf.get()
                            ACT(vf[:, 0:256], psf(Sv)[:, j * 256:(j + 1) * 256], AF.Identity, [("ps", Sv)], [vfk])
                            DMA(pv[l], vf[:, 0:256], [vfk], [], is_out=True)
                        if last and i == 6:
                            vf, vfk = tmpf.get()
                            ACT(vf[0:NSMP, 0:256], psf(Sv)[0:NSMP, j * 256:(j + 1) * 256], AF.Identity, [("ps", Sv)], [vfk])
                            DMA(sv_new[l], vf[0:NSMP, 0:256], [vfk], [], is_out=True)

                    chk()
                    P.tag = "P6"
                    gated_branch(l, glayer, T, "wb", "gb", buf2, lambda kc: [("ar", 8 + kc)], 1)
                    if KDBG and gi == 0 and l == 0:
                        DMA(dbgf[1], merged[:], [("mg", c) for c in range(8)], [], is_out=True)

                    chk()
                    P.tag = "P7"
                    qkeys = [("ar", c) for c in range(8)]
                    def att_a(i, g):
                        ti = gi * 6 + i
                        mprev = 2 if ti == 2 else 1
                        tsl = slice(i * 128, (i + 1) * 128)
                        G2, e = g // 2, g % 2
                        rows = slice(e * 64, (e + 1) * 64)
                        SA = alloc()
                        qr = buf1[rows, G2 * 4:(G2 + 1) * 4, tsl]
                        if i == 0:
                            kp, kpk = kprev[rows, l, G2, :], ("kprev", l)
                        else:
                            kp, kpk = kcur[rows, G2, (i - 1) * 128:i * 128], ("kcur", G2)
                        kc_ = kcur[rows, G2, tsl]
                        MM(psf(SA)[:, 0:512], kp, qr, True, False, [kpk] + qkeys[G2 * 4:(G2 + 1) * 4], [("ps", SA)])
                        MM(psf(SA)[:, 0:512], identb[:], maskb[:, mprev - 1, 0:512], False, True, ["identb", "maskb"], [("ps", SA)])
                        MM(psf(SA)[:, 512:1024], kc_, qr, True, False, [("kcur", G2)] + qkeys[G2 * 4:(G2 + 1) * 4], [("ps", SA)])
                        MM(psf(SA)[:, 512:1024], identb[:], maskb[:, mprev - 1, 512:1024], False, True, ["identb", "maskb"], [("ps", SA)])
                        pt, ptk = pTs.get()
                        ACT(pt[:, :], psf(SA)[:, :], AF.Exp, [("ps", SA)], [ptk], scale=0.125)
                        return (i, g, pt, ptk)

                    def att_b(i, g, pt, ptk):
                        tsl = slice(i * 128, (i + 1) * 128)
                        G2, e = g // 2, g % 2
                        rows = slice(e * 64, (e + 1) * 64)
                        SB = alloc()
                        if i == 0:
                            vp, vpk = vprev[:, l, g * 64:(g + 1) * 64], ("vprev", l)
                        else:
                            vp, vpk = vcur[:, i - 1, g * 64:(g + 1) * 64], ("vcur", i - 1)
                        MM(psf(SB)[rows, 0:512], vp, pt[:, 0:512], True, False, [vpk, ptk], [("ps", SB)])
                        MM(psf(SB)[rows, 0:512], vcur[:, i, g * 64:(g + 1) * 64], pt[:, 512:1024], False, True, [("vcur", i), ptk], [("ps", SB)])
                        MM(psf(SB)[:, 512:1024], ones[:], pt[:, 0:512], True, False, ["ones", ptk], [("ps", SB)])
                        MM(psf(SB)[:, 512:1024], ones[:], pt[:, 512:1024], False, True, ["ones", ptk], [("ps", SB)])
                        tl, tlk = tmpf.get()
                        hc0 = l * 16 + 4 * g
                        STT(tl[:, 0:512].rearrange("p (r t) -> p r t", r=4), psf(SB)[:, 512:1024].rearrange("p (r t) -> p r t", r=4), 1.0,
                            sinkexp[:, hc0:hc0 + 4].unsqueeze(2).broadcast_to([128, 4, 128]), ALU.mult, ALU.add,
                            [("ps", SB), "sinkexp"], [tlk])
                        ACT(tl[:, 0:512], tl[:, 0:512], AF.Ln, [tlk], [tlk])
                        ACT(tl[:, 0:512], tl[:, 0:512], AF.Exp, [tlk], [tlk], scale=-1.0)
                        TT(buf2[rows, G2 * 4:(G2 + 1) * 4, tsl], psf(SB)[rows, 0:512].rearrange("p (r t) -> p r t", r=4),
                           tl[rows, 0:512].rearrange("p (r t) -> p r t", r=4), ALU.mult,
                           [("ps", SB), tlk], [("ar", 8 + G2 * 4 + r) for r in range(4)])

                    pend = []
                    for i in range(i0, 6):
                        for g in range(4):
                            pend.append(att_a(i, g))
                            if len(pend) > ATT_DEPTH:
                                att_b(*pend.pop(0))
                    while pend:
                        att_b(*pend.pop(0))
                    if last:
                        chk()
                        SB = alloc(pin=True)
                        for g in range(4):
                            G2, e = g // 2, g % 2
                            rows = slice(e * 64, (e + 1) * 64)
                            qg = buf1[rows, G2 * 4:(G2 + 1) * 4, TP:T]
                            SA = alloc()
                            for s in range(NSMP):
                                MM(psf(SA)[:, s * 64:(s + 1) * 64], kcT[rows, s, G2, :], qg, True, True,
                                   ["kcT"] + qkeys[G2 * 4:(G2 + 1) * 4], [("ps", SA)])
                            pt, ptk = pTs.get()
                            ACT(pt[:, :], psf(SA)[:, :], AF.Exp, [("ps", SA)], [ptk], scale=0.125)
                            TT(pt[:, :], pt[:, :], cmb[:, :], ALU.mult, [ptk, "cmb"], [ptk])
                            SN2 = alloc()
                            MM(psf(SN2)[0:NSMP, 0:64], kcur[rows, G2, TP:T], qg, True, True,
                               [("kcur", G2)] + qkeys[G2 * 4:(G2 + 1) * 4], [("ps", SN2)])
                            tn, tnk = tmpf.get()
                            ACT(tn[0:NSMP, 0:64], psf(SN2)[0:NSMP, 0:64], AF.Exp, [("ps", SN2)], [tnk], scale=0.125)
                            TT(pd_sb[:, g * 64:(g + 1) * 64], tn[0:NSMP, 0:64], idm16[:, 0:64], ALU.mult, [tnk, "idm16"], [("pd", g)])
                            for s in range(NSMP):
                                MM(psf(SB)[rows, g * 64:(g + 1) * 64], vc[:, s, g * 64:(g + 1) * 64], pt[:, s * 64:(s + 1) * 64], s == 0, False,
                                   ["vc", ptk], [("ps", SB)])
                            MM(psf(SB)[rows, g * 64:(g + 1) * 64], vcur[0:NSMP, 6, g * 64:(g + 1) * 64], pd_sb[:, g * 64:(g + 1) * 64], False, True,
                               [("vcur", 6), ("pd", g)], [("ps", SB)])
                            for s in range(NSMP):
                                MM(psf(SB)[:, 512 + g * 64:512 + (g + 1) * 64], ones[:], pt[:, s * 64:(s + 1) * 64], s == 0, False,
                                   ["ones", ptk], [("ps", SB)])
                            MM(psf(SB)[:, 512 + g * 64:512 + (g + 1) * 64], ones[0:NSMP, :], pd_sb[:, g * 64:(g + 1) * 64], False, True,
                               ["ones", ("pd", g)], [("ps", SB)])
                        tl, tlk = tmpf.get()
                        for g in range(4):
                            hc0 = l * 16 + 4 * g
                            STT(tl[:, g * 64:(g + 1) * 64].rearrange("p (r s) -> p r s", r=4),
                                psf(SB)[:, 512 + g * 64:512 + (g + 1) * 64].rearrange("p (r s) -> p r s", r=4), 1.0,
                                sinkexp[:, hc0:hc0 + 4].unsqueeze(2).broadcast_to([128, 4, NSMP]), ALU.mult, ALU.add,
                                [("ps", SB), "sinkexp"], [tlk])
                        ACT(tl[:, 0:256], tl[:, 0:256], AF.Ln, [tlk], [tlk])
                        ACT(tl[:, 0:256], tl[:, 0:256], AF.Exp, [tlk], [tlk], scale=-1.0)
                        for g in range(4):
                            G2, e = g // 2, g % 2
                            rows = slice(e * 64, (e + 1) * 64)
                            TT(buf2[rows, G2 * 4:(G2 + 1) * 4, TP:T], psf(SB)[rows, g * 64:(g + 1) * 64].rearrange("p (r s) -> p r s", r=4),
                               tl[rows, g * 64:(g + 1) * 64].rearrange("p (r s) -> p r s", r=4), ALU.mult,
                               [("ps", SB), tlk], [("ar", 8 + G2 * 4 + r) for r in range(4)])
                        unpin(SB)
                    CP(kprev[:, l], kcur[:, :, 640:768], [("kcur", 0), ("kcur", 1)], [("kprev", l)], q="act")
                    CP(vprev[:, l, :], vcur[:, 5, :], [("vcur", 5)], [("vprev", l)], q="act")

                    chk()
                    P.tag = "P8"
                    if KDBG and gi == 0 and l == 0:
                        DMA(dbgb[0], buf2[:], [("ar", 8 + c) for c in range(8)], [], is_out=True)
                    gated_branch(l, glayer, T, "wc", "gc", buf2, lambda kc: [("ar", 8 + kc)], 2)
                    if KDBG and gi == 0 and l == 0:
                        DMA(dbgb[1], buf1[:], [("ar", c) for c in range(8)], [], is_out=True)

                    chk()
                    P.tag = "P9"
                    SN = alloc(pin=True)
                    for c in range(8):
                        u = w_unit("wo", c, glayer)
                        S = alloc()
                        chunk_mm(S, u, lambda kc: buf1[:, kc, :], lambda kc: [("ar", kc)], T)
                        ACT(merged[:, c, CS[0]:T], psf(S)[:, CS[0]:T], AF.Identity, [("ps", S)], [("mg", c)])
                        sq, sk = stats_sq(psf(S)[:, CS[0]:T], [("ps", S)], T)
                        if c > 0:
                            stats_mm(SN, *pend_sq, c - 1, T)
                        pend_sq = (sq, sk)
                    stats_mm(SN, *pend_sq, 7, T)
                    stats_finish(SN, T)
                    post_apply(l, K_POSTMIX, T)

                    chk()
                    P.tag = "P10..12"
                    norm_to_hb(l, K_PREFFN, T)
                    for j in range(22):
                        ug = w_unit("fg", j, glayer)
                        S1 = alloc()
                        chunk_mm(S1, ug, hbr, hbk, T)
                        uu = w_unit("fu", j, glayer)
                        S2 = alloc()
                        chunk_mm(S2, uu, hbr, hbk, T)
                        t, tk = tmpf.get()
                        ACT(t[:, CS[0]:T], psf(S1)[:, CS[0]:T], AF.Silu, [("ps", S1)], [tk])
                        TT(actb[:, j, CS[0]:T], psf(S2)[:, CS[0]:T], t[:, CS[0]:T], ALU.mult, [("ps", S2), tk], [("ar", j)])
                    SN = alloc(pin=True)
                    for c in range(8):
                        k, wkey = w_down(c, glayer)
                        S = alloc()
                        for kc in range(22):
                            lhsT = ring[:, k, kc * 128:(kc + 1) * 128]
                            for (c0, c1) in ((CS[0], 512), (512, T)):
                                MM(psf(S)[:, c0:c1], lhsT, actb[:, kc, c0:c1], kc == 0, kc == 21, [wkey, ("ar", kc)], [("ps", S)])
                        ACT(merged[:, c, CS[0]:T], psf(S)[:, CS[0]:T], AF.Identity, [("ps", S)], [("mg", c)])
                        sq, sk = stats_sq(psf(S)[:, CS[0]:T], [("ps", S)], T)
                        if c > 0:
                            stats_mm(SN, *pend_sq, c - 1, T)
                        pend_sq = (sq, sk)
                    stats_mm(SN, *pend_sq, 7, T)
                    stats_finish(SN, T)
                    post_apply(l, K_POSTFFN, T)

                chk()
                if gi == 0:
                    DMA(yT[:, :, 0:512], xs[:, :, 256:768], [("xs", c) for c in range(8)], [], is_out=True)
                elif gi == 1:
                    DMA(yT[:, :, 512:1280], xs[:, :, 0:768], [("xs", c) for c in range(8)], [], is_out=True)
                else:
                    DMA(yT[:, :, 1280:NYT], xs[:, :, 0:T], [("xs", c) for c in range(8)], [], is_out=True)

        try:
            main_loops()
        except _Stop:
            pass
        P.op("sp", lambda e: e.nop(), extra_deps=list(out_toks))
        stats = P.emit(block, sems, dsems)
    _CACHE['plan'] = P
    return nc, stats


IN_OFF = dict(ua=0, va=1024, bg=2048, cg=3072, hb=4096, q=5120, k=6144, v=6400, ga=6656, gb=7680, gc=8704)


def _unit(W, cols, rows=None):
    Wr = W if rows is None else W[rows]
    blk = Wr[:, cols]
    KC = blk.shape[0] // 128
    return blk.reshape(KC, 128, 128).transpose(1, 0, 2).reshape(128, KC * 128)


def pack_weights(inp):
    out = np.zeros((2, NSLAB, 128, 4096), np.float32)
    ar = np.arange(128)
    qcols = {}
    for c in range(8):
        cols = np.empty(128, np.int64)
        for e in range(2):
            h = head_of(c, e)
            cols[e * 64:(e + 1) * 64] = h * 64 + np.arange(64)
        qcols[c] = cols
    rows_c = np.concatenate([qcols[c] for c in range(8)])
    for l in range(2):
        w_in = np.asarray(inp["w_in"][l])
        mats = dict(wa=np.asarray(inp["w_br_a"][l]), wb=np.asarray(inp["w_br_b"][l]), wc=np.asarray(inp["w_br_c"][l]),
                    wo=np.asarray(inp["w_out"][l]), fg=np.asarray(inp["w_ffn_gate"][l]), fu=np.asarray(inp["w_ffn_up"][l]))
        fd = np.asarray(inp["w_ffn_down"][l])
        units = []
        for (name, idx) in unit_sequence():
            if name in ("ua", "va", "bg", "cg", "hb", "k", "v", "ga", "gb", "gc"):
                units.append(_unit(w_in, IN_OFF[name] + idx * 128 + ar))
            elif name == "q":
                units.append(_unit(w_in, IN_OFF["q"] + qcols[idx]))
            elif name == "wc":
                units.append(_unit(mats["wc"], idx * 128 + ar, rows=rows_c))
            else:
                units.append(_unit(mats[name], idx * 128 + ar))
        for s in range(38):
            out[l, s] = np.concatenate(units[s * 4:(s + 1) * 4], axis=1)
        for c in range(8):
            out[l, 38 + c, :, 0:2816] = _unit(fd, c * 128 + ar)
    return out


def fm(v):
    v = np.asarray(v)
    return np.moveaxis(v.reshape(v.shape[:-1] + (8, 128)), -1, 0)


def prep(inp):
    f32 = np.float32
    x_prompt = np.asarray(inp["x_prompt"], f32)[0]
    x_sample = np.asarray(inp["x_sample"], f32)[:, 0]
    state_conv = np.asarray(inp["state_conv"], f32)
    ck = np.asarray(inp["cache_win_k"], f32)
    cvv = np.asarray(inp["cache_win_v"], f32)
    shared = {}
    shared["wst"] = pack_weights(inp)
    cvec = np.zeros((128, 160), f32)
    for l in range(2):
        vecs = [inp["norm_pre_mix"][l], inp["norm_post_mix"][l], inp["norm_pre_ffn"][l], inp["norm_post_ffn"][l],
                inp["chunk_ln_g"][l], inp["chunk_ln_b"][l], inp["conv_w"][l][0], inp["conv_w"][l][1], inp["conv_w"][l][2],
                inp["conv_b"][l]]
        for kd, v in enumerate(vecs):
            cvec[:, l * 80 + kd * 8:l * 80 + kd * 8 + 8] = np.asarray(v, f32).reshape(8, 128).T
    shared["cvec"] = cvec
    wsp = np.asarray(inp["w_spatial"], f32)
    shared["wsT_in"] = np.ascontiguousarray(wsp.transpose(0, 3, 1, 2))
    bsp = np.asarray(inp["b_spatial"], f32)
    shared["bs_bc"] = np.ascontiguousarray(np.broadcast_to(bsp[:, None], (2, 128, 8, 128)))
    shared["identf_in"] = np.eye(128, dtype=f32)
    shared["tri_in"] = np.triu(np.ones((128, 128), f32))
    rot = np.zeros((128, 128), f32)
    for m in range(128):
        if m % 64 < 32:
            rot[m + 32, m] = -1.0
        else:
            rot[m - 32, m] = 1.0
    shared["rotm_in"] = rot.astype(ml_dtypes.bfloat16)
    sinks = np.asarray(inp["attn_sinks"], f32)
    shared["sinks_bc"] = np.ascontiguousarray(np.broadcast_to(sinks.reshape(1, 32), (128, 32)))
    shared["ws00_bc"] = np.ascontiguousarray(np.broadcast_to(wsp[:, :, 0, 0].reshape(1, 16), (16, 16)))
    idm = np.zeros((16, 4, 4, 16), f32)
    for s in range(16):
        idm[s, :, :, s] = 1.0
    shared["idm16_in"] = idm.reshape(16, 256)
    cm = np.zeros((128, 16, 4, 16), f32)
    for s in range(16):
        cm[:, s, :, s] = 1.0
    shared["cm_in"] = cm.reshape(128, 1024)
    kk = np.arange(128)[:, None]
    qq = np.arange(128)[None, :]
    m_cur = np.where(kk <= qq, 0.0, MASKNEG).astype(f32)
    m_prev = np.where(kk >= qq, 0.0, MASKNEG).astype(f32)
    m_none = np.full((128, 128), MASKNEG, f32)
    inv = np.power(np.float32(10000.0), -np.arange(32, dtype=f32) * np.float32(2.0 / 64)).astype(f32)
    invp = inv[(np.arange(128) % 64) % 32]

    xpad = np.concatenate([np.zeros((256, D), f32), x_prompt], axis=0)
    per_core = []
    for core in range(NCORE):
        d = {}
        xp = xpad[core * 2048:core * 2048 + 2304]
        xsmp = x_sample[core * NSMP:(core + 1) * NSMP]
        xa = np.concatenate([xp, xsmp], axis=0)
        d["xT"] = np.ascontiguousarray(xa.reshape(NT, 8, 128).transpose(2, 1, 0))
        pos = np.concatenate([np.arange(core * 2048 - 256, core * 2048 + 2048), np.full(NSMP, PAST)]).astype(np.int32)
        ang = pos.astype(f32)[None, :] * invp[:, None]
        d["cosT"] = np.cos(ang).astype(f32)
        d["sinT"] = np.sin(ang).astype(f32)
        mp0 = m_none if core == 0 else m_prev
        def rep4(m):
            return np.broadcast_to(m[:, None, :], (128, 4, 128)).reshape(128, 512)
        d["maskb_in"] = np.ascontiguousarray(np.stack([np.concatenate([rep4(m_prev), rep4(m_cur)], axis=1),
                                                       np.concatenate([rep4(mp0), rep4(m_cur)], axis=1)], axis=1))
        sl = slice(core * NSMP, (core + 1) * NSMP)
        sc = state_conv[:, sl]
        d["sc_nat"] = np.ascontiguousarray(sc)
        d["histT"] = np.ascontiguousarray(sc.reshape(2, NSMP, 2, 8, 128).transpose(0, 4, 2, 3, 1))
        ckc = ck[:, sl]
        d["ck_nat"] = np.ascontiguousarray(ckc.reshape(2, NSMP, 128, 256))
        d["cv_nat"] = np.ascontiguousarray(cvv[:, sl].reshape(2, NSMP, 128, 256))
        t = ckc.reshape(2, NSMP, 128, 2, 2, 64)
        d["kcT_in"] = np.ascontiguousarray(t.transpose(0, 4, 5, 1, 3, 2).reshape(2, 128, NSMP, 2, 128))
        d.update(shared)
        per_core.append(d)
    return per_core


def kernel(**inputs):
    if "nc" not in _CACHE:
        _CACHE["nc"] = build_program()
    nc, _ = _CACHE["nc"]
    in_maps = prep(inputs)
    res = run_bass_kernel_spmd(nc, in_maps, core_ids=list(range(NCORE)))
    R = res.results
    f32 = np.float32
    y_prompt = np.zeros((1, 16384, D), f32)
    y_sample = np.zeros((128, 1, D), f32)
    for core in range(NCORE):
        yT = R[core]["yT"]
        ya = yT.transpose(2, 1, 0).reshape(NYT, D)
        y_prompt[0, core * 2048:(core + 1) * 2048] = ya[:2048]
        y_sample[core * NSMP:(core + 1) * NSMP, 0] = ya[2048:]
    last = R[NCORE - 1]
    prompt_conv = np.zeros((2, 1, 2, D), f32)
    prompt_k = np.zeros((2, 1, 128, 4, 64), f32)
    prompt_v = np.zeros((2, 1, 128, 4, 64), f32)
    for l in range(2):
        prompt_conv[l, 0] = last["pconvT"][l].transpose(2, 1, 0).reshape(2, D)
        t = last["pkT"][l].reshape(2, 64, 2, 128)
        prompt_k[l, 0] = t.transpose(3, 2, 0, 1).reshape(128, 4, 64)
        prompt_v[l, 0] = last["pv"][l].reshape(128, 4, 64)
    sample_conv = np.zeros((2, 128, 2, D), f32)
    sample_k = np.zeros((2, 128, 128, 4, 64), f32)
    sample_v = np.zeros((2, 128, 128, 4, 64), f32)
    sample_cv = np.zeros((2, 128, 1, D), f32)
    for core in range(NCORE):
        r = R[core]
        sl = slice(core * NSMP, (core + 1) * NSMP)
        for l in range(2):
            sample_conv[l, sl, 0] = r["sconv_old"][l]
            sample_conv[l, sl, 1] = r["sconv_newT"][l].transpose(2, 1, 0).reshape(NSMP, D)
            sample_k[l, sl, 0:127] = r["sk_shift"][l].reshape(NSMP, 127, 4, 64)
            t = r["sk_newT"][l].reshape(2, 64, 2, NSMP)
            sample_k[l, sl, 127] = t.transpose(3, 2, 0, 1).reshape(NSMP, 4, 64)
            sample_v[l, sl, 0:127] = r["sv_shift"][l].reshape(NSMP, 127, 4, 64)
            sample_v[l, sl, 127] = r["sv_new"][l].reshape(NSMP, 4, 64)
            sample_cv[l, sl, 0] = r["scvT"][l].transpose(2, 1, 0).reshape(NSMP, D)
    return (y_prompt, y_sample, prompt_conv, prompt_k, prompt_v, sample_conv, sample_k, sample_v, sample_cv)
```

```python
import numpy as np
import ml_dtypes
from contextlib import ExitStack
import concourse.bass as bass
import concourse.mybir as mybir
from concourse.bass_utils import run_bass_kernel_spmd

F32 = mybir.dt.float32
BF16 = mybir.dt.bfloat16
AF = mybir.ActivationFunctionType
ALU = mybir.AluOpType

NCORE = 8
D = 1024
TP = 768
NGRP = 3
NSMP = 16
TMAX = TP + NSMP
NT = NGRP * TP + NSMP
NYT = 2048 + NSMP
NSLAB = 46
NB = 3
ATT_DEPTH = 2
PAST = 16384
EPS_RMS = 1e-6
EPS_LN = 1e-5
MASKNEG = -30000.0

K_PREMIX, K_POSTMIX, K_PREFFN, K_POSTFFN, K_LNG, K_LNB, K_CW0, K_CW1, K_CW2, K_CB = range(10)

QUEUES = ("pe", "act", "dve", "pool", "sp")
NDMASEM = 24
EPOCH = 400
NEPOCH = 8
KLIMIT = 9999
KGROUPS = ""
KSKIP = ""
KDBG = False


_CACHE = {}


class _Stop(Exception):
    pass


class Plan:
    def __init__(self):
        self.ops = {q: [] for q in QUEUES}
        self.state = {}
        self.dma_uses = [0] * NDMASEM
        self.tag = "setup"
        self.dma_rr = {"sp": 0, "pool": 0, "act": 0}
        self.dma_rng = {"sp": (0, 14), "pool": (14, 8), "act": (22, 2)}

    def _st(self, k):
        s = self.state.get(k)
        if s is None:
            s = [None, {}]
            self.state[k] = s
        return s

    def op(self, q, fn, reads=(), writes=(), dma=False, extra_deps=()):
        deps = []
        for k in reads:
            s = self._st(k)
            if s[0] is not None:
                deps.append(s[0])
        for k in writes:
            s = self._st(k)
            if s[0] is not None:
                deps.append(s[0])
            deps.extend(s[1].values())
        deps.extend(extra_deps)
        idx = len(self.ops[q])
        rec = dict(fn=fn, deps=deps, signal=False, dma=None, tag=self.tag)
        if dma:
            base, n = self.dma_rng[q]
            si = base + self.dma_rr[q] % n
            self.dma_rr[q] += 1
            prev = self.dma_uses[si]
            self.dma_uses[si] += 1
            if prev > 0:
                rec["deps"].append(("d", si, 16 * prev))
            rec["dma"] = si
            tok = ("d", si, 16 * (prev + 1))
        else:
            tok = ("e", q, idx)
        self.ops[q].append(rec)
        for k in reads:
            s = self._st(k)
            tk = tok[:2]
            old = s[1].get(tk)
            if old is None or old[2] < tok[2]:
                s[1][tk] = tok
        for k in writes:
            s = self._st(k)
            s[0] = tok
            s[1] = {}
        return tok

    def emit(self, block, sems, dsems):
        for q in QUEUES:
            for rec in self.ops[q]:
                for d in rec["deps"]:
                    if d[0] == "e":
                        if d[1] == q and q in ("pe", "sp"):
                            continue
                        self.ops[d[1]][d[2]]["signal"] = True
        cnt = {}
        for q in QUEUES:
            c = 0
            arr = []
            for rec in self.ops[q]:
                if rec["signal"]:
                    c += 1
                arr.append(c)
            cnt[q] = arr
        engs = dict(pe=block.tensor, act=block.scalar, dve=block.vector, pool=block.gpsimd, sp=block.sync)
        stats = {}
        for q in QUEUES:
            ops = self.ops[q]
            if not ops:
                continue

            def body(eng, q=q, ops=ops):
                waited = {}
                nw = 0
                mycnt = [0]
                for rec in ops:
                    need = {}
                    for d in rec["deps"]:
                        if d[0] == "e":
                            if d[1] == q and q in ("pe", "sp"):
                                continue
                            key = ("e", d[1])
                            val = cnt[d[1]][d[2]]
                        else:
                            key = ("d", d[1])
                            val = d[2]
                        if waited.get(key, 0) < val and need.get(key, 0) < val:
                            need[key] = val
                    for key, val in need.items():
                        if key[0] == "e":
                            eng.wait_ge(sems[key[1]][(val - 1) // EPOCH], (val - 1) % EPOCH + 1)
                        else:
                            eng.wait_ge(dsems[key[1]], val)
                        waited[key] = val
                        nw += 1
                    ins = rec["fn"](eng)
                    if rec["dma"] is not None:
                        ins.then_inc(dsems[rec["dma"]], 16)
                    elif rec["signal"]:
                        mycnt[0] += 1
                        ins.then_inc(sems[q][(mycnt[0] - 1) // EPOCH], 1)
                stats[q] = (len(ops), nw)

            engs[q](body)
        return stats


class Rot:
    def __init__(self, name, aps):
        self.name, self.aps, self.i = name, aps, 0

    def get(self):
        k = self.i % len(self.aps)
        self.i += 1
        return self.aps[k], (self.name, k)


def unit_sequence():
    seq = []
    seq += [("va", c) for c in range(8)]
    seq += [("ua", c) for c in range(8)]
    for c in range(8):
        seq += [("cg", c), ("hb", c), ("bg", c)]
    for c in range(8):
        seq += [("wa", c), ("ga", c)]
    seq += [("q", c) for c in range(8)]
    seq += [("k", 0), ("k", 1), ("v", 0), ("v", 1)]
    for c in range(8):
        seq += [("wb", c), ("gb", c)]
    for c in range(8):
        seq += [("wc", c), ("gc", c)]
    seq += [("wo", c) for c in range(8)]
    for j in range(22):
        seq += [("fg", j), ("fu", j)]
    return seq


def head_of(c, e):
    G2, r = c // 4, c % 4
    return 4 * (2 * G2 + e) + r


def build_program():
    nc = bass.Bass("TRN2", target_bir_lowering=False)

    def din(name, shape, dt=F32):
        return nc.dram_tensor(name, list(shape), dt, kind="ExternalInput").ap()

    def dout(name, shape, dt=F32):
        return nc.dram_tensor(name, list(shape), dt, kind="ExternalOutput").ap()

    xT = din("xT", [128, 8, NT])
    cosT = din("cosT", [128, NT])
    sinT = din("sinT", [128, NT])
    wst = din("wst", [2, NSLAB, 128, 4096])
    cvec = din("cvec", [128, 160])
    wsT_in = din("wsT_in", [2, 128, 8, 128])
    bs_bc = din("bs_bc", [2, 128, 8, 128])
    maskb_in = din("maskb_in", [128, 2, 1024])
    identf_in = din("identf_in", [128, 128])
    tri_in = din("tri_in", [128, 128])
    rotm_in = din("rotm_in", [128, 128], BF16)
    sinks_bc = din("sinks_bc", [128, 32])
    ws00_bc = din("ws00_bc", [16, 16])
    idm16_in = din("idm16_in", [16, 256])
    cm_in = din("cm_in", [128, 1024])
    histT = din("histT", [2, 128, 2, 8, NSMP])
    kcT_in = din("kcT_in", [2, 128, NSMP, 2, 128])
    ck_nat = din("ck_nat", [2, NSMP, 128, 256])
    cv_nat = din("cv_nat", [2, NSMP, 128, 256])
    sc_nat = din("sc_nat", [2, NSMP, 2, 1024])

    yT = dout("yT", [128, 8, NYT])
    pconvT = dout("pconvT", [2, 128, 8, 2])
    pkT = dout("pkT", [2, 128, 2, 128])
    pv = dout("pv", [2, 128, 256])
    sconv_newT = dout("sconv_newT", [2, 128, 8, NSMP])
    sconv_old = dout("sconv_old", [2, NSMP, 1024])
    sk_shift = dout("sk_shift", [2, NSMP, 127, 256])
    sk_newT = dout("sk_newT", [2, 128, 2, NSMP])
    sv_shift = dout("sv_shift", [2, NSMP, 127, 256])
    sv_new = dout("sv_new", [2, NSMP, 256])
    scvT = dout("scvT", [2, 128, 8, NSMP])
    if KDBG:
        dbgf = dout("dbgf", [3, 128, 8, TMAX])
        dbgb = dout("dbgb", [2, 128, 8, TMAX], BF16)

    P = Plan()
    out_toks = []

    with ExitStack() as es:
        def sb(name, shape, dt):
            return es.enter_context(nc.sbuf_tensor(name, list(shape), dt))

        xs = sb("xs", [128, 8, TMAX], F32)
        hb = sb("hb", [128, 8, TMAX], BF16)
        arena = sb("arena", [128, 22 * TMAX], BF16)
        merged = sb("merged", [128, 8, TMAX], F32)
        kcur = sb("kcur", [128, 2, TMAX], BF16)
        kprev = sb("kprev", [128, 2, 2, 128], BF16)
        vcur = sb("vcur", [128, 7, 256], BF16)
        vprev = sb("vprev", [128, 2, 256], BF16)
        kTf = sb("kTf", [128, 2, 144], F32)
        cxe_t = sb("cxe", [128, 2, 2 + TMAX], F32)
        cxh = sb("cxh", [128, 2, 8, 2], F32)
        scx = sb("scx", [128, 8, NSMP], F32)
        scv_sb = sb("scv_sb", [128, 8, NSMP], F32)
        cos_sb = sb("cos_sb", [128, TMAX], F32)
        sin_sb = sb("sin_sb", [128, TMAX], F32)
        rstd = sb("rstd", [128, TMAX], F32)
        sq_t = sb("sq_t", [128, 3, TMAX], BF16)
        tmpf_t = sb("tmpf_t", [128, 3, 1024], F32)
        pT_t = sb("pT_t", [128, ATT_DEPTH + 1, 1024], BF16)
        ring = sb("ring", [128, NB, 4096], BF16)
        cv_sb = sb("cv_sb", [128, 160], F32)
        wsTb = sb("wsTb", [128, 2, 8, 128], BF16)
        bias2 = sb("bias2", [128, 2, 8, 128], F32)
        maskb = sb("maskb", [128, 2, 1024], BF16)
        identf = sb("identf", [128, 128], F32)
        identb = sb("identb", [128, 128], BF16)
        tri = sb("tri", [128, 128], F32)
        ones = sb("ones", [128, 128], BF16)
        rotm = sb("rotm", [128, 128], BF16)
        sinkexp = sb("sinkexp", [128, 32], F32)
        ws00 = sb("ws00", [16, 16], F32)
        wsd = sb("wsd", [16, 2, 8, 16], BF16)
        idm16 = sb("idm16", [16, 256], F32)
        cst = sb("cst", [128, 4], F32)
        hist_sb = sb("hist_sb", [128, 2, 8, NSMP], F32)
        kcT = sb("kcT", [128, NSMP, 2, 128], BF16)
        vc = sb("vc", [128, NSMP, 256], BF16)
        lnst = sb("lnst", [128, 7, 2, 6], F32)
        lnmv = sb("lnmv", [128, 7, 2], F32)
        lnr = sb("lnr", [128, 7], F32)
        pd_sb = sb("pd_sb", [16, 256], BF16)
        cmb = sb("cmb", [128, 1024], BF16)

        ps = es.enter_context(nc.psum_tensor("ps", [128, 4096], F32))
        sems = {q: [es.enter_context(nc.semaphore("s_%s_%d" % (q, i))) for i in range(NEPOCH)] for q in QUEUES}
        dsems = [es.enter_context(nc.semaphore("d%d" % i)) for i in range(NDMASEM)]
        block = es.enter_context(nc.Block())

        buf1 = arena[:, 0:8 * TMAX].rearrange("p (c t) -> p c t", c=8)
        B2OFF = 8 * TMAX
        buf2 = arena[:, B2OFF:B2OFF + 8 * TMAX].rearrange("p (c t) -> p c t", c=8)
        vatm = arena[:, B2OFF:B2OFF + 7 * 1024].rearrange("p (i f) -> p i f", i=7)
        actb = arena[:, :].rearrange("p (c t) -> p c t", c=22)

        def arkeys(start, end):
            return [("ar", k) for k in range(start // TMAX, (end - 1) // TMAX + 1)]

        def vatm_keys(i):
            return arkeys(B2OFF + i * 1024, B2OFF + (i + 1) * 1024)

        tmpf = Rot("tmpf", [tmpf_t[:, i, :] for i in range(3)])
        sqb = Rot("sq", [sq_t[:, i, :] for i in range(3)])
        qbs = sqb
        pTs = Rot("pT", [pT_t[:, i, :] for i in range(ATT_DEPTH + 1)])
        cxes = Rot("cxe", [cxe_t[:, i, :] for i in range(2)])

        def psf(S):
            return ps[:, S * 1024:(S + 1) * 1024]

        CS = [0]
        slot_state = dict(rr=0, pinned=set())

        def alloc(pin=False):
            while True:
                s = slot_state["rr"] % 4
                slot_state["rr"] += 1
                if s not in slot_state["pinned"]:
                    break
            if pin:
                slot_state["pinned"].add(s)
            return s

        def unpin(s):
            slot_state["pinned"].discard(s)

        def cv(l, kind, c):
            i = l * 80 + kind * 8 + c
            return cv_sb[:, i:i + 1]

        def MM(out, lhsT, rhs, start, stop, reads, writes):
            P.op("pe", lambda e: e.matmul(out, lhsT=lhsT, rhs=rhs, start=start, stop=stop), reads, writes)

        def ACT(out, in_, func, reads, writes, scale=1.0, bias=None):
            if bias is None:
                P.op("act", lambda e: e.activation(out=out, in_=in_, func=func, scale=scale), reads, writes)
            else:
                P.op("act", lambda e: e.activation(out=out, in_=in_, func=func, scale=scale, bias=bias), reads, writes)

        def TT(out, in0, in1, op, reads, writes, q="dve"):
            if op == ALU.add and "ttadd" not in KSKIP:
                P.op(q, lambda e: e.scalar_tensor_tensor(out=out, in0=in0, scalar=1.0, in1=in1, op0=ALU.mult, op1=ALU.add), reads, writes)
            else:
                P.op(q, lambda e: e.tensor_tensor(out=out, in0=in0, in1=in1, op=op), reads, writes)

        def STT(out, in0, scalar, in1, op0, op1, reads, writes):
            P.op("dve", lambda e: e.scalar_tensor_tensor(out=out, in0=in0, scalar=scalar, in1=in1, op0=op0, op1=op1), reads, writes)

        def TS(out, in0, s1, s2, op0, op1, reads, writes, q="dve"):
            if s2 is None:
                P.op(q, lambda e: e.tensor_scalar(out=out, in0=in0, scalar1=s1, scalar2=None, op0=op0), reads, writes)
            else:
                P.op(q, lambda e: e.tensor_scalar(out=out, in0=in0, scalar1=s1, scalar2=s2, op0=op0, op1=op1), reads, writes)

        def CP(out, in_, reads, writes, q="dve"):
            if q == "act":
                P.op(q, lambda e: e.activation(out=out, in_=in_, func=AF.Identity), reads, writes)
            else:
                P.op(q, lambda e: e.tensor_copy(out=out, in_=in_), reads, writes)

        def DMA(out, in_, reads, writes, q="sp", is_out=False, **kw):
            t = P.op(q, lambda e: e.dma_start(out=out, in_=in_, **kw), reads, writes, dma=True)
            if is_out:
                out_toks.append(t)
            return t

        useq = unit_sequence()
        total_slabs = NGRP * 2 * NSLAB
        ws = dict(next_plan=0, oldest=0, upos=0, hold=False)

        def slab_coords(n):
            gl, s = divmod(n, NSLAB)
            return gl % 2, s

        def w_ensure():
            while ws["next_plan"] < min(ws["oldest"] + NB, total_slabs):
                n = ws["next_plan"]
                l, s = slab_coords(n)
                k = n % NB
                ncol = 4096 if s < 38 else 2816
                DMA(ring[:, k, 0:ncol], wst[l, s, :, 0:ncol], [], [("ring", k)], q="pool", max_dma_last_dim=8192)
                ws["next_plan"] += 1

        def w_unit(name, idx, glayer):
            u = ws["upos"]
            assert useq[u] == (name, idx), (useq[u], name, idx)
            ws["upos"] += 1
            n = glayer * NSLAB + u // 4
            if not ws["hold"] and n > ws["oldest"]:
                ws["oldest"] = n
            w_ensure()
            return n % NB, u % 4, ("ring", n % NB)

        def w_down(c, glayer):
            n = glayer * NSLAB + 38 + c
            if n > ws["oldest"]:
                ws["oldest"] = n
            w_ensure()
            return n % NB, ("ring", n % NB)

        def lhs_unit(uinfo, kc):
            k, j, _ = uinfo
            o = j * 1024 + kc * 128
            return ring[:, k, o:o + 128]

        def chunk_mm(S, uinfo, rhs_fn, rkey_fn, T):
            wkey = uinfo[2]
            for kc in range(8):
                lhsT = lhs_unit(uinfo, kc)
                r = rhs_fn(kc)
                for (c0, c1) in ((CS[0], 512), (512, T)):
                    MM(psf(S)[:, c0:c1], lhsT, r[:, c0:c1], kc == 0, kc == 7, [wkey] + rkey_fn(kc), [("ps", S)])

        DMA(cv_sb[:], cvec, [], ["cv"])
        DMA(identf[:], identf_in, [], ["identf"])
        DMA(tri[:], tri_in, [], ["tri"])
        DMA(rotm[:], rotm_in, [], ["rotm"])
        DMA(sinkexp[:], sinks_bc, [], ["sinkexp"])
        DMA(ws00[:], ws00_bc, [], ["ws00"])
        DMA(idm16[:], idm16_in, [], ["idm16"])
        DMA(bias2[:, 0], bs_bc[0], [], [("bias2", 0)])
        DMA(bias2[:, 1], bs_bc[1], [], [("bias2", 1)])
        DMA(maskb[:], maskb_in, [], ["maskb"], q="pool", max_dma_last_dim=8192)
        DMA(cmb[:], cm_in, [], ["cmb"], q="pool", max_dma_last_dim=8192)
        P.op("dve", lambda e: e.memset(ones[:], 1.0), [], ["ones"])
        P.op("dve", lambda e: e.memset(kprev[:], 0.0), [], [("kprev", 0), ("kprev", 1)])
        P.op("dve", lambda e: e.memset(vprev[:], 0.0), [], [("vprev", 0), ("vprev", 1)])
        P.op("dve", lambda e: e.memset(cxh[:], 0.0), [], [("cxh", 0), ("cxh", 1)])
        P.op("dve", lambda e: e.memset(cst[:, 0:1], EPS_RMS), [], ["cst"])
        P.op("dve", lambda e: e.memset(cst[:, 1:2], EPS_LN), [], ["cst"])
        CP(identb[:], identf[:], ["identf"], ["identb"])
        ACT(sinkexp[:], sinkexp[:], AF.Exp, ["sinkexp"], ["sinkexp"])
        for l in range(2):
            tw, twk = tmpf.get()
            twv = tw.rearrange("p (g t) -> p g t", g=8)
            DMA(twv, wsT_in[l], [], [twk])
            TT(wsTb[:, l], twv, tri[:].unsqueeze(1).broadcast_to([128, 8, 128]), ALU.mult, [twk, "tri"], [("wsTb", l)])
            S = alloc()
            for g in range(8):
                MM(psf(S)[:, g * 128:(g + 1) * 128], ones[:], wsTb[:, l, g, :], True, True, ["ones", ("wsTb", l)], [("ps", S)])
            for g in range(8):
                STT(bias2[:, l, g, :], psf(S)[:, g * 128:(g + 1) * 128], cv(l, K_LNB, g), bias2[:, l, g, :], ALU.mult, ALU.add,
                    [("ps", S), "cv", ("bias2", l)], [("bias2", l)])
            for g in range(8):
                TS(wsd[:, l, g, :], identf[0:16, 0:16], ws00[:, l * 8 + g:l * 8 + g + 1], None, ALU.mult, None, ["identf", "ws00"], ["wsd"])

        def stats_finish(SN, T):
            ACT(rstd[:, CS[0]:T], psf(SN)[:, CS[0]:T], AF.Ln, [("ps", SN), "cst"], ["rstd"], scale=1.0 / D, bias=cst[:, 0:1])
            ACT(rstd[:, CS[0]:T], rstd[:, CS[0]:T], AF.Exp, ["rstd"], ["rstd"], scale=-0.5)
            unpin(SN)

        def stats_sq(src, skeys, T):
            sq, sk = sqb.get()
            ACT(sq[:, CS[0]:T], src, AF.Square, skeys, [sk])
            return sq, sk

        def stats_mm(SN, sq, sk, c, T):
            for (c0, c1) in ((CS[0], 512), (512, T)):
                MM(psf(SN)[:, c0:c1], ones[:], sq[:, c0:c1], c == 0, c == 7, [sk, "ones"], [("ps", SN)])

        def stats_add(SN, src, skeys, c, T):
            sq, sk = stats_sq(src, skeys, T)
            stats_mm(SN, sq, sk, c, T)

        def norm_to_hb(l, kind, T):
            SN = alloc(pin=True)
            for c in range(8):
                stats_add(SN, xs[:, c, CS[0]:T], [("xs", c)], c, T)
            stats_finish(SN, T)
            for c in range(8):
                STT(hb[:, c, CS[0]:T], xs[:, c, CS[0]:T], cv(l, kind, c), rstd[:, CS[0]:T], ALU.mult, ALU.mult,
                    [("xs", c), "cv", "rstd"], [("hb", c)])

        def post_apply(l, kind, T):
            c0 = CS[0]
            mkeys = [("mg", c) for c in range(8)]
            TT(merged[:, :, c0:T], merged[:, :, c0:T], rstd[:, c0:T].unsqueeze(1).broadcast_to([128, 8, T - c0]), ALU.mult,
               mkeys + ["rstd"], mkeys)
            for c in range(8):
                STT(xs[:, c, c0:T], merged[:, c, c0:T], cv(l, kind, c), xs[:, c, c0:T], ALU.mult, ALU.add,
                    [("mg", c), "cv", ("xs", c)], [("xs", c)])

        hbr = lambda kc: hb[:, kc, :]
        hbk = lambda kc: [("hb", kc)]

        def gated_branch(l, glayer, T, wname, gname, src, srckeys, mode):
            for c in range(8):
                uw = w_unit(wname, c, glayer)
                S = alloc()
                chunk_mm(S, uw, lambda kc: src[:, kc, :], lambda kc: srckeys(kc), T)
                ug = w_unit(gname, c, glayer)
                S2 = alloc()
                chunk_mm(S2, ug, hbr, hbk, T)
                tg, tgk = tmpf.get()
                ACT(tg[:, CS[0]:T], psf(S2)[:, CS[0]:T], AF.Sigmoid, [("ps", S2)], [tgk])
                if mode == 0:
                    TT(merged[:, c, CS[0]:T], psf(S)[:, CS[0]:T], tg[:, CS[0]:T], ALU.mult, [("ps", S), tgk], [("mg", c)])
                else:
                    TT(tg[:, CS[0]:T], psf(S)[:, CS[0]:T], tg[:, CS[0]:T], ALU.mult, [("ps", S), tgk], [tgk])
                    if mode == 1:
                        TT(merged[:, c, CS[0]:T], merged[:, c, CS[0]:T], tg[:, CS[0]:T], ALU.add, [("mg", c), tgk], [("mg", c)])
                    else:
                        TT(buf1[:, c, CS[0]:T], merged[:, c, CS[0]:T], tg[:, CS[0]:T], ALU.add, [("mg", c), tgk], [("ar", c)])

        phase = [0]

        def chk():
            phase[0] += 1
            if phase[0] > KLIMIT:
                raise _Stop()

        def main_loops():
          for gi in range(NGRP):
            if KGROUPS and str(gi) not in KGROUPS:
                continue
            if KGROUPS and ws["next_plan"] < gi * 2 * NSLAB:
                ws["next_plan"] = ws["oldest"] = gi * 2 * NSLAB
            if True:
                T = TP + (NSMP if gi == NGRP - 1 else 0)
                last = gi == NGRP - 1
                ntile = 7 if last else 6
                col0 = gi * TP
                for c in range(8):
                    DMA(xs[:, c, :T], xT[:, c, col0:col0 + T], [], [("xs", c)])
                DMA(cos_sb[:, :T], cosT[:, col0:col0 + T], [], ["cos"])
                DMA(sin_sb[:, :T], sinT[:, col0:col0 + T], [], ["sin"])

                for l in range(2):
                    glayer = gi * 2 + l
                    ws["upos"] = 0
                    ck = (0, 128)[l] if gi == 0 else 0
                    cs = (128, 256)[l] if gi == 0 else 0
                    i0 = cs // 128
                    i0k = ck // 128
                    if last:
                        DMA(hist_sb[:], histT[l], [], ["hist"])
                        DMA(kcT[:], kcT_in[l], [], ["kcT"], q="pool", max_dma_last_dim=8192)
                        DMA(vc[:], cv_nat[l].rearrange("s k f -> k s f"), [], ["vc"], q="pool", max_dma_last_dim=8192)
                        if gi == NGRP - 1:
                            DMA(sk_shift[l], ck_nat[l, :, 1:128, :], [], [], is_out=True)
                            DMA(sv_shift[l], cv_nat[l, :, 1:128, :], [], [], is_out=True)
                            DMA(sconv_old[l], sc_nat[l, :, 1, :], [], [], is_out=True)

                    chk()
                    P.tag = "P1"
                    CS[0] = ck
                    norm_to_hb(l, K_PREMIX, T)
                    CS[0] = cs

                    chk()
                    P.tag = "P2"
                    ws["hold"] = True
                    uva = [w_unit("va", c, glayer) for c in range(8)]
                    for i in range(i0, ntile):
                        M = 128 if i < 6 else NSMP
                        tc0 = i * 128
                        S = alloc()
                        for h2 in range(2):
                            u0 = uva[h2 * 4]
                            slab = ring[:, u0[0], :].rearrange("p (u k c) -> p u k c", u=4, k=8)
                            for kc in range(8):
                                MM(psf(S)[0:M, h2 * 512:(h2 + 1) * 512], hb[:, kc, tc0:tc0 + M], slab[:, :, kc, :], kc == 0, kc == 7,
                                   [u0[2], ("hb", kc)], [("ps", S)])
                        tg, tgk = tmpf.get()
                        ACT(tg[0:M, :], psf(S)[0:M, :], AF.Gelu_apprx_tanh, [("ps", S)], [tgk])
                        for h2 in range(2):
                            P.op("dve", lambda e, tg=tg, M=M, i=i, h2=h2: e.bn_stats(out=lnst[0:M, i, h2, :], in_=tg[0:M, h2 * 512:(h2 + 1) * 512]),
                                 [tgk], [("lnst", i)])
                        P.op("dve", lambda e, M=M, i=i: e.bn_aggr(out=lnmv[0:M, i, :], in_=lnst[0:M, i, :, :].rearrange("p a b -> p (a b)")),
                             [("lnst", i)], [("lnmv", i)])
                        CP(vatm[0:M, i, :], tg[0:M, :], [tgk], vatm_keys(i))
                        if i == 6:
                            tlast, tlastk = tg, tgk
                    ws["hold"] = False
                    nl = ntile
                    ACT(lnr[:, i0:nl], lnmv[:, i0:nl, 1], AF.Ln, [("lnmv", i) for i in range(i0, nl)] + ["cst"], ["lnr"], bias=cst[:, 1:2])
                    ACT(lnr[:, i0:nl], lnr[:, i0:nl], AF.Exp, ["lnr"], ["lnr"], scale=-0.5)
                    for i in range(i0, ntile):
                        M = 128 if i < 6 else NSMP
                        TS(vatm[0:M, i, :], vatm[0:M, i, :], lnmv[0:M, i, 0:1], lnr[0:M, i:i + 1], ALU.subtract, ALU.mult,
                           vatm_keys(i) + [("lnmv", i), "lnr"], vatm_keys(i))
                    if last:
                        TS(tlast[0:NSMP, :], tlast[0:NSMP, :], lnmv[0:NSMP, 6, 0:1], lnr[0:NSMP, 6:7], ALU.subtract, ALU.mult,
                           [tlastk, ("lnmv", 6), "lnr"], [tlastk])
                        S = alloc()
                        for c in range(8):
                            P.op("pe", lambda e, S=S, c=c, tlast=tlast: e.transpose(psf(S)[:, c * NSMP:(c + 1) * NSMP], tlast[0:NSMP, c * 128:(c + 1) * 128], identf[0:NSMP, 0:NSMP]),
                                 [tlastk, "identf"], [("ps", S)])
                        for c in range(8):
                            ACT(scv_sb[:, c, :], psf(S)[:, c * NSMP:(c + 1) * NSMP], AF.Identity, [("ps", S), "cv"], ["scv"],
                                scale=cv(l, K_LNG, c), bias=cv(l, K_LNB, c))
                        DMA(scvT[l], scv_sb[:], ["scv"], [], is_out=True)

                    def spatial_g(g):
                        S = alloc()
                        for i in range(i0, 6):
                            MM(psf(S)[:, i * 128:(i + 1) * 128], vatm[:, i, g * 128:(g + 1) * 128], wsTb[:, l, g, :], True, True,
                               vatm_keys(i) + [("wsTb", l)], [("ps", S)])
                        if last:
                            MM(psf(S)[:, TP:T], vatm[0:NSMP, 6, g * 128:(g + 1) * 128], wsd[:, l, g, :], True, True,
                               vatm_keys(6) + ["wsd"], [("ps", S)])
                        t, tk = tmpf.get()
                        STT(t[:, cs:TP].rearrange("p (i t) -> p i t", i=6 - i0), psf(S)[:, cs:TP].rearrange("p (i t) -> p i t", i=6 - i0), cv(l, K_LNG, g),
                            bias2[:, l, g, :].unsqueeze(1).broadcast_to([128, 6 - i0, 128]), ALU.mult, ALU.add,
                            [("ps", S), "cv", ("bias2", l)], [tk])
                        if last:
                            STT(t[:, TP:T], psf(S)[:, TP:T], cv(l, K_LNG, g), bias2[:, l, g, 0:1].broadcast_to([128, NSMP]), ALU.mult, ALU.add,
                                [("ps", S), "cv", ("bias2", l)], [tk])
                        TT(buf1[:, g, CS[0]:T], buf1[:, g, CS[0]:T], t[:, CS[0]:T], ALU.mult, [("ar", g), tk], [("ar", g)])

                    for c in range(8):
                        u = w_unit("ua", c, glayer)
                        S = alloc()
                        chunk_mm(S, u, hbr, hbk, T)
                        ACT(buf1[:, c, CS[0]:T], psf(S)[:, CS[0]:T], AF.Gelu_apprx_tanh, [("ps", S)], [("ar", c)])
                        if c >= 1:
                            spatial_g(c - 1)
                    spatial_g(7)
                    chk()
                    P.tag = "spatial"
                    chk()
                    P.tag = "P3"
                    for c in range(8):
                        CS[0] = ck
                        uc = w_unit("cg", c, glayer)
                        Sc = alloc()
                        chunk_mm(Sc, uc, hbr, hbk, T)
                        uh = w_unit("hb", c, glayer)
                        Sh = alloc()
                        chunk_mm(Sh, uh, hbr, hbk, T)
                        CS[0] = cs
                        ub = w_unit("bg", c, glayer)
                        Sb = alloc()
                        chunk_mm(Sb, ub, hbr, hbk, T)
                        CS[0] = ck
                        th, thk = tmpf.get()
                        ACT(th[:, CS[0]:T], psf(Sh)[:, CS[0]:T], AF.Identity, [("ps", Sh)], [thk])
                        cxe, cxk = cxes.get()
                        CP(cxe[:, 0:2], cxh[:, l, c, :], [("cxh", l)], [cxk], q="act")
                        TT(cxe[:, 2 + ck:2 + T], psf(Sc)[:, CS[0]:T], th[:, CS[0]:T], ALU.mult, [("ps", Sc), thk], [cxk])
                        CS[0] = cs
                        ACT(th[:, cs:TP], cxe[:, 2 + cs:2 + TP], AF.Identity, [cxk, "cv"], [thk], scale=cv(l, K_CW2, c), bias=cv(l, K_CB, c))
                        STT(th[:, cs:TP], cxe[:, 1 + cs:1 + TP], cv(l, K_CW1, c), th[:, cs:TP], ALU.mult, ALU.add, [cxk, "cv", thk], [thk])
                        STT(th[:, cs:TP], cxe[:, cs:TP], cv(l, K_CW0, c), th[:, cs:TP], ALU.mult, ALU.add, [cxk, "cv", thk], [thk])
                        TT(buf2[:, c, cs:TP], psf(Sb)[:, cs:TP], th[:, cs:TP], ALU.mult, [("ps", Sb), thk], [("ar", 8 + c)])
                        CP(cxh[:, l, c, :], cxe[:, TP:TP + 2], [cxk], [("cxh", l)], q="act")
                        if last:
                            sc_ = slice(TP, T)
                            ACT(th[:, sc_], cxe[:, 2 + TP:2 + T], AF.Identity, [cxk, "cv"], [thk], scale=cv(l, K_CW2, c), bias=cv(l, K_CB, c))
                            STT(th[:, sc_], hist_sb[:, 1, c, :], cv(l, K_CW1, c), th[:, sc_], ALU.mult, ALU.add, ["hist", "cv", thk], [thk])
                            STT(th[:, sc_], hist_sb[:, 0, c, :], cv(l, K_CW0, c), th[:, sc_], ALU.mult, ALU.add, ["hist", "cv", thk], [thk])
                            TT(buf2[:, c, sc_], psf(Sb)[:, sc_], th[:, sc_], ALU.mult, [("ps", Sb), thk], [("ar", 8 + c)])
                            CP(scx[:, c, :], cxe[:, 2 + TP:2 + T], [cxk], ["scx"], q="act")
                    if last:
                        DMA(pconvT[l], cxh[:, l], [("cxh", l)], [], is_out=True)
                        DMA(sconv_newT[l], scx[:], ["scx"], [], is_out=True)

                    chk()
                    P.tag = "P4"
                    gated_branch(l, glayer, T, "wa", "ga", buf1, lambda kc: [("ar", kc)], 0)
                    if KDBG and gi == 0 and l == 0:
                        DMA(dbgf[0], merged[:], [("mg", c) for c in range(8)], [], is_out=True)

                    chk()
                    P.tag = "P5"
                    def rope_tail(c, qb, qbk):
                        isq = c < 8
                        CS[0] = cs if isq else ck
                        S2 = alloc()
                        for (c0, c1) in ((CS[0], 512), (512, T)):
                            MM(psf(S2)[:, c0:c1], rotm[:], qb[:, c0:c1], True, True, ["rotm", qbk], [("ps", S2)])
                        t1, t1k = tmpf.get()
                        t2, t2k = tmpf.get()
                        TT(t1[:, CS[0]:T], qb[:, CS[0]:T], cos_sb[:, CS[0]:T], ALU.mult, [qbk, "cos"], [t1k])
                        TT(t2[:, CS[0]:T], psf(S2)[:, CS[0]:T], sin_sb[:, CS[0]:T], ALU.mult, [("ps", S2), "sin"], [t2k])
                        if isq:
                            TT(buf1[:, c, CS[0]:T], t1[:, CS[0]:T], t2[:, CS[0]:T], ALU.add, [t1k, t2k], [("ar", c)])
                        else:
                            G2 = c - 8
                            TT(kcur[:, G2, CS[0]:T], t1[:, CS[0]:T], t2[:, CS[0]:T], ALU.add, [t1k, t2k], [("kcur", G2)])
                            if last:
                                TT(kTf[:, G2, 0:T - 640], t1[:, 640:T], t2[:, 640:T], ALU.add, [t1k, t2k], ["kTf"])

                    pend = None
                    for c in range(10):
                        isq = c < 8
                        CS[0] = cs if isq else ck
                        u = w_unit("q" if isq else "k", c if isq else c - 8, glayer)
                        S = alloc()
                        chunk_mm(S, u, hbr, hbk, T)
                        qb, qbk = qbs.get()
                        ACT(qb[:, CS[0]:T], psf(S)[:, CS[0]:T], AF.Identity, [("ps", S)], [qbk])
                        if pend is not None:
                            rope_tail(*pend)
                        pend = (c, qb, qbk)
                    rope_tail(*pend)
                    CS[0] = cs
                    if last:
                        DMA(pkT[l], kTf[:, :, 0:128], ["kTf"], [], is_out=True)
                        DMA(sk_newT[l], kTf[:, :, 128:144], ["kTf"], [], is_out=True)
                    uv0 = w_unit("v", 0, glayer)
                    uv1 = w_unit("v", 1, glayer)
                    vslab = ring[:, uv0[0], :].rearrange("p (u k c) -> p u k c", u=4, k=8)
                    Sv = None
                    for i in range(i0k, ntile):
                        M = 128 if i < 6 else NSMP
                        j = (i - i0k) % 4
                        if j == 0:
                            Sv = alloc()
                        for kc in range(8):
                            MM(psf(Sv)[0:M, j * 256:(j + 1) * 256], hb[:, kc, i * 128:i * 128 + M], vslab[:, 2:4, kc, :], kc == 0, kc == 7,
                               [uv0[2], ("hb", kc)], [("ps", Sv)])
                        ACT(vcur[0:M, i, :], psf(Sv)[0:M, j * 256:(j + 1) * 256], AF.Identity, [("ps", Sv)], [("vcur", i)])
                        if last and i == 5:
                            vf, vfk = tmpf.get()
                            ACT(vf[:, 0:256], psf(Sv)[:, j * 256:(j + 1) * 256], AF.Identity, [("ps", Sv)], [vfk])
                            DMA(pv[l], vf[:, 0:256], [vfk], [], is_out=True)
                        if last and i == 6:
                            vf, vfk = tmpf.get()
                            ACT(vf[0:NSMP, 0:256], psf(Sv)[0:NSMP, j * 256:(j + 1) * 256], AF.Identity, [("ps", Sv)], [vfk])
                            DMA(sv_new[l], vf[0:NSMP, 0:256], [vfk], [], is_out=True)

                    chk()
                    P.tag = "P6"
                    gated_branch(l, glayer, T, "wb", "gb", buf2, lambda kc: [("ar", 8 + kc)], 1)
                    if KDBG and gi == 0 and l == 0:
                        DMA(dbgf[1], merged[:], [("mg", c) for c in range(8)], [], is_out=True)

                    chk()
                    P.tag = "P7"
                    qkeys = [("ar", c) for c in range(8)]
                    def att_a(i, g):
                        ti = gi * 6 + i
                        mprev = 2 if ti == 2 else 1
                        tsl = slice(i * 128, (i + 1) * 128)
                        G2, e = g // 2, g % 2
                        rows = slice(e * 64, (e + 1) * 64)
                        SA = alloc()
                        qr = buf1[rows, G2 * 4:(G2 + 1) * 4, tsl]
                        if i == 0:
                            kp, kpk = kprev[rows, l, G2, :], ("kprev", l)
                        else:
                            kp, kpk = kcur[rows, G2, (i - 1) * 128:i * 128], ("kcur", G2)
                        kc_ = kcur[rows, G2, tsl]
                        MM(psf(SA)[:, 0:512], kp, qr, True, False, [kpk] + qkeys[G2 * 4:(G2 + 1) * 4], [("ps", SA)])
                        MM(psf(SA)[:, 0:512], identb[:], maskb[:, mprev - 1, 0:512], False, True, ["identb", "maskb"], [("ps", SA)])
                        MM(psf(SA)[:, 512:1024], kc_, qr, True, False, [("kcur", G2)] + qkeys[G2 * 4:(G2 + 1) * 4], [("ps", SA)])
                        MM(psf(SA)[:, 512:1024], identb[:], maskb[:, mprev - 1, 512:1024], False, True, ["identb", "maskb"], [("ps", SA)])
                        pt, ptk = pTs.get()
                        ACT(pt[:, :], psf(SA)[:, :], AF.Exp, [("ps", SA)], [ptk], scale=0.125)
                        return (i, g, pt, ptk)

                    def att_b(i, g, pt, ptk):
                        tsl = slice(i * 128, (i + 1) * 128)
                        G2, e = g // 2, g % 2
                        rows = slice(e * 64, (e + 1) * 64)
                        SB = alloc()
                        if i == 0:
                            vp, vpk = vprev[:, l, g * 64:(g + 1) * 64], ("vprev", l)
                        else:
                            vp, vpk = vcur[:, i - 1, g * 64:(g + 1) * 64], ("vcur", i - 1)
                        MM(psf(SB)[rows, 0:512], vp, pt[:, 0:512], True, False, [vpk, ptk], [("ps", SB)])
                        MM(psf(SB)[rows, 0:512], vcur[:, i, g * 64:(g + 1) * 64], pt[:, 512:1024], False, True, [("vcur", i), ptk], [("ps", SB)])
                        MM(psf(SB)[:, 512:1024], ones[:], pt[:, 0:512], True, False, ["ones", ptk], [("ps", SB)])
                        MM(psf(SB)[:, 512:1024], ones[:], pt[:, 512:1024], False, True, ["ones", ptk], [("ps", SB)])
                        tl, tlk = tmpf.get()
                        hc0 = l * 16 + 4 * g
                        STT(tl[:, 0:512].rearrange("p (r t) -> p r t", r=4), psf(SB)[:, 512:1024].rearrange("p (r t) -> p r t", r=4), 1.0,
                            sinkexp[:, hc0:hc0 + 4].unsqueeze(2).broadcast_to([128, 4, 128]), ALU.mult, ALU.add,
                            [("ps", SB), "sinkexp"], [tlk])
                        ACT(tl[:, 0:512], tl[:, 0:512], AF.Ln, [tlk], [tlk])
                        ACT(tl[:, 0:512], tl[:, 0:512], AF.Exp, [tlk], [tlk], scale=-1.0)
                        TT(buf2[rows, G2 * 4:(G2 + 1) * 4, tsl], psf(SB)[rows, 0:512].rearrange("p (r t) -> p r t", r=4),
                           tl[rows, 0:512].rearrange("p (r t) -> p r t", r=4), ALU.mult,
                           [("ps", SB), tlk], [("ar", 8 + G2 * 4 + r) for r in range(4)])

                    pend = []
                    for i in range(i0, 6):
                        for g in range(4):
                            pend.append(att_a(i, g))
                            if len(pend) > ATT_DEPTH:
                                att_b(*pend.pop(0))
                    while pend:
                        att_b(*pend.pop(0))
                    if last:
                        chk()
                        SB = alloc(pin=True)
                        for g in range(4):
                            G2, e = g // 2, g % 2
                            rows = slice(e * 64, (e + 1) * 64)
                            qg = buf1[rows, G2 * 4:(G2 + 1) * 4, TP:T]
                            SA = alloc()
                            for s in range(NSMP):
                                MM(psf(SA)[:, s * 64:(s + 1) * 64], kcT[rows, s, G2, :], qg, True, True,
                                   ["kcT"] + qkeys[G2 * 4:(G2 + 1) * 4], [("ps", SA)])
                            pt, ptk = pTs.get()
                            ACT(pt[:, :], psf(SA)[:, :], AF.Exp, [("ps", SA)], [ptk], scale=0.125)
                            TT(pt[:, :], pt[:, :], cmb[:, :], ALU.mult, [ptk, "cmb"], [ptk])
                            SN2 = alloc()
                            MM(psf(SN2)[0:NSMP, 0:64], kcur[rows, G2, TP:T], qg, True, True,
                               [("kcur", G2)] + qkeys[G2 * 4:(G2 + 1) * 4], [("ps", SN2)])
                            tn, tnk = tmpf.get()
                            ACT(tn[0:NSMP, 0:64], psf(SN2)[0:NSMP, 0:64], AF.Exp, [("ps", SN2)], [tnk], scale=0.125)
                            TT(pd_sb[:, g * 64:(g + 1) * 64], tn[0:NSMP, 0:64], idm16[:, 0:64], ALU.mult, [tnk, "idm16"], [("pd", g)])
                            for s in range(NSMP):
                                MM(psf(SB)[rows, g * 64:(g + 1) * 64], vc[:, s, g * 64:(g + 1) * 64], pt[:, s * 64:(s + 1) * 64], s == 0, False,
                                   ["vc", ptk], [("ps", SB)])
                            MM(psf(SB)[rows, g * 64:(g + 1) * 64], vcur[0:NSMP, 6, g * 64:(g + 1) * 64], pd_sb[:, g * 64:(g + 1) * 64], False, True,
                               [("vcur", 6), ("pd", g)], [("ps", SB)])
                            for s in range(NSMP):
                                MM(psf(SB)[:, 512 + g * 64:512 + (g + 1) * 64], ones[:], pt[:, s * 64:(s + 1) * 64], s == 0, False,
                                   ["ones", ptk], [("ps", SB)])
                            MM(psf(SB)[:, 512 + g * 64:512 + (g + 1) * 64], ones[0:NSMP, :], pd_sb[:, g * 64:(g + 1) * 64], False, True,
                               ["ones", ("pd", g)], [("ps", SB)])
                        tl, tlk = tmpf.get()
                        for g in range(4):
                            hc0 = l * 16 + 4 * g
                            STT(tl[:, g * 64:(g + 1) * 64].rearrange("p (r s) -> p r s", r=4),
                                psf(SB)[:, 512 + g * 64:512 + (g + 1) * 64].rearrange("p (r s) -> p r s", r=4), 1.0,
                                sinkexp[:, hc0:hc0 + 4].unsqueeze(2).broadcast_to([128, 4, NSMP]), ALU.mult, ALU.add,
                                [("ps", SB), "sinkexp"], [tlk])
                        ACT(tl[:, 0:256], tl[:, 0:256], AF.Ln, [tlk], [tlk])
                        ACT(tl[:, 0:256], tl[:, 0:256], AF.Exp, [tlk], [tlk], scale=-1.0)
                        for g in range(4):
                            G2, e = g // 2, g % 2
                            rows = slice(e * 64, (e + 1) * 64)
                            TT(buf2[rows, G2 * 4:(G2 + 1) * 4, TP:T], psf(SB)[rows, g * 64:(g + 1) * 64].rearrange("p (r s) -> p r s", r=4),
                               tl[rows, g * 64:(g + 1) * 64].rearrange("p (r s) -> p r s", r=4), ALU.mult,
                               [("ps", SB), tlk], [("ar", 8 + G2 * 4 + r) for r in range(4)])
                        unpin(SB)
                    CP(kprev[:, l], kcur[:, :, 640:768], [("kcur", 0), ("kcur", 1)], [("kprev", l)], q="act")
                    CP(vprev[:, l, :], vcur[:, 5, :], [("vcur", 5)], [("vprev", l)], q="act")

                    chk()
                    P.tag = "P8"
                    if KDBG and gi == 0 and l == 0:
                        DMA(dbgb[0], buf2[:], [("ar", 8 + c) for c in range(8)], [], is_out=True)
                    gated_branch(l, glayer, T, "wc", "gc", buf2, lambda kc: [("ar", 8 + kc)], 2)
                    if KDBG and gi == 0 and l == 0:
                        DMA(dbgb[1], buf1[:], [("ar", c) for c in range(8)], [], is_out=True)

                    chk()
                    P.tag = "P9"
                    SN = alloc(pin=True)
                    for c in range(8):
                        u = w_unit("wo", c, glayer)
                        S = alloc()
                        chunk_mm(S, u, lambda kc: buf1[:, kc, :], lambda kc: [("ar", kc)], T)
                        ACT(merged[:, c, CS[0]:T], psf(S)[:, CS[0]:T], AF.Identity, [("ps", S)], [("mg", c)])
                        sq, sk = stats_sq(psf(S)[:, CS[0]:T], [("ps", S)], T)
                        if c > 0:
                            stats_mm(SN, *pend_sq, c - 1, T)
                        pend_sq = (sq, sk)
                    stats_mm(SN, *pend_sq, 7, T)
                    stats_finish(SN, T)
                    post_apply(l, K_POSTMIX, T)

                    chk()
                    P.tag = "P10..12"
                    norm_to_hb(l, K_PREFFN, T)
                    for j in range(22):
                        ug = w_unit("fg", j, glayer)
                        S1 = alloc()
                        chunk_mm(S1, ug, hbr, hbk, T)
                        uu = w_unit("fu", j, glayer)
                        S2 = alloc()
                        chunk_mm(S2, uu, hbr, hbk, T)
                        t, tk = tmpf.get()
                        ACT(t[:, CS[0]:T], psf(S1)[:, CS[0]:T], AF.Silu, [("ps", S1)], [tk])
                        TT(actb[:, j, CS[0]:T], psf(S2)[:, CS[0]:T], t[:, CS[0]:T], ALU.mult, [("ps", S2), tk], [("ar", j)])
                    SN = alloc(pin=True)
                    for c in range(8):
                        k, wkey = w_down(c, glayer)
                        S = alloc()
                        for kc in range(22):
                            lhsT = ring[:, k, kc * 128:(kc + 1) * 128]
                            for (c0, c1) in ((CS[0], 512), (512, T)):
                                MM(psf(S)[:, c0:c1], lhsT, actb[:, kc, c0:c1], kc == 0, kc == 21, [wkey, ("ar", kc)], [("ps", S)])
                        ACT(merged[:, c, CS[0]:T], psf(S)[:, CS[0]:T], AF.Identity, [("ps", S)], [("mg", c)])
                        sq, sk = stats_sq(psf(S)[:, CS[0]:T], [("ps", S)], T)
                        if c > 0:
                            stats_mm(SN, *pend_sq, c - 1, T)
                        pend_sq = (sq, sk)
                    stats_mm(SN, *pend_sq, 7, T)
                    stats_finish(SN, T)
                    post_apply(l, K_POSTFFN, T)

                chk()
                for c in range(8):
                    if gi == 0:
                        DMA(yT[:, c, 0:512], xs[:, c, 256:768], [("xs", c)], [], is_out=True)
                    elif gi == 1:
                        DMA(yT[:, c, 512:1280], xs[:, c, 0:768], [("xs", c)], [], is_out=True)
                    else:
                        DMA(yT[:, c, 1280:NYT], xs[:, c, 0:T], [("xs", c)], [], is_out=True)

        try:
            main_loops()
        except _Stop:
            pass
        P.op("sp", lambda e: e.nop(), extra_deps=list(out_toks))
        stats = P.emit(block, sems, dsems)
    _CACHE['plan'] = P
    return nc, stats


IN_OFF = dict(ua=0, va=1024, bg=2048, cg=3072, hb=4096, q=5120, k=6144, v=6400, ga=6656, gb=7680, gc=8704)


def _unit(W, cols, rows=None):
    Wr = W if rows is None else W[rows]
    blk = Wr[:, cols]
    KC = blk.shape[0] // 128
    return blk.reshape(KC, 128, 128).transpose(1, 0, 2).reshape(128, KC * 128)


def pack_weights(inp):
    out = np.zeros((2, NSLAB, 128, 4096), np.float32)
    ar = np.arange(128)
    qcols = {}
    for c in range(8):
        cols = np.empty(128, np.int64)
        for e in range(2):
            h = head_of(c, e)
            cols[e * 64:(e + 1) * 64] = h * 64 + np.arange(64)
        qcols[c] = cols
    rows_c = np.concatenate([qcols[c] for c in range(8)])
    for l in range(2):
        w_in = np.asarray(inp["w_in"][l])
        mats = dict(wa=np.asarray(inp["w_br_a"][l]), wb=np.asarray(inp["w_br_b"][l]), wc=np.asarray(inp["w_br_c"][l]),
                    wo=np.asarray(inp["w_out"][l]), fg=np.asarray(inp["w_ffn_gate"][l]), fu=np.asarray(inp["w_ffn_up"][l]))
        fd = np.asarray(inp["w_ffn_down"][l])
        units = []
        for (name, idx) in unit_sequence():
            if name in ("ua", "va", "bg", "cg", "hb", "k", "v", "ga", "gb", "gc"):
                units.append(_unit(w_in, IN_OFF[name] + idx * 128 + ar))
            elif name == "q":
                units.append(_unit(w_in, IN_OFF["q"] + qcols[idx]))
            elif name == "wc":
                units.append(_unit(mats["wc"], idx * 128 + ar, rows=rows_c))
            else:
                units.append(_unit(mats[name], idx * 128 + ar))
        for s in range(38):
            out[l, s] = np.concatenate(units[s * 4:(s + 1) * 4], axis=1)
        for c in range(8):
            out[l, 38 + c, :, 0:2816] = _unit(fd, c * 128 + ar)
    return out


def fm(v):
    v = np.asarray(v)
    return np.moveaxis(v.reshape(v.shape[:-1] + (8, 128)), -1, 0)


def prep(inp):
    f32 = np.float32
    x_prompt = np.asarray(inp["x_prompt"], f32)[0]
    x_sample = np.asarray(inp["x_sample"], f32)[:, 0]
    state_conv = np.asarray(inp["state_conv"], f32)
    ck = np.asarray(inp["cache_win_k"], f32)
    cvv = np.asarray(inp["cache_win_v"], f32)
    shared = {}
    shared["wst"] = pack_weights(inp)
    cvec = np.zeros((128, 160), f32)
    for l in range(2):
        vecs = [inp["norm_pre_mix"][l], inp["norm_post_mix"][l], inp["norm_pre_ffn"][l], inp["norm_post_ffn"][l],
                inp["chunk_ln_g"][l], inp["chunk_ln_b"][l], inp["conv_w"][l][0], inp["conv_w"][l][1], inp["conv_w"][l][2],
                inp["conv_b"][l]]
        for kd, v in enumerate(vecs):
            cvec[:, l * 80 + kd * 8:l * 80 + kd * 8 + 8] = np.asarray(v, f32).reshape(8, 128).T
    shared["cvec"] = cvec
    wsp = np.asarray(inp["w_spatial"], f32)
    shared["wsT_in"] = np.ascontiguousarray(wsp.transpose(0, 3, 1, 2))
    bsp = np.asarray(inp["b_spatial"], f32)
    shared["bs_bc"] = np.ascontiguousarray(np.broadcast_to(bsp[:, None], (2, 128, 8, 128)))
    shared["identf_in"] = np.eye(128, dtype=f32)
    shared["tri_in"] = np.triu(np.ones((128, 128), f32))
    rot = np.zeros((128, 128), f32)
    for m in range(128):
        if m % 64 < 32:
            rot[m + 32, m] = -1.0
        else:
            rot[m - 32, m] = 1.0
    shared["rotm_in"] = rot.astype(ml_dtypes.bfloat16)
    sinks = np.asarray(inp["attn_sinks"], f32)
    shared["sinks_bc"] = np.ascontiguousarray(np.broadcast_to(sinks.reshape(1, 32), (128, 32)))
    shared["ws00_bc"] = np.ascontiguousarray(np.broadcast_to(wsp[:, :, 0, 0].reshape(1, 16), (16, 16)))
    idm = np.zeros((16, 4, 4, 16), f32)
    for s in range(16):
        idm[s, :, :, s] = 1.0
    shared["idm16_in"] = idm.reshape(16, 256)
    cm = np.zeros((128, 16, 4, 16), f32)
    for s in range(16):
        cm[:, s, :, s] = 1.0
    shared["cm_in"] = cm.reshape(128, 1024)
    kk = np.arange(128)[:, None]
    qq = np.arange(128)[None, :]
    m_cur = np.where(kk <= qq, 0.0, MASKNEG).astype(f32)
    m_prev = np.where(kk >= qq, 0.0, MASKNEG).astype(f32)
    m_none = np.full((128, 128), MASKNEG, f32)
    inv = np.power(np.float32(10000.0), -np.arange(32, dtype=f32) * np.float32(2.0 / 64)).astype(f32)
    invp = inv[(np.arange(128) % 64) % 32]

    xpad = np.concatenate([np.zeros((256, D), f32), x_prompt], axis=0)
    per_core = []
    for core in range(NCORE):
        d = {}
        xp = xpad[core * 2048:core * 2048 + 2304]
        xsmp = x_sample[core * NSMP:(core + 1) * NSMP]
        xa = np.concatenate([xp, xsmp], axis=0)
        d["xT"] = np.ascontiguousarray(xa.reshape(NT, 8, 128).transpose(2, 1, 0))
        pos = np.concatenate([np.arange(core * 2048 - 256, core * 2048 + 2048), np.full(NSMP, PAST)]).astype(np.int32)
        ang = pos.astype(f32)[None, :] * invp[:, None]
        d["cosT"] = np.cos(ang).astype(f32)
        d["sinT"] = np.sin(ang).astype(f32)
        mp0 = m_none if core == 0 else m_prev
        def rep4(m):
            return np.broadcast_to(m[:, None, :], (128, 4, 128)).reshape(128, 512)
        d["maskb_in"] = np.ascontiguousarray(np.stack([np.concatenate([rep4(m_prev), rep4(m_cur)], axis=1),
                                                       np.concatenate([rep4(mp0), rep4(m_cur)], axis=1)], axis=1))
        sl = slice(core * NSMP, (core + 1) * NSMP)
        sc = state_conv[:, sl]
        d["sc_nat"] = np.ascontiguousarray(sc)
        d["histT"] = np.ascontiguousarray(sc.reshape(2, NSMP, 2, 8, 128).transpose(0, 4, 2, 3, 1))
        ckc = ck[:, sl]
        d["ck_nat"] = np.ascontiguousarray(ckc.reshape(2, NSMP, 128, 256))
        d["cv_nat"] = np.ascontiguousarray(cvv[:, sl].reshape(2, NSMP, 128, 256))
        t = ckc.reshape(2, NSMP, 128, 2, 2, 64)
        d["kcT_in"] = np.ascontiguousarray(t.transpose(0, 4, 5, 1, 3, 2).reshape(2, 128, NSMP, 2, 128))
        d.update(shared)
        per_core.append(d)
    return per_core


def kernel(**inputs):
    if "nc" not in _CACHE:
        _CACHE["nc"] = build_program()
    nc, _ = _CACHE["nc"]
    in_maps = prep(inputs)
    res = run_bass_kernel_spmd(nc, in_maps, core_ids=list(range(NCORE)))
    R = res.results
    f32 = np.float32
    y_prompt = np.zeros((1, 16384, D), f32)
    y_sample = np.zeros((128, 1, D), f32)
    for core in range(NCORE):
        yT = R[core]["yT"]
        ya = yT.transpose(2, 1, 0).reshape(NYT, D)
        y_prompt[0, core * 2048:(core + 1) * 2048] = ya[:2048]
        y_sample[core * NSMP:(core + 1) * NSMP, 0] = ya[2048:]
    last = R[NCORE - 1]
    prompt_conv = np.zeros((2, 1, 2, D), f32)
    prompt_k = np.zeros((2, 1, 128, 4, 64), f32)
    prompt_v = np.zeros((2, 1, 128, 4, 64), f32)
    for l in range(2):
        prompt_conv[l, 0] = last["pconvT"][l].transpose(2, 1, 0).reshape(2, D)
        t = last["pkT"][l].reshape(2, 64, 2, 128)
        prompt_k[l, 0] = t.transpose(3, 2, 0, 1).reshape(128, 4, 64)
        prompt_v[l, 0] = last["pv"][l].reshape(128, 4, 64)
    sample_conv = np.zeros((2, 128, 2, D), f32)
    sample_k = np.zeros((2, 128, 128, 4, 64), f32)
    sample_v = np.zeros((2, 128, 128, 4, 64), f32)
    sample_cv = np.zeros((2, 128, 1, D), f32)
    for core in range(NCORE):
        r = R[core]
        sl = slice(core * NSMP, (core + 1) * NSMP)
        for l in range(2):
            sample_conv[l, sl, 0] = r["sconv_old"][l]
            sample_conv[l, sl, 1] = r["sconv_newT"][l].transpose(2, 1, 0).reshape(NSMP, D)
            sample_k[l, sl, 0:127] = r["sk_shift"][l].reshape(NSMP, 127, 4, 64)
            t = r["sk_newT"][l].reshape(2, 64, 2, NSMP)
            sample_k[l, sl, 127] = t.transpose(3, 2, 0, 1).reshape(NSMP, 4, 64)
            sample_v[l, sl, 0:127] = r["sv_shift"][l].reshape(NSMP, 127, 4, 64)
            sample_v[l, sl, 127] = r["sv_new"][l].reshape(NSMP, 4, 64)
            sample_cv[l, sl, 0] = r["scvT"][l].transpose(2, 1, 0).reshape(NSMP, D)
    return (y_prompt, y_sample, prompt_conv, prompt_k, prompt_v, sample_conv, sample_k, sample_v, sample_cv)
```
